# Optimizing a Trainium2 kernel written in Bass

```python
import math
import jax, jax.numpy as jnp
from jax import lax
import numpy as np

D_MODEL = 2048
BATCH = 8
SEQ = 2048
DEPTH = 2

HEAD_DIM = 64
GROUP_WIDTH = D_MODEL // 4
MIX_WIDTH = 4 * GROUP_WIDTH
H_A = GROUP_WIDTH // HEAD_DIM
H_B = GROUP_WIDTH // HEAD_DIM
H_C = GROUP_WIDTH // HEAD_DIM
DIFF_HALF = HEAD_DIM
DIFF_VDIM = 2 * DIFF_HALF
H_D = GROUP_WIDTH // DIFF_VDIM
W_LORA = max(32, int(round(D_MODEL ** 0.5 * 1.8 / 32)) * 32)
A_LORA = max(32, int(round(D_MODEL ** 0.5 * 1.8 / 32)) * 32)
V_LORA = max(32, int(round(D_MODEL ** 0.5 * 1.3 / 32)) * 32)
G_LORA = max(32, int(round(D_MODEL ** 0.8 / 32)) * 32)
N_A_COLS = 3 * GROUP_WIDTH + W_LORA + A_LORA + G_LORA
N_IN = N_A_COLS + 9 * GROUP_WIDTH
A_SPLITS = [GROUP_WIDTH, 2 * GROUP_WIDTH, 3 * GROUP_WIDTH,
            3 * GROUP_WIDTH + W_LORA, 3 * GROUP_WIDTH + W_LORA + A_LORA]
D_FF = ((8 * D_MODEL + 3 * 256 - 1) // (3 * 256)) * 256
DILATIONS = ((128, 1), (512, 4), (2048, 16))
BLOCK = 128
NUM_BUCKETS = 32
MAX_DISTANCE = 2048
NORM_EPS = 1e-6
RWKV_LN_EPS = 64e-5
SUBLN_EPS = 1e-5

kernel_name = "hybrid_parallel_heads_rwkv7_stickbreak_dilated_diffattn"


def rms_norm(x, gain, eps=NORM_EPS):
    xf = x.astype(jnp.float32)
    y = xf * lax.rsqrt(jnp.mean(xf * xf, axis=-1, keepdims=True) + eps)
    return y * gain.astype(jnp.float32)


def rel_bucket(dist):
    max_exact = NUM_BUCKETS // 2
    d_f = jnp.maximum(dist, 1).astype(jnp.float32)
    large = max_exact + (jnp.log(d_f / max_exact) / math.log(MAX_DISTANCE / max_exact)
                         * (NUM_BUCKETS - max_exact)).astype(jnp.int32)
    large = jnp.minimum(large, NUM_BUCKETS - 1)
    return jnp.where(dist < max_exact, dist, large)


def token_shift(p, mu):
    prev = jnp.pad(p, ((0, 0), (1, 0), (0, 0)))[:, :-1]
    return p + (prev - p) * mu


def rwkv7_time_mix(r, k, v, w_lo, a_lo, g_lo, w0, w_up, a0, a_up, g_up,
                   k_k, k_a, r_k, ln_w, ln_b):
    f32 = jnp.float32
    B, S, C = r.shape
    r, k, v = r.astype(f32), k.astype(f32), v.astype(f32)
    w_log = -jax.nn.softplus(-(w0 + jnp.tanh(w_lo.astype(f32)) @ w_up)) - 0.5
    decay = jnp.exp(-jnp.exp(w_log))
    a = jax.nn.sigmoid(a0 + a_lo.astype(f32) @ a_up)
    g = jax.nn.sigmoid(g_lo.astype(f32)) @ g_up
    heads = lambda t: t.reshape(B, S, H_A, HEAD_DIM)
    kk = heads(k * k_k)
    kk = kk * lax.rsqrt(jnp.maximum(jnp.sum(kk * kk, axis=-1, keepdims=True), 1e-24))
    k = k * (1.0 + (a - 1.0) * k_a)
    rh, kh, vh, wh, ah = heads(r), heads(k), heads(v), heads(decay), heads(a)

    def step(state, inp):
        r_t, w_t, k_t, v_t, kk_t, b_t = inp
        sa = jnp.einsum('bhvk,bhk->bhv', state, -kk_t)
        state = (state * w_t[:, :, None, :] + sa[..., None] * b_t[:, :, None, :]
                 + v_t[..., None] * k_t[:, :, None, :])
        return state, jnp.einsum('bhvk,bhk->bhv', state, r_t)

    tm = lambda t: jnp.moveaxis(t, 1, 0)
    state0 = jnp.zeros((B, H_A, HEAD_DIM, HEAD_DIM), f32)
    _, y = lax.scan(step, state0, (tm(rh), tm(wh), tm(kh), tm(vh), tm(kk), tm(kk * ah)))
    y = jnp.moveaxis(y, 0, 1)
    mu = jnp.mean(y, axis=-1, keepdims=True)
    var = jnp.mean(jnp.square(y - mu), axis=-1, keepdims=True)
    y = ((y - mu) * lax.rsqrt(var + RWKV_LN_EPS)).reshape(B, S, C) * ln_w + ln_b
    bonus = jnp.sum(rh * kh * r_k, axis=-1, keepdims=True) * vh
    return (y + bonus.reshape(B, S, C)) * g


def stick_breaking_attention(q, k, v):
    f32 = jnp.float32
    B, S, H, Dh = q.shape
    nb = S // BLOCK
    scale = Dh ** -0.5
    kf, vf = k.astype(f32), v.astype(f32)
    qb = jnp.moveaxis(q.astype(f32).reshape(B, nb, BLOCK, H, Dh), 1, 0)
    key_pos = jnp.arange(S)

    def block(args):
        q_blk, i = args
        z = jnp.einsum('bqhd,bkhd->bhqk', q_blk, kf) * scale
        qpos = i * BLOCK + jnp.arange(BLOCK)
        past = key_pos[None, :] < qpos[:, None]
        log_1m = jnp.where(past, jax.nn.log_sigmoid(-z), 0.0)
        between = lax.cumsum(log_1m, axis=3, reverse=True) - log_1m
        att = jnp.where(past, jnp.exp(jax.nn.log_sigmoid(z) + between), 0.0)
        return jnp.einsum('bhqk,bkhd->bqhd', att, vf)

    o = lax.map(block, (qb, jnp.arange(nb)))
    return jnp.moveaxis(o, 0, 1).reshape(B, S, H * Dh)


def dilated_window_pattern(q, k, v, rel_tab, window, dil):
    f32 = jnp.float32
    B, S, H, Dh = q.shape
    n_back = window // dil
    assert n_back <= BLOCK
    L = S // dil
    nc = -(-L // BLOCK)
    Lp = nc * BLOCK
    scale = Dh ** -0.5

    def gather_stride(t):
        t = t.astype(f32).reshape(B, L, dil, H, Dh).transpose(0, 2, 1, 3, 4).reshape(B * dil, L, H, Dh)
        t = jnp.pad(t, ((0, 0), (0, Lp - L), (0, 0), (0, 0)))
        return t.reshape(B * dil, nc, BLOCK, H, Dh)

    def with_prev(t):
        prev = jnp.pad(t, ((0, 0), (1, 0), (0, 0), (0, 0), (0, 0)))[:, :-1]
        return jnp.concatenate([prev, t], axis=2)

    qc = gather_stride(q)
    kb, vb = with_prev(gather_stride(k)), with_prev(gather_stride(v))
    qi = jnp.arange(BLOCK)[:, None]
    ki = jnp.arange(2 * BLOCK)[None, :]
    steps = qi + BLOCK - ki
    band = (steps >= 0) & (steps <= n_back)
    chunk = jnp.arange(nc)[:, None, None]
    valid = band[None] & (chunk * BLOCK + ki[None] - BLOCK >= 0)
    bias = jnp.transpose(rel_tab[rel_bucket(jnp.maximum(steps, 0) * dil)], (2, 0, 1))
    s = jnp.einsum('gnqhd,gnkhd->gnhqk', qc, kb) * scale + bias[None, None]
    s = jnp.where(valid[None, :, None], s, -jnp.inf)
    lse = jax.nn.logsumexp(s, axis=-1)
    p = jnp.exp(s - lse[..., None])
    o = jnp.einsum('gnhqk,gnkhd->gnqhd', p, vb).reshape(B * dil, Lp, H, Dh)[:, :L]
    o = o.reshape(B, dil, L, H, Dh).transpose(0, 2, 1, 3, 4).reshape(B, S, H, Dh)
    lse = jnp.transpose(lse, (0, 1, 3, 2)).reshape(B * dil, Lp, H)[:, :L]
    lse = lse.reshape(B, dil, L, H).transpose(0, 2, 1, 3).reshape(B, S, H)
    return o, lse


def dilated_attention(q, k, v, rel_tab):
    B, S, H, Dh = q.shape
    outs, lses = [], []
    for window, dil in DILATIONS:
        o, lse = dilated_window_pattern(q, k, v, rel_tab, window, dil)
        outs.append(o)
        lses.append(lse)
    wts = jax.nn.softmax(jnp.stack(lses), axis=0)
    o = jnp.sum(wts[..., None] * jnp.stack(outs), axis=0)
    return o.reshape(B, S, H * Dh)


def differential_attention(q, k, v, lam_params, lam_init, subln, rel_tab):
    f32 = jnp.float32
    B, S, H, _ = q.shape
    nb = S // BLOCK
    scale = DIFF_HALF ** -0.5
    lp = lam_params.astype(f32)
    lam = jnp.exp(jnp.sum(lp[0] * lp[1])) - jnp.exp(jnp.sum(lp[2] * lp[3])) + lam_init
    kf = k.astype(f32).reshape(B, S, H, 2, DIFF_HALF)
    vf = v.astype(f32)
    qb = jnp.moveaxis(q.astype(f32).reshape(B, nb, BLOCK, H, 2, DIFF_HALF), 1, 0)
    key_pos = jnp.arange(S)

    def block(args):
        q_blk, i = args
        qpos = i * BLOCK + jnp.arange(BLOCK)
        dist = qpos[:, None] - key_pos[None, :]
        bias = jnp.transpose(rel_tab[rel_bucket(jnp.maximum(dist, 0))], (2, 0, 1))
        s = jnp.einsum('bqhcd,bkhcd->bchqk', q_blk, kf) * scale + bias[None, None]
        s = jnp.where(dist >= 0, s, -jnp.inf)
        p = jax.nn.softmax(s, axis=-1)
        return jnp.einsum('bhqk,bkhd->bqhd', p[:, 0] - lam * p[:, 1], vf)

    o = jnp.moveaxis(lax.map(block, (qb, jnp.arange(nb))), 0, 1).reshape(B, S, H, DIFF_VDIM)
    o = rms_norm(o, subln, SUBLN_EPS) * (1.0 - lam_init)
    return o.reshape(B, S, H * DIFF_VDIM)


def setup_inputs(seed: int = 0) -> dict:
    key = jax.random.key(seed)
    ks = jax.random.split(key, 32)
    f32 = jnp.float32
    nrm = lambda kk, shape, scale: jax.random.normal(kk, shape, f32) * scale
    uni = lambda kk, shape, lo, hi: jax.random.uniform(kk, shape, f32, lo, hi)
    D, C = D_MODEL, GROUP_WIDTH
    return {
        "x": nrm(ks[0], (BATCH, SEQ, D), 1.0),
        "c": nrm(ks[1], (BATCH, D), 1.0),
        "w_ada": nrm(ks[2], (DEPTH, D, 6 * D), 0.5 * D ** -0.5),
        "b_ada": nrm(ks[3], (DEPTH, 6 * D), 0.02),
        "norm_gain": 1.0 + nrm(ks[4], (DEPTH, 2, D), 0.05),
        "w_in": nrm(ks[5], (DEPTH, D, N_IN), D ** -0.5),
        "w_out": nrm(ks[6], (DEPTH, MIX_WIDTH, D), MIX_WIDTH ** -0.5),
        "rel_bias": nrm(ks[7], (NUM_BUCKETS, H_C + H_D), 0.5),
        "rwkv_mu": uni(ks[8], (DEPTH, N_A_COLS), 0.0, 1.0),
        "rwkv_w0": uni(ks[9], (DEPTH, C), -6.0, 1.0),
        "rwkv_w_up": nrm(ks[10], (DEPTH, W_LORA, C), W_LORA ** -0.5),
        "rwkv_a0": nrm(ks[11], (DEPTH, C), 0.5),
        "rwkv_a_up": nrm(ks[12], (DEPTH, A_LORA, C), A_LORA ** -0.5),
        "rwkv_g_up": nrm(ks[13], (DEPTH, G_LORA, C), G_LORA ** -0.5),
        "rwkv_k_k": 0.85 + nrm(ks[14], (DEPTH, C), 0.05),
        "rwkv_k_a": 1.0 + nrm(ks[15], (DEPTH, C), 0.05),
        "rwkv_r_k": nrm(ks[16], (DEPTH, H_A, HEAD_DIM), 0.1),
        "rwkv_ln_w": 1.0 + nrm(ks[17], (DEPTH, C), 0.05),
        "rwkv_ln_b": nrm(ks[18], (DEPTH, C), 0.02),
        "vres_down": nrm(ks[19], (DEPTH - 1, D, V_LORA), D ** -0.5),
        "vres_mu": uni(ks[20], (DEPTH - 1, V_LORA), 0.0, 1.0),
        "vres_up": nrm(ks[21], (DEPTH - 1, V_LORA, C), V_LORA ** -0.5),
        "vres_bias": nrm(ks[22], (DEPTH - 1, C), 0.5),
        "diff_lambda": nrm(ks[23], (DEPTH, 4, DIFF_HALF), 0.1),
        "diff_subln": 1.0 + nrm(ks[24], (DEPTH, DIFF_VDIM), 0.05),
        "ffn_w13": nrm(ks[25], (DEPTH, D, 2 * D_FF), D ** -0.5),
        "ffn_w2": nrm(ks[26], (DEPTH, D_FF, D), D_FF ** -0.5),
        "final_gain": 1.0 + nrm(ks[27], (D,), 0.05),
    }


def reference(x, c, w_ada, b_ada, norm_gain, w_in, w_out, rel_bias, rwkv_mu, rwkv_w0,
              rwkv_w_up, rwkv_a0, rwkv_a_up, rwkv_g_up, rwkv_k_k, rwkv_k_a, rwkv_r_k,
              rwkv_ln_w, rwkv_ln_b, vres_down, vres_mu, vres_up, vres_bias, diff_lambda,
              diff_subln, ffn_w13, ffn_w2, final_gain):
    out_dtype = x.dtype
    B, S, D = x.shape
    x = x.astype(jnp.float32)
    cond = jax.nn.silu(c.astype(jnp.float32))
    rel_tab_c = rel_bias[:, :H_C].astype(jnp.float32)
    rel_tab_d = rel_bias[:, H_C:].astype(jnp.float32)
    heads = lambda t, dh: t.reshape(B, S, -1, dh)
    v_first = None
    for l in range(DEPTH):
        mod = cond @ w_ada[l] + b_ada[l]
        sh1, sc1, g1, sh2, sc2, g2 = jnp.split(mod[:, None, :], 6, axis=-1)
        h = rms_norm(x, norm_gain[l, 0]) * (1.0 + sc1) + sh1
        w_proj = w_in[l] if l == 0 else jnp.concatenate([w_in[l], vres_down[l - 1]], axis=1)
        p = h @ w_proj
        pa = token_shift(p[..., :N_A_COLS], rwkv_mu[l])
        r_a, k_a_, v_a, w_lo, a_lo, g_lo = jnp.split(pa, A_SPLITS, axis=-1)
        if l == 0:
            v_first = v_a
        else:
            pv = token_shift(p[..., N_IN:], vres_mu[l - 1])
            v_a = v_a + (v_first - v_a) * jax.nn.sigmoid(vres_bias[l - 1] + pv @ vres_up[l - 1])
        y_a = rwkv7_time_mix(r_a, k_a_, v_a, w_lo, a_lo, g_lo, rwkv_w0[l], rwkv_w_up[l],
                             rwkv_a0[l], rwkv_a_up[l], rwkv_g_up[l], rwkv_k_k[l],
                             rwkv_k_a[l], rwkv_r_k[l], rwkv_ln_w[l], rwkv_ln_b[l])
        q_b, k_b, v_b, q_c, k_c, v_c, q_d, k_d, v_d = jnp.split(p[..., N_A_COLS:N_IN], 9, axis=-1)
        y_b = stick_breaking_attention(heads(q_b, HEAD_DIM), heads(k_b, HEAD_DIM), heads(v_b, HEAD_DIM))
        y_c = dilated_attention(heads(q_c, HEAD_DIM), heads(k_c, HEAD_DIM), heads(v_c, HEAD_DIM), rel_tab_c)
        lam_init = 0.8 - 0.6 * math.exp(-0.3 * l)
        y_d = differential_attention(heads(q_d, 2 * DIFF_HALF), heads(k_d, 2 * DIFF_HALF),
                                     heads(v_d, DIFF_VDIM), diff_lambda[l], lam_init,
                                     diff_subln[l], rel_tab_d)
        y_mix = jnp.concatenate([y_a, y_b, y_c, y_d], axis=-1) @ w_out[l]
        x = x + g1 * y_mix
        h2 = rms_norm(x, norm_gain[l, 1]) * (1.0 + sc2) + sh2
        gate, up = jnp.split(h2 @ ffn_w13[l], 2, axis=-1)
        x = x + g2 * ((jax.nn.silu(gate) * up) @ ffn_w2[l])
    return rms_norm(x, final_gain).astype(out_dtype)
```

```python
import math
import contextlib
import numpy as np
import concourse.bass as bass
import concourse.mybir as mybir
from concourse.bass_utils import run_bass_kernel_spmd

F32 = mybir.dt.float32
BF16 = mybir.dt.bfloat16
AF = mybir.ActivationFunctionType
ALU = mybir.AluOpType

D = 2048
KC = 16
DFF = 5632
FC = 44
NIN = 6784
NEG = -30000.0
C_R, C_K, C_V, C_WLO, C_ALO, C_GLO = 0, 512, 1024, 1536, 1632, 1728
C_QB, C_KB, C_VB = 2176, 2688, 3200
C_QC, C_KC, C_VC = 3712, 4224, 4736
C_QD, C_KD, C_VD = 5248, 5760, 6272
C_PV = 6784
TBW = 2432


class View:
    __slots__ = ("t", "ap")

    def __init__(self, t, ap):
        self.t = t
        self.ap = ap

    def m(self, f):
        return View(self.t, f(self.ap))

    def __getitem__(self, k):
        return View(self.t, self.ap[k])


class T:
    __slots__ = ("h", "lw", "rd", "name", "psum")

    def __init__(self, h, name="", psum=False):
        self.h = h
        self.lw = None
        self.rd = []
        self.name = name
        self.psum = psum

    def __getitem__(self, k):
        return View(self, self.h[k])

    def v(self, ap):
        return View(self, ap)


def _ap(x):
    return x.ap if isinstance(x, View) else x


def _ts(*xs):
    return [x.t for x in xs if isinstance(x, View)]


class Prog:
    NQ = 8

    def __init__(self, nc):
        self.nc = nc
        self.ops = []
        self.pos = {}
        self.last = {}
        self.bar = None
        self.bar_seen = set()
        self.unconsumed = set()

    def op(self, eng, fn, R=(), W=(), dma=False, extra=()):
        idx = len(self.ops)
        deps = set(extra)
        for t in R:
            if t.lw is not None:
                deps.add(t.lw)
            if t.psum:
                deps.update(r for r in t.rd if self.ops[r]["eng"] != eng)
        for t in W:
            if t.lw is not None:
                deps.add(t.lw)
            deps.update(t.rd)
        if self.bar is not None and eng not in self.bar_seen:
            deps.add(self.bar)
            self.bar_seen.add(eng)
        pos = self.pos.get(eng, 0)
        keep = set()
        for d in deps:
            o = self.ops[d]
            if o["eng"] == eng and not o["dma"] and not dma:
                if eng == "pe":
                    continue
                if pos - o["pos"] > 3:
                    continue
            keep.add(d)
            if o["dma"]:
                self.unconsumed.discard(d)
        best = {}
        keep2 = set()
        for d in keep:
            o = self.ops[d]
            if o["dma"]:
                keep2.add(d)
            elif best.get(o["eng"], -1) < d:
                best[o["eng"]] = d
        keep2.update(best.values())
        keep = keep2
        self.ops.append(dict(eng=eng, fn=fn, deps=keep, dma=dma, pos=pos))
        self.pos[eng] = pos + 1
        self.last[eng] = idx
        if dma:
            self.unconsumed.add(idx)
        for t in R:
            t.rd.append(idx)
        for t in W:
            t.lw = idx
            t.rd = []
        return idx

    def barrier(self):
        deps = set(self.last.values()) | set(self.unconsumed)
        self.unconsumed = set()
        nc = self.nc
        self.bar = None
        idx = self.op("sp", lambda: nc.sync.nop(nofuse=True), extra=deps)
        self.bar = idx
        self.bar_seen = {"sp"}
        return idx

    def emit(self, es):
        nc = self.nc
        engs = {"pe": nc.tensor, "act": nc.scalar, "dve": nc.vector, "pool": nc.gpsimd, "sp": nc.sync}
        ops = self.ops
        flagged = [False] * len(ops)
        for o in ops:
            for d in o["deps"]:
                flagged[d] = True
        sems = {}
        for e in engs:
            sems[e] = es.enter_context(nc.semaphore("s_" + e))
        dsems = {}
        for e in ("sp", "pool"):
            dsems[e] = [es.enter_context(nc.semaphore("d_%s%d" % (e, i))) for i in range(self.NQ)]
        cnt = {e: 0 for e in engs}
        dcnt = {"sp": 0, "pool": 0}
        known = {e: {} for e in engs}
        ev = [None] * len(ops)
        for idx, o in enumerate(ops):
            e = o["eng"]
            E = engs[e]
            need = {}
            for d in o["deps"]:
                sm, val = ev[d]
                if need.get(sm, (None, 0))[1] < val:
                    need[sm] = (sm, val)
            if o["dma"]:
                k = dcnt[e]
                sm = dsems[e][k % self.NQ]
                prev = 16 * (k // self.NQ)
                if prev > 0 and need.get(sm, (None, 0))[1] < prev:
                    need[sm] = (sm, prev)
            for sm, val in need.values():
                key = id(sm)
                if known[e].get(key, 0) >= val:
                    continue
                E.wait_ge(sm, val)
                known[e][key] = val
            ins = o["fn"]()
            if o["dma"]:
                k = dcnt[e]
                sm = dsems[e][k % self.NQ]
                ins.then_inc(sm, 16)
                ev[idx] = (sm, 16 * (k // self.NQ + 1))
                dcnt[e] = k + 1
            elif flagged[idx]:
                cnt[e] += 1
                ins.then_inc(sems[e], 1)
                ev[idx] = (sems[e], cnt[e])


class K:
    def __init__(self, nc, P, es):
        self.nc = nc
        self.P = P
        self.es = es
        self.eng = {"act": nc.scalar, "dve": nc.vector, "pool": nc.gpsimd}
        self.rr = 0

    def sb(self, es, name, shape, dt=F32):
        self.uid = getattr(self, "uid", 0) + 1
        name = "t%d_%s" % (self.uid, name)
        return T(es.enter_context(self.nc.sbuf_tensor(name, list(shape), dt)), name)

    def dma(self, q, out, in_):
        nc = self.nc
        E = nc.sync if q == "sp" else nc.gpsimd
        o, i = _ap(out), _ap(in_)
        return self.P.op(q, lambda: E.dma_start(out=o, in_=i), R=_ts(in_), W=_ts(out), dma=True)

    def mm(self, out, lhsT, rhs, start=True, stop=True):
        nc = self.nc
        o, l, r = out.ap, lhsT.ap, rhs.ap
        return self.P.op("pe", lambda: nc.tensor.matmul(o, l, r, start=start, stop=stop),
                         R=_ts(lhsT, rhs), W=_ts(out))

    def tr(self, out, in_, ident):
        nc = self.nc
        o, i, d = out.ap, in_.ap, ident.ap
        return self.P.op("pe", lambda: nc.tensor.transpose(o, i, d), R=_ts(in_, ident), W=_ts(out))

    def tt(self, eng, out, in0, in1, op):
        E = self.eng[eng]
        o, a, b = out.ap, _ap(in0), _ap(in1)
        return self.P.op(eng, lambda: E.tensor_tensor(out=o, in0=a, in1=b, op=op), R=_ts(in0, in1), W=_ts(out))

    def ts(self, eng, out, in0, s1, op0, s2=None, op1=None):
        E = self.eng[eng]
        o, a = out.ap, _ap(in0)
        x1, x2 = _ap(s1), _ap(s2)
        if op1 is None:
            f = lambda: E.tensor_scalar(out=o, in0=a, scalar1=x1, scalar2=None, op0=op0)
        else:
            f = lambda: E.tensor_scalar(out=o, in0=a, scalar1=x1, scalar2=x2, op0=op0, op1=op1)
        return self.P.op(eng, f, R=_ts(in0, s1, s2), W=_ts(out))

    def stt(self, out, in0, sc, in1, op0, op1):
        E = self.nc.vector
        o, a, s, b = out.ap, _ap(in0), _ap(sc), _ap(in1)
        return self.P.op("dve", lambda: E.scalar_tensor_tensor(out=o, in0=a, scalar=s, in1=b, op0=op0, op1=op1),
                         R=_ts(in0, sc, in1), W=_ts(out))

    def cp(self, eng, out, in_):
        o, a = out.ap, _ap(in_)
        if eng == "act":
            E = self.nc.scalar
            return self.P.op(eng, lambda: E.copy(out=o, in_=a), R=_ts(in_), W=_ts(out))
        E = self.eng[eng]
        return self.P.op(eng, lambda: E.tensor_copy(out=o, in_=a), R=_ts(in_), W=_ts(out))

    def cpa(self, out, in_):
        self.rr ^= 1
        return self.cp("act" if self.rr else "dve", out, in_)

    def act(self, out, in_, func, bias=None, scale=1.0):
        E = self.nc.scalar
        o, a, b = out.ap, _ap(in_), _ap(bias)
        sc = _ap(scale)
        kw = {}
        if bias is not None:
            kw["bias"] = b
        f = lambda: E.activation(out=o, in_=a, func=func, scale=sc, **kw)
        return self.P.op("act", f, R=_ts(in_, bias, scale), W=_ts(out))

    def recip(self, out, in_):
        E = self.nc.vector
        o, a = out.ap, _ap(in_)
        return self.P.op("dve", lambda: E.reciprocal(out=o, in_=a), R=_ts(in_), W=_ts(out))

    def memset(self, eng, out, val):
        E = self.eng[eng]
        o = out.ap
        return self.P.op(eng, lambda: E.memset(o, val), W=_ts(out))

    def scan(self, out, d0, d1):
        E = self.nc.vector
        o, a, b = out.ap, _ap(d0), _ap(d1)
        return self.P.op("dve", lambda: E.tensor_tensor_scan(out=o, data0=a, data1=b, initial=0.0,
                                                             op0=ALU.mult, op1=ALU.add),
                         R=_ts(d0, d1), W=_ts(out))


def host_consts(S):
    c = {}
    i = np.arange(128)
    ident = np.eye(128, dtype=np.float32)
    ones = np.ones((128, 128), np.float32)
    blk = np.zeros((128, 128), np.float32)
    blk[:64, :64] = 1
    blk[64:, 64:] = 1
    MU = (i[None, :] > i[:, None]).astype(np.float32)
    MU0 = (i[None, :] >= i[:, None]).astype(np.float32)
    ML = (i[:, None] > i[None, :]).astype(np.float32)
    c["cst"] = np.concatenate([ident, ones, blk, MU, MU0, ML, ident[::-1].copy(), ident, ident, MU, MU, MU0, MU0, ML, ML], axis=1)
    r = np.arange(-512, 2048)
    rp = np.maximum(r, 0)
    d_f = np.maximum(rp, 1).astype(np.float32)
    large = 16 + (np.log(d_f / np.float32(16)) / np.float32(math.log(2048 / 16)) * np.float32(16)).astype(np.int32)
    large = np.minimum(large, 31)
    bucket = np.where(rp < 16, rp, large)
    E = np.zeros((34, 2560), np.float32)
    E[bucket, np.arange(2560)] = 1.0
    E[:32, r < 0] = 0.0
    cnt = ((rp <= 128).astype(np.int32) + ((rp % 4 == 0) & (rp <= 512)).astype(np.int32)
           + ((rp % 16 == 0) & (rp <= 2048)).astype(np.int32))
    exC = np.where(cnt > 0, np.log(np.maximum(cnt, 1).astype(np.float64)), NEG).astype(np.float32)
    exC[r < 0] = NEG
    exD = np.where(r < 0, NEG, 0.0).astype(np.float32)
    E[32] = exC
    E[33] = exD
    c["e34"] = E
    ind = np.zeros((2, 12), np.float32)
    ind[0, :8] = 1
    ind[1, 8:] = 1
    c["ind2"] = ind
    xx = np.arange(TBW)[None, :] - 384 - i[:, None]
    c["m0"] = np.where(xx > 0, 0.0, NEG).astype(np.float32)
    c["m1"] = (xx > 0).astype(np.float32)
    rm = np.ones((128, S), np.float32)
    rm[:, ::128] = 0.0
    c["rmask"] = rm
    return c


class StopBuild(Exception):
    pass


def build(S=2048, NL=2, dbg=False, stop_after=None):
    nc = bass.Bass("TRN2", target_bir_lowering=False)
    NT = S // 512
    NB = S // 128
    es0 = contextlib.ExitStack()
    P = Prog(nc)
    k = K(nc, P, es0)

    def din(name, shape):
        return nc.dram_tensor(name, list(shape), F32, kind="ExternalInput").ap()

    x_d = din("x", [S, D])
    c_d = din("c", [D])
    w_ada = din("w_ada", [2, D, 6 * D])
    b_ada = din("b_ada", [2, 6 * D])
    norm_gain = din("norm_gain", [2, 2, D])
    w_in = din("w_in", [2, D, NIN])
    w_out = din("w_out", [2, D, D])
    rel_bias = din("rel_bias", [32, 12])
    rwkv_mu = din("rwkv_mu", [2, 2176])
    rwkv_w0 = din("rwkv_w0", [2, 512])
    rwkv_w_up = din("rwkv_w_up", [2, 96, 512])
    rwkv_a0 = din("rwkv_a0", [2, 512])
    rwkv_a_up = din("rwkv_a_up", [2, 96, 512])
    rwkv_g_up = din("rwkv_g_up", [2, 448, 512])
    rwkv_k_k = din("rwkv_k_k", [2, 512])
    rwkv_k_a = din("rwkv_k_a", [2, 512])
    rwkv_r_k = din("rwkv_r_k", [2, 512])
    rwkv_ln_w = din("rwkv_ln_w", [2, 512])
    rwkv_ln_b = din("rwkv_ln_b", [2, 512])
    vres_down = din("vres_down", [1, D, 64])
    vres_mu = din("vres_mu", [1, 64])
    vres_up = din("vres_up", [1, 64, 512])
    vres_bias = din("vres_bias", [1, 512])
    diff_lambda = din("diff_lambda", [2, 256])
    diff_subln = din("diff_subln", [2, 128])
    ffn_w13 = din("ffn_w13", [2, D, 2 * DFF])
    ffn_w2 = din("ffn_w2", [2, DFF, D])
    final_gain = din("final_gain", [D])
    cst_d = din("cst", [128, 1920])
    e34_d = din("e34", [34, 2560])
    ind2_d = din("ind2", [2, 12])
    m0_d = din("m0", [128, TBW])
    m1_d = din("m1", [128, TBW])
    rmask_d = din("rmask", [128, S])
    out_d = nc.dram_tensor("out", [S, D], F32, kind="ExternalOutput").ap()

    okind = "ExternalOutput" if dbg else "Internal"
    xT_h = nc.dram_tensor("xT", [D, S], F32, kind=okind)
    pT_h = nc.dram_tensor("pT", [NIN + 64, S], F32, kind=okind)
    ym_h = nc.dram_tensor("ymT", [D, S], BF16, kind=okind)
    vtok_h = nc.dram_tensor("vtok", [3, S, 512], BF16, kind="Internal")
    bias_h = nc.dram_tensor("biasd", [12, 2560], F32, kind=okind)
    vf_h = nc.dram_tensor("vfT", [512, S], F32, kind="Internal")
    mod_h = nc.dram_tensor("modd", [128, 192], F32, kind=okind)
    xT_d, pT_d, ym_d, vtok_d, bias_d, vf_d = (h.ap() for h in (xT_h, pT_h, ym_h, vtok_h, bias_h, vf_h))
    xT3 = xT_d.rearrange("(c p) s -> p c s", p=128)
    ym3 = ym_d.rearrange("(c p) s -> p c s", p=128)

    cst = k.sb(es0, "cst", [128, 1920])
    ident, ones, blk64, MU, MU0, ML, JX = (cst[:, i * 128:(i + 1) * 128] for i in range(7))
    onesb = k.sb(es0, "onesb", [128, 128], BF16)
    modc = k.sb(es0, "modc", [128, 2 * 96])
    gcol = k.sb(es0, "gcol", [128, 4 * 16 + 16])
    A1 = k.sb(es0, "A1", [128, 2 * 2 * 16])
    lamc = k.sb(es0, "lamc", [128, 4])
    PS = [T(es0.enter_context(nc.psum_tensor("ps%d" % i, [128, 512], F32)), "ps%d" % i, psum=True) for i in range(8)]
    ncd = contextlib.ExitStack()
    ncd.enter_context(nc.allow_non_contiguous_dma(reason="small per-channel parameter vectors"))

    def colload(dst, vec, n):
        k.dma("sp", dst, vec.rearrange("(c p) -> p c", p=128))

    k.dma("sp", cst[:, :], cst_d)
    k.cp("dve", onesb[:, :], ones)

    def done(tag):
        if stop_after == tag:
            raise StopBuild()
    try:

        with contextlib.ExitStack() as es:
            condT = k.sb(es, "condT", [128, 16])
            colload(condT[:, :], c_d, 16)
            k.act(condT[:, :], condT[:, :], AF.Silu)
            wb = [k.sb(es, "wada%d" % i, [128, 16, 512]) for i in range(2)]
            bcol = k.sb(es, "bcol", [128, 192])
            for l in range(NL):
                colload(bcol[:, l * 96:(l + 1) * 96], b_ada[l], 96)
                for i in range(2):
                    colload(gcol[:, (l * 2 + i) * 16:(l * 2 + i + 1) * 16], norm_gain[l, i], 16)
            colload(gcol[:, 64:80], final_gain, 16)
            for l in range(NL):
                for nb in range(24):
                    w = wb[nb % 2]
                    src = w_ada[l][:, nb * 512:(nb + 1) * 512].rearrange("(c p) n -> p c n", p=128)
                    k.dma("sp", w[:, 0:8, :], src[:, 0:8, :])
                    k.dma("sp", w[:, 8:16, :], src[:, 8:16, :])
                    for m in range(4):
                        j = nb * 4 + m
                        for kc in range(16):
                            k.mm(PS[0][:, j:j + 1], w[:, kc, m * 128:(m + 1) * 128], condT[:, kc:kc + 1],
                                 start=(kc == 0), stop=(kc == 15))
                k.tt("dve", modc[:, l * 96:(l + 1) * 96], PS[0][:, 0:96], bcol[:, l * 96:(l + 1) * 96], ALU.add)
                for i in range(2):
                    sc = modc[:, l * 96 + i * 48 + 16: l * 96 + i * 48 + 32]
                    k.stt(A1[:, (l * 2 + i) * 16:(l * 2 + i + 1) * 16], sc, 1.0,
                          gcol[:, (l * 2 + i) * 16:(l * 2 + i + 1) * 16], ALU.add, ALU.mult)
            if dbg:
                k.dma("sp", mod_h.ap()[:, 0:NL * 96], modc[:, 0:NL * 96])
            lp = k.sb(es, "lp", [1, 512])
            lsum = k.sb(es, "lsum", [1, 8])
            for l in range(NL):
                k.dma("sp", lp[0:1, 0:256], diff_lambda[l:l + 1, :])
                k.tt("dve", lp[0:1, 256:320], lp[0:1, 0:64], lp[0:1, 64:128], ALU.mult)
                k.tt("dve", lp[0:1, 320:384], lp[0:1, 128:192], lp[0:1, 192:256], ALU.mult)
                nc_ = nc
                o1, i1 = lsum[0:1, 0:2].ap, lp[0:1, 256:384].ap.rearrange("p (a b) -> p a b", a=2)
                P.op("dve", lambda o1=o1, i1=i1: nc_.vector.tensor_reduce(out=o1, in_=i1, axis=mybir.AxisListType.X,
                                                                         op=ALU.add), R=[lp], W=[lsum])
                k.act(lsum[0:1, 2:4], lsum[0:1, 0:2], AF.Exp)
                lam_init = 0.8 - 0.6 * math.exp(-0.3 * l)
                k.tt("dve", lsum[0:1, 4:5], lsum[0:1, 3:4], lsum[0:1, 2:3], ALU.subtract)
                k.ts("dve", lsum[0:1, 5:6], lsum[0:1, 4:5], -lam_init, ALU.add)
                k.mm(PS[1][:, l:l + 1], ones[0:1, :], lsum[0:1, 5:6])
                k.cp("dve", lamc[:, l:l + 1], PS[1][:, l:l + 1])
        P.barrier()
        done("mod")

        with contextlib.ExitStack() as es:
            l34 = k.sb(es, "l34", [34, 12])
            e34 = k.sb(es, "e34", [34, 2560])
            bf = k.sb(es, "bf", [12, 2560])
            k.dma("sp", l34[0:32, :], rel_bias)
            k.dma("sp", l34[32:34, :], ind2_d)
            k.dma("sp", e34[:, :], e34_d)
            for i in range(5):
                k.mm(PS[i][0:12, :], l34[:, :], e34[:, i * 512:(i + 1) * 512])
                k.cp("dve", bf[:, i * 512:(i + 1) * 512], PS[i][0:12, :])
            k.dma("sp", bias_d, bf[:, :])
        P.barrier()
        done("bias")

        with contextlib.ExitStack() as es:
            xin = [k.sb(es, "xin%d" % i, [128, D]) for i in range(2)]
            stg = [k.sb(es, "xstg%d" % i, [128, 16, 128]) for i in range(2)]
            for tb in range(NB):
                xi, st = xin[tb % 2], stg[tb % 2]
                k.dma("sp", xi[:, :], x_d[tb * 128:(tb + 1) * 128, :])
                for g in range(4):
                    ps = PS[(tb * 4 + g) % 8]
                    for q in range(4):
                        kc = g * 4 + q
                        k.tr(ps[:, q * 128:(q + 1) * 128], xi[:, kc * 128:(kc + 1) * 128], ident)
                    k.cpa(st[:, g * 4:(g + 1) * 4, :], ps[:, :].m(lambda a: a.rearrange("p (q t) -> p q t", q=4)))
                k.dma("sp", xT3[:, :, tb * 128:(tb + 1) * 128], st[:, :, :])
        P.barrier()
        done("x0")

        def norm_phase(es, hT, li, t0, ntok, acol, bcolv):
            xt = k.sb(es, "nx", [128, 16, 512])
            sq = [k.sb(es, "nsq%d" % i, [128, 512]) for i in range(2)]
            rs = k.sb(es, "nrs", [128, 512])
            tmp = [k.sb(es, "ntmp%d" % i, [128, 512]) for i in range(2)]
            for tcn in range(ntok // 512):
                k.dma("sp", xt[:, :, :], xT3[:, :, t0 + tcn * 512: t0 + (tcn + 1) * 512])
                ps = PS[tcn % 2]
                for kc in range(16):
                    s_ = sq[kc % 2]
                    k.act(s_[:, :], xt[:, kc, :], AF.Square)
                    k.mm(ps[:, :], ones, s_[:, :], start=(kc == 0), stop=(kc == 15))
                k.ts("dve", rs[:, :], ps[:, :], 1.0 / D, ALU.mult, 1e-6, ALU.add)
                k.act(rs[:, :], rs[:, :], AF.Sqrt)
                k.recip(rs[:, :], rs[:, :])
                for kc in range(16):
                    t_ = tmp[kc % 2]
                    k.tt("dve", t_[:, :], xt[:, kc, :], rs[:, :], ALU.mult)
                    dst = hT[:, kc, tcn * 512:(tcn + 1) * 512]
                    if bcolv is None:
                        k.ts("dve", dst, t_[:, :], acol.m(lambda a, kc=kc: a[:, kc:kc + 1]), ALU.mult)
                    else:
                        k.act(dst, t_[:, :], AF.Identity, bias=bcolv.m(lambda a, kc=kc: a[:, kc:kc + 1]),
                              scale=acol.m(lambda a, kc=kc: a[:, kc:kc + 1]))

        def wload(wt, src, kcs, width):
            s3 = src.rearrange("(c p) n -> p c n", p=128)
            step = 8
            for a in range(0, kcs, step):
                b = min(kcs, a + step)
                k.dma("pool", wt[:, a:b, 0:width], s3[:, a:b, :])

        def resid_epilogue(es_stage, ps, c0, tcs, gc):
            xs = es_stage[resid_epilogue.n % len(es_stage)]
            resid_epilogue.n += 1
            k.dma("sp", xs[:, :], xT_d[c0:c0 + 128, tcs])
            k.stt(xs[:, :], ps[:, :], gc, xs[:, :], ALU.mult, ALU.add)
            k.dma("sp", xT_d[c0:c0 + 128, tcs], xs[:, :])
        resid_epilogue.n = 0

        def diag_ap(t):
            base = t.h[:, :, :]
            return View(t, bass.AP(tensor=base.tensor, offset=base.offset, ap=[list(base.ap[0]), [192, 2], [1, 64]]))

        def bc2(v):
            return v.m(lambda a: a.unsqueeze(1).to_broadcast([128, 2, 128]))

        def shift_into(dst, rows, praw, tmp, mucol, r0):
            k.dma("sp", praw[0:rows, 16:S + 16], pT_d[r0:r0 + rows, :])
            k.tt("dve", tmp[0:rows, 0:S], praw[0:rows, 15:S + 15], praw[0:rows, 16:S + 16], ALU.subtract)
            k.stt(dst[0:rows, 0:S], tmp[0:rows, 0:S], mucol, praw[0:rows, 16:S + 16], ALU.mult, ALU.add)

        def rwkv_phase(es, l):
            SW = S + 16
            names = ["raw", "ta", "tb", "KS", "RS", "VS", "LW", "AA", "KK", "KP", "BB", "CU", "BT"]
            Wt = {n: k.sb(es, "rw_" + n, [128, SW]) for n in names}
            raw, ta, tb = Wt["raw"], Wt["ta"], Wt["tb"]
            KS, RS, VS, LW, AA, KK, KP, BB, CU, BT = (Wt[n] for n in names[3:])
            AT, KT, RT, KH, BH, BON, G, YT = KS, LW, AA, KK, tb, CU, raw, BB
            TWt = k.sb(es, "TWt", [96, S])
            TAt = k.sb(es, "TAt", [96, S])
            TG = k.sb(es, "TG", [128, 4, S], BF16)
            wup = k.sb(es, "wup", [96, 512])
            aup = k.sb(es, "aup", [96, 512])
            gup = k.sb(es, "gup", [128, 4, 512], BF16)
            cols = k.sb(es, "rwcols", [128, 64])
            c5 = [k.sb(es, "c5_%d" % i, [128, 512]) for i in range(3)]
            ostg = [k.sb(es, "rwo%d" % i, [128, 512], BF16) for i in range(2)]
            k.memset("dve", raw[:, 15:16], 0.0)
            colload(cols[:, 0:12], rwkv_mu[l][0:1536], 12)
            k.dma("sp", cols[0:96, 12:13], rwkv_mu[l][C_WLO:C_WLO + 96].rearrange("(c p) -> p c", p=96))
            k.dma("sp", cols[0:96, 13:14], rwkv_mu[l][C_ALO:C_ALO + 96].rearrange("(c p) -> p c", p=96))
            colload(cols[:, 14:17], rwkv_mu[l][C_GLO:C_GLO + 384], 3)
            k.dma("sp", cols[0:64, 17:18], rwkv_mu[l][C_GLO + 384:C_GLO + 448].rearrange("(c p) -> p c", p=64))
            colload(cols[:, 18:22], rwkv_w0[l], 4)
            colload(cols[:, 22:26], rwkv_a0[l], 4)
            colload(cols[:, 26:30], rwkv_k_k[l], 4)
            colload(cols[:, 30:34], rwkv_k_a[l], 4)
            k.ts("dve", cols[:, 34:38], cols[:, 30:34], -1.0, ALU.mult, 1.0, ALU.add)
            colload(cols[:, 38:42], rwkv_r_k[l], 4)
            colload(cols[:, 42:46], rwkv_ln_w[l], 4)
            colload(cols[:, 46:50], rwkv_ln_b[l], 4)
            k.dma("sp", wup[:, :], rwkv_w_up[l])
            k.dma("sp", aup[:, :], rwkv_a_up[l])
            k.memset("dve", gup[:, 3, :], 0.0)
            k.memset("dve", TG[:, 3, :], 0.0)
            for gi in range(4):
                rows = 128 if gi < 3 else 64
                k.dma("pool", gup[0:rows, gi, :], rwkv_g_up[l][gi * 128: gi * 128 + rows, :])
            if l == 1:
                PVs = k.sb(es, "PVs", [64, S])
                vup = k.sb(es, "vup", [64, 512])
                k.dma("sp", cols[0:64, 50:51], vres_mu[0].rearrange("(c p) -> p c", p=64))
                colload(cols[:, 51:55], vres_bias[0], 4)
                k.dma("sp", vup[:, :], vres_up[0])
                shift_into(PVs, 64, raw, ta, cols[0:64, 50:51], C_PV)
            shift_into(TWt, 96, raw, ta, cols[0:96, 12:13], C_WLO)
            k.act(TWt[:, :], TWt[:, :], AF.Tanh)
            shift_into(TAt, 96, raw, ta, cols[0:96, 13:14], C_ALO)
            for gi in range(4):
                rows = 128 if gi < 3 else 64
                shift_into(tb, rows, raw, ta, cols[0:rows, 14 + gi:15 + gi], C_GLO + gi * 128)
                k.act(TG[0:rows, gi, :], tb[0:rows, 0:S], AF.Sigmoid)
            done("rw1")
            def pair(name):
                return k.sb(es, name, [128, 2, 128])
            Nn = [pair("Nn0"), pair("Nn1")]
            Nt = [pair("Nt0"), pair("Nt1")]
            MTt, ARK, ARB = pair("MTt"), pair("ARK"), pair("ARB")
            PP = [pair("PP0"), pair("PP1")]
            RH, WU, U0P, VP, KHP, BHP = (pair(n) for n in ("RH", "WU", "U0P", "VP", "KHP", "BHP"))
            for t_ in (WU, U0P, VP, KHP, BHP):
                k.memset("dve", t_[:, :, :], 0.0)
            GyT = k.sb(es, "GyT", [128, 128])
            GhT = k.sb(es, "GhT", [128, 128])
            Hs = k.sb(es, "Hs", [128, 64])
            Hbd = k.sb(es, "Hbd", [128, 128])
            gC = k.sb(es, "gC", [128, NB])
            identv = ident
            id2, MU2, MU02, ML2 = (cst[:, 896 + i * 256: 896 + (i + 1) * 256].m(lambda a_: a_.rearrange("p (h w) -> p h w", h=2)) for i in range(4))
            Lz = k.sb(es, "Lz", [128, 6, 128])
            hmc = (blk64[:, 0:1], blk64[:, 64:65])

            def p3(ps, c0, w):
                return ps[:, c0:c0 + 2 * w].m(lambda a: a.rearrange("p (h w) -> p h w", h=2))

            for ct in range(4):
                cc = lambda j: cols[:, j + ct:j + ct + 1]
                r0 = ct * 128
                k.memset("dve", raw[:, 15:16], 0.0)
                shift_into(KS, 128, raw, ta, cc(4), C_K + r0)
                shift_into(RS, 128, raw, ta, cc(0), C_R + r0)
                shift_into(VS, 128, raw, ta, cc(8), C_V + r0)
                for tcn in range(NT):
                    sl = slice(tcn * 512, (tcn + 1) * 512)
                    ps = PS[tcn % 2]
                    k.mm(ps[:, :], wup[:, r0:r0 + 128], TWt[:, sl])
                    k.act(LW[:, sl], ps[:, :], AF.Sigmoid, bias=cc(18))
                    ps = PS[2 + tcn % 2]
                    k.mm(ps[:, :], aup[:, r0:r0 + 128], TAt[:, sl])
                    k.act(AA[:, sl], ps[:, :], AF.Sigmoid, bias=cc(22))
                k.ts("dve", LW[:, 0:S], LW[:, 0:S], -0.6065306597126334, ALU.mult)
                k.ts("dve", ta[:, 0:S], KS[:, 0:S], cc(26), ALU.mult)
                k.act(tb[:, 0:S], ta[:, 0:S], AF.Square)
                for tcn in range(NT):
                    sl = slice(tcn * 512, (tcn + 1) * 512)
                    ps = PS[4 + tcn % 2]
                    c_ = c5[tcn % 3]
                    k.mm(ps[:, :], blk64, tb[:, sl])
                    k.ts("dve", c_[:, :], ps[:, :], 1e-24, ALU.max)
                    k.act(c_[:, :], c_[:, :], AF.Sqrt)
                    k.recip(c_[:, :], c_[:, :])
                    k.tt("dve", KK[:, sl], ta[:, sl], c_[:, :], ALU.mult)
                k.ts("dve", ta[:, 0:S], AA[:, 0:S], cc(30), ALU.mult, cc(34), ALU.add)
                k.tt("dve", KP[:, 0:S], KS[:, 0:S], ta[:, 0:S], ALU.mult)
                k.tt("dve", BB[:, 0:S], KK[:, 0:S], AA[:, 0:S], ALU.mult)
                if l == 0:
                    k.dma("sp", vf_d[r0:r0 + 128, :], VS[:, 0:S])
                else:
                    k.dma("sp", ta[:, 0:S], vf_d[r0:r0 + 128, :])
                    for tcn in range(NT):
                        sl = slice(tcn * 512, (tcn + 1) * 512)
                        ps = PS[6 + tcn % 2]
                        c_, c2 = c5[tcn % 2], c5[2]
                        k.mm(ps[:, :], vup[:, r0:r0 + 128], PVs[:, sl])
                        k.act(c_[:, :], ps[:, :], AF.Sigmoid, bias=cc(51))
                        k.tt("dve", c2[:, :], ta[:, sl], VS[:, sl], ALU.subtract)
                        k.tt("dve", c2[:, :], c2[:, :], c_[:, :], ALU.mult)
                        k.tt("dve", VS[:, sl], VS[:, sl], c2[:, :], ALU.add)
                for c in range(NB):
                    k.scan(CU[:, c * 128:(c + 1) * 128], ones, LW[:, c * 128:(c + 1) * 128])
                k.tt("dve", ta[:, 0:S], CU[:, 0:S], LW[:, 0:S], ALU.subtract)
                k.act(ta[:, 0:S], ta[:, 0:S], AF.Exp)
                k.stt(AT[:, 0:S], KK[:, 0:S], -1.0, ta[:, 0:S], ALU.mult, ALU.mult)
                k.act(ta[:, 0:S], CU[:, 0:S], AF.Exp, scale=-1.0)
                k.tt("dve", KT[:, 0:S], KP[:, 0:S], ta[:, 0:S], ALU.mult)
                k.tt("dve", BT[:, 0:S], BB[:, 0:S], ta[:, 0:S], ALU.mult)
                k.act(ta[:, 0:S], CU[:, 0:S], AF.Exp)
                k.tt("dve", RT[:, 0:S], RS[:, 0:S], ta[:, 0:S], ALU.mult)
                cu3 = CU[:, 0:S].m(lambda a: a.rearrange("p (c t) -> p c t", t=128))
                cuC = cu3.m(lambda a: a[:, :, 127:128].to_broadcast([128, NB, 128]))
                ta3 = ta[:, 0:S].m(lambda a: a.rearrange("p (c t) -> p c t", t=128))
                k.tt("dve", ta3, cuC, cu3, ALU.subtract)
                k.act(ta[:, 0:S], ta[:, 0:S], AF.Exp)
                k.act(gC[:, :], cu3.m(lambda a: a[:, :, 127]), AF.Exp)
                k.tt("dve", KH[:, 0:S], KP[:, 0:S], ta[:, 0:S], ALU.mult)
                k.tt("dve", BH[:, 0:S], BB[:, 0:S], ta[:, 0:S], ALU.mult)
                k.tt("dve", ta[:, 0:S], RS[:, 0:S], KP[:, 0:S], ALU.mult)
                k.ts("dve", ta[:, 0:S], ta[:, 0:S], cc(38), ALU.mult)
                for tcn in range(NT):
                    sl = slice(tcn * 512, (tcn + 1) * 512)
                    ps = PS[tcn % 2]
                    k.mm(ps[:, :], blk64, ta[:, sl])
                    k.tt("dve", BON[:, sl], ps[:, :], VS[:, sl], ALU.mult)
                for tcn in range(NT):
                    sl = slice(tcn * 512, (tcn + 1) * 512)
                    ps = PS[2 + tcn % 2]
                    for gi in range(4):
                        k.mm(ps[:, :], gup[:, gi, r0:r0 + 128], TG[:, gi, sl], start=(gi == 0), stop=(gi == 3))
                    k.cpa(G[:, sl], ps[:, :])
                done("rw2")
                import os
                if os.environ.get("KVAR", "") != "nomem":
                    k.memset("dve", Hs[:, :], 0.0)
                    k.memset("dve", Hbd[:, :], 0.0)
                for c in range(NB):
                    sl = slice(c * 128, (c + 1) * 128)
                    hp = lambda t_, hh: t_[hh * 64:(hh + 1) * 64, sl]
                    psT = PS[0]
                    import os
                    KVAR = os.environ.get("KVAR", "")
                    if KVAR != "notr":
                        for i_, src in enumerate((AT, VS, KH, BH)):
                            k.tr(psT[:, i_ * 128:(i_ + 1) * 128], src[:, sl], ident)
                    if KVAR != "noev":
                        k.cpa(RH[:, :, 0:64], p3(psT, 0, 64))
                        k.cpa(diag_ap(VP), p3(psT, 128, 64))
                        k.cpa(diag_ap(KHP), p3(psT, 256, 64))
                        k.cpa(diag_ap(BHP), p3(psT, 384, 64))
                    done("rw3a")
                    for i_, src_ in enumerate((BT, AT, KT)):
                        for hh in range(2):
                            k.ts("dve", Lz[:, i_ * 2 + hh, :], src_[:, sl], hmc[hh], ALU.mult)
                    specs = [(PS[1], 0, 0, AT, Nn[0], MU2), (PS[1], 256, 1, BT, Nt[0], ML2),
                             (PS[2], 0, 2, AT, MTt, MU2), (PS[2], 256, 2, RT, ARK, MU02),
                             (PS[3], 0, 0, RT, ARB, MU02)]
                    for ps, c0, li_, rt, dst, msk in specs:
                        for hh in range(2):
                            k.mm(ps[:, c0 + hh * 128:c0 + (hh + 1) * 128], Lz[:, li_ * 2 + hh, :], rt[:, sl])
                        k.tt("dve", dst[:, :, :], p3(ps, c0, 128), msk, ALU.mult)
                    k.tt("dve", PP[0][:, :, :], Nn[0][:, :, :], id2, ALU.add)
                    done("rw3")
                    a = 0
                    b = 0
                    for lev in range(1, 7):
                        psq = PS[4 + 2 * (lev % 2)]
                        psc = PS[5 + 2 * (lev % 2)]
                        for hh in range(2):
                            k.mm(psq[:, 256 + hh * 128:256 + (hh + 1) * 128], Nn[a][:, hh, :], Nt[a][:, hh, :])
                        if lev < 6:
                            for hh in range(2):
                                k.mm(psq[:, hh * 128:(hh + 1) * 128], Nt[a][:, hh, :], Nn[a][:, hh, :])
                        k.cpa(Nt[1 - a][:, :, :], p3(psq, 256, 128))
                        if lev < 6:
                            k.cpa(Nn[1 - a][:, :, :], p3(psq, 0, 128))
                        for hh in range(2):
                            k.mm(psc[:, hh * 128:(hh + 1) * 128], identv, PP[b][:, hh, :], start=True, stop=False)
                            k.mm(psc[:, hh * 128:(hh + 1) * 128], Nt[1 - a][:, hh, :], PP[b][:, hh, :],
                                 start=False, stop=True)
                        k.cpa(PP[1 - b][:, :, :], p3(psc, 0, 128))
                        a, b = 1 - a, 1 - b
                    Pf = PP[b]
                    done("rw4")
                    psM = PS[1]
                    for hh in range(2):
                        k.mm(psM[:, hh * 64:(hh + 1) * 64], MTt[:, hh, :], VP[:, hh, hh * 64:(hh + 1) * 64])
                    k.cpa(RH[:, :, 64:128], p3(psM, 0, 64))
                    psW = PS[2]
                    for hh in range(2):
                        k.mm(psW[:, hh * 128:(hh + 1) * 128], Pf[:, hh, :], RH[:, hh, :])
                    pw3 = p3(psW, 0, 128)
                    k.cpa(diag_ap(WU), pw3.m(lambda a_: a_[:, :, 0:64]))
                    k.cpa(diag_ap(U0P), pw3.m(lambda a_: a_[:, :, 64:128]))
                    psG = PS[3]
                    for hh in range(2):
                        k.mm(psG[:, 0:128], WU[:, hh, :], ARB[:, hh, :], start=(hh == 0), stop=(hh == 1))
                    k.tt("dve", GyT[:, :], psG[:, 0:128], RT[:, sl], ALU.add)
                    psY = PS[1]
                    seq = [(VP, ARK, 0), (U0P, ARB, 0), (VP, ARK, 1), (U0P, ARB, 1)]
                    for i_, (lt, rt, hh) in enumerate(seq):
                        k.mm(psY[:, 256:384], lt[:, hh, :], rt[:, hh, :], start=(i_ == 0), stop=False)
                    k.mm(psY[:, 256:384], Hbd[:, :], GyT[:, :], start=False, stop=True)
                    k.cpa(YT[:, sl], psY[:, 256:384])
                    psH = PS[2]
                    for hh in range(2):
                        k.mm(psH[:, 256:384], WU[:, hh, :], BHP[:, hh, :], start=(hh == 0), stop=(hh == 1))
                    k.stt(GhT[:, :], ident, gC[:, c:c + 1], psH[:, 256:384], ALU.mult, ALU.add)
                    psS = PS[3]
                    seq = [(KHP, VP, 0), (KHP, VP, 1), (BHP, U0P, 0), (BHP, U0P, 1)]
                    for i_, (lt, rt, hh) in enumerate(seq):
                        k.mm(psS[:, 256:320], lt[:, hh, :], rt[:, hh, hh * 64:(hh + 1) * 64], start=(i_ == 0), stop=False)
                    k.mm(psS[:, 256:320], GhT[:, :], Hs[:, :], start=False, stop=True)
                    k.cp("dve", Hs[:, :], psS[:, 256:320])
                    k.cp("dve", Hbd[0:64, 0:64], Hs[0:64, :])
                    k.cp("act", Hbd[64:128, 64:128], Hs[64:128, :])
                    done("rw5")
                for tcn in range(NT):
                    sl = slice(tcn * 512, (tcn + 1) * 512)
                    ps1, ps2 = PS[4 + tcn % 2], PS[6 + tcn % 2]
                    d_, q_ = c5[0], c5[1]
                    k.mm(ps1[:, :], blk64, YT[:, sl])
                    k.stt(d_[:, :], ps1[:, :], -1.0 / 64, YT[:, sl], ALU.mult, ALU.add)
                    k.act(q_[:, :], d_[:, :], AF.Square)
                    k.mm(ps2[:, :], blk64, q_[:, :])
                    k.ts("dve", q_[:, :], ps2[:, :], 1.0 / 64, ALU.mult, 64e-5, ALU.add)
                    k.act(q_[:, :], q_[:, :], AF.Sqrt)
                    k.recip(q_[:, :], q_[:, :])
                    k.tt("dve", d_[:, :], d_[:, :], q_[:, :], ALU.mult)
                    k.ts("dve", d_[:, :], d_[:, :], cc(42), ALU.mult, cc(46), ALU.add)
                    k.tt("dve", d_[:, :], d_[:, :], BON[:, sl], ALU.add)
                    o_ = ostg[tcn % 2]
                    k.tt("dve", o_[:, :], d_[:, :], G[:, sl], ALU.mult)
                    k.dma("sp", ym_d[r0:r0 + 128, sl], o_[:, :])

        def attn_phase(es, l):
            qT = [k.sb(es, "qT%d" % i, [64, S], BF16) for i in range(4)]
            kT = [k.sb(es, "kT%d" % i, [64, S], BF16) for i in range(4)]
            vt = [k.sb(es, "vt%d" % i, [128, NB, 128], BF16) for i in range(2)]
            TB = [k.sb(es, "TB%d" % i, [128, TBW]) for i in range(2)]
            M0 = k.sb(es, "M0", [128, TBW])
            M1 = k.sb(es, "M1", [128, TBW])
            tS = [k.sb(es, "tS%d" % i, [128, 512]) for i in range(4)]
            eB = [k.sb(es, "eB%d" % i, [128, 512], BF16) for i in range(2)]
            Rr = k.sb(es, "Rr", [128, 512])
            O0 = k.sb(es, "O0", [128, 512])
            O1 = k.sb(es, "O1", [128, 512])
            ostg = [k.sb(es, "aostg%d" % i, [128, 512], BF16) for i in range(2)]
            sub = k.sb(es, "subc", [128, 2])
            k.dma("sp", M0[:, :], m0_d)
            k.dma("sp", M1[:, :], m1_d)
            colload(sub[:, 0:1], diff_subln[l], 1)
            lam_init = 0.8 - 0.6 * math.exp(-0.3 * l)
            k.ts("dve", sub[:, 1:2], sub[:, 0:1], 1.0 - lam_init, ALU.mult)
            vtok3 = [vtok_d[i].rearrange("(b p) c -> p b c", p=128) for i in range(3)]
            cnt = {"u": 0, "a": 0, "t": 0, "o": 0}

            def load_qk(slot, qrow, krow):
                k.dma("pool", qT[slot][:, :], pT_d[qrow:qrow + 64, :])
                k.dma("pool", kT[slot][:, :], pT_d[krow:krow + 64, :])

            Hk = k.sb(es, "Hk", [128, TBW])

            def load_tb(slot, hb):
                src = bass.AP(tensor=bias_h, offset=hb * 2560 + 1, ap=[[1, 128], [1, TBW]])
                k.dma("sp", Hk[:, :], src)
                for i_, c0 in enumerate(range(0, TBW, 512)):
                    w_ = min(512, TBW - c0)
                    ps = PS[i_ % 4]
                    k.mm(ps[:, 0:w_], JX, Hk[:, c0:c0 + w_])
                    k.cpa(TB[slot][:, c0:c0 + w_], ps[:, 0:w_])

            def softmax_pass(qv, kv, vv, dv, tb, qc, dst):
                u = cnt["u"]
                cnt["u"] += 1
                psN, psD = PS[4 + u % 2], PS[6 + u % 2]
                jl = 4 * qc + 4
                for j in range(jl):
                    off = 384 + 512 * qc - 128 * j
                    psA = PS[cnt["a"] % 4]
                    cnt["a"] += 1
                    t_ = tS[cnt["t"] % 2]
                    e_ = eB[cnt["t"] % 2]
                    cnt["t"] += 1
                    k.mm(psA[:, :], kv[:, j * 128:(j + 1) * 128], qv[:, qc * 512:(qc + 1) * 512])
                    k.stt(t_[:, :], psA[:, :], 0.125, tb[:, off:off + 512], ALU.mult, ALU.add)
                    k.act(e_[:, :], t_[:, :], AF.Exp)
                    k.mm(psN[0:dv, :], vv[:, j, 0:dv], e_[:, :], start=(j == 0), stop=(j == jl - 1))
                    k.mm(psD[:, :], onesb[:, :], e_[:, :], start=(j == 0), stop=(j == jl - 1))
                rd = tS[2]
                k.recip(rd[:, :], psD[:, :])
                k.tt("dve", dst[0:dv, :], psN[0:dv, :], rd[0:dv, :], ALU.mult)

            load_qk(0, C_QC, C_KC)
            load_tb(0, 0)
            k.dma("sp", vt[0][:, :, 0:64], vtok3[1][:, :, 0:64])
            for h in range(8):
                s_ = h % 2
                if h + 1 < 8:
                    load_qk(1 - s_, C_QC + (h + 1) * 64, C_KC + (h + 1) * 64)
                    load_tb(1 - s_, h + 1)
                    k.dma("sp", vt[1 - s_][:, :, 0:64], vtok3[1][:, :, (h + 1) * 64:(h + 2) * 64])
                for qc in range(NT):
                    softmax_pass(qT[s_], kT[s_], vt[s_], 64, TB[s_], qc, O0)
                    o_ = ostg[cnt["o"] % 2]
                    cnt["o"] += 1
                    k.cp("act", o_[0:64, :], O0[0:64, :])
                    k.dma("sp", ym_d[1024 + h * 64:1024 + (h + 1) * 64, qc * 512:(qc + 1) * 512], o_[0:64, :])
            for hd in range(4):
                s_ = hd % 2
                for c in range(2):
                    load_qk(2 * s_ + c, C_QD + hd * 128 + c * 64, C_KD + hd * 128 + c * 64)
                load_tb(s_, 8 + hd)
                k.dma("sp", vt[s_][:, :, :], vtok3[2][:, :, hd * 128:(hd + 1) * 128])
                for qc in range(NT):
                    softmax_pass(qT[2 * s_], kT[2 * s_], vt[s_], 128, TB[s_], qc, O0)
                    softmax_pass(qT[2 * s_ + 1], kT[2 * s_ + 1], vt[s_], 128, TB[s_], qc, O1)
                    k.stt(O0[:, :], O1[:, :], lamc[:, l:l + 1], O0[:, :], ALU.mult, ALU.add)
                    sq = tS[3]
                    k.act(sq[:, :], O0[:, :], AF.Square)
                    psX = PS[cnt["a"] % 4]
                    cnt["a"] += 1
                    k.mm(psX[:, :], ones, sq[:, :])
                    k.ts("dve", sq[:, :], psX[:, :], 1.0 / 128, ALU.mult, 1e-5, ALU.add)
                    k.act(sq[:, :], sq[:, :], AF.Sqrt)
                    k.recip(sq[:, :], sq[:, :])
                    k.tt("dve", O0[:, :], O0[:, :], sq[:, :], ALU.mult)
                    o_ = ostg[cnt["o"] % 2]
                    cnt["o"] += 1
                    k.ts("dve", o_[:, :], O0[:, :], sub[:, 1:2], ALU.mult)
                    k.dma("sp", ym_d[1536 + hd * 128:1536 + (hd + 1) * 128, qc * 512:(qc + 1) * 512], o_[:, :])
            load_qk(0, C_QB, C_KB)
            k.dma("sp", vt[0][:, :, 0:64], vtok3[0][:, :, 0:64])
            for h in range(8):
                s_ = h % 2
                if h + 1 < 8:
                    load_qk(1 - s_, C_QB + (h + 1) * 64, C_KB + (h + 1) * 64)
                    k.dma("sp", vt[1 - s_][:, :, 0:64], vtok3[0][:, :, (h + 1) * 64:(h + 2) * 64])
                qv, kv, vv = qT[s_], kT[s_], vt[s_]
                for qc in range(NT):
                    u = cnt["u"]
                    cnt["u"] += 1
                    psN = PS[6 + u % 2]
                    first = True
                    jl = 4 * qc + 4
                    for j in range(jl - 1, -1, -1):
                        off = 384 + 512 * qc - 128 * j
                        diag = j >= 4 * qc
                        a_ = cnt["a"]
                        cnt["a"] += 1
                        psA, psB, psC = PS[a_ % 2], PS[2 + a_ % 2], PS[4 + a_ % 2]
                        e1, sp, u_ = tS[0], tS[1], tS[2]
                        k.mm(psA[:, :], kv[:, j * 128:(j + 1) * 128], qv[:, qc * 512:(qc + 1) * 512])
                        k.act(e1[:, :], psA[:, :], AF.Exp, scale=0.125)
                        k.act(sp[:, :], e1[:, :], AF.Ln, bias=1.0)
                        if diag:
                            k.tt("dve", sp[:, :], sp[:, :], M1[:, off:off + 512], ALU.mult)
                        k.mm(psB[:, :], ML, sp[:, :])
                        if j > 0:
                            k.mm(psC[:, :], ones, sp[:, :])
                        k.stt(u_[:, :], psA[:, :], 0.125, sp[:, :], ALU.mult, ALU.subtract)
                        k.tt("dve", u_[:, :], u_[:, :], psB[:, :], ALU.subtract)
                        if not first:
                            k.tt("dve", u_[:, :], u_[:, :], Rr[:, :], ALU.subtract)
                        if diag:
                            k.tt("dve", u_[:, :], u_[:, :], M0[:, off:off + 512], ALU.add)
                        e_ = eB[a_ % 2]
                        k.act(e_[:, :], u_[:, :], AF.Exp)
                        k.mm(psN[0:64, :], vv[:, j, 0:64], e_[:, :], start=first, stop=(j == 0))
                        if j > 0:
                            if first:
                                k.cp("dve", Rr[:, :], psC[:, :])
                            else:
                                k.tt("dve", Rr[:, :], Rr[:, :], psC[:, :], ALU.add)
                        first = False
                    o_ = ostg[cnt["o"] % 2]
                    cnt["o"] += 1
                    k.cp("act", o_[0:64, :], psN[0:64, :])
                    k.dma("sp", ym_d[512 + h * 64:512 + (h + 1) * 64, qc * 512:(qc + 1) * 512], o_[0:64, :])

        for l in range(NL):
            mc = lambda a, b, l=l: modc[:, l * 96 + a: l * 96 + b]
            ncols_in = NIN + (64 if l == 1 else 0)
            with contextlib.ExitStack() as es:
                hT = k.sb(es, "hT", [128, 16, S], BF16)
                with contextlib.ExitStack() as es2:
                    norm_phase(es2, hT, l, 0, S, A1[:, (l * 2) * 16:(l * 2 + 1) * 16], mc(0, 16))
                P.barrier()
                wts = [k.sb(es, "win%d" % i, [128, 16, 512], BF16) for i in range(2)]
                stg = [k.sb(es, "pstg%d" % i, [128, 512]) for i in range(4)]
                stgb = [k.sb(es, "vstg%d" % i, [128, 512], BF16) for i in range(2)]
                segs = [(0, C_VB), (C_QC, C_VC), (C_QD, C_VD)]
                blocks = []
                for (a, b) in segs:
                    for c0 in range(a, b, 512):
                        blocks.append(("fm", c0, min(512, b - c0)))
                if l == 1:
                    blocks.append(("pv", C_PV, 64))
                for i, c0 in enumerate((C_VB, C_VC, C_VD)):
                    blocks.append(("tm", c0, 512, i))

                def load_block(bi):
                    blk = blocks[bi]
                    wt = wts[bi % 2]
                    if blk[0] == "pv":
                        wload(wt, vres_down[0], 16, 64)
                    else:
                        wload(wt, w_in[l][:, blk[1]:blk[1] + blk[2]], 16, blk[2])
                load_block(0)
                nps = 0
                for bi, blk in enumerate(blocks):
                    if bi + 1 < len(blocks):
                        load_block(bi + 1)
                    wt = wts[bi % 2]
                    if blk[0] in ("fm", "pv"):
                        c0, wd = blk[1], blk[2]
                        for m in range(0, wd, 128):
                            mw = min(128, wd - m)
                            for tcn in range(NT):
                                ps = PS[nps % 4]
                                nps += 1
                                for kc in range(16):
                                    k.mm(ps[0:mw, :], wt[:, kc, m:m + mw], hT[:, kc, tcn * 512:(tcn + 1) * 512],
                                         start=(kc == 0), stop=(kc == 15))
                                st = stg[nps % 4]
                                k.cpa(st[0:mw, :], ps[0:mw, :])
                                k.dma("sp", pT_d[c0 + m:c0 + m + mw, tcn * 512:(tcn + 1) * 512], st[0:mw, :])
                    else:
                        vi = blk[3]
                        for tb in range(NB):
                            ps = PS[nps % 4]
                            nps += 1
                            for kc in range(16):
                                k.mm(ps[:, :], hT[:, kc, tb * 128:(tb + 1) * 128], wt[:, kc, :],
                                     start=(kc == 0), stop=(kc == 15))
                            st = stgb[nps % 2]
                            k.cpa(st[:, :], ps[:, :])
                            k.dma("sp", vtok_d[vi, tb * 128:(tb + 1) * 128, :], st[:, :])
            P.barrier()
            done("gemm1")

            with contextlib.ExitStack() as es:
                rwkv_phase(es, l)
            P.barrier()
            done("rwkv")
            with contextlib.ExitStack() as es:
                attn_phase(es, l)
            P.barrier()
            done("attn")

            with contextlib.ExitStack() as es:
                ymT = k.sb(es, "ymT", [128, 16, S], BF16)
                for kc in range(16):
                    k.dma("sp", ymT[:, kc, :], ym3[:, kc, :])
                wts = [k.sb(es, "wout%d" % i, [128, 16, 512], BF16) for i in range(2)]
                xst = [k.sb(es, "xst%d" % i, [128, 512]) for i in range(4)]
                wload(wts[0], w_out[l][:, 0:512], 16, 512)
                nps = 0
                for bi in range(4):
                    if bi + 1 < 4:
                        wload(wts[(bi + 1) % 2], w_out[l][:, (bi + 1) * 512:(bi + 2) * 512], 16, 512)
                    wt = wts[bi % 2]
                    for m in range(4):
                        dc = bi * 4 + m
                        for tcn in range(NT):
                            ps = PS[nps % 4]
                            nps += 1
                            for kc in range(16):
                                k.mm(ps[:, :], wt[:, kc, m * 128:(m + 1) * 128], ymT[:, kc, tcn * 512:(tcn + 1) * 512],
                                     start=(kc == 0), stop=(kc == 15))
                            resid_epilogue(xst, ps, dc * 128, slice(tcn * 512, (tcn + 1) * 512), mc(32 + dc, 33 + dc))
            P.barrier()
            done("wout")

            HT = min(S, 1024)
            for half in range(S // HT):
                t0 = half * HT
                with contextlib.ExitStack() as es:
                    h2T = k.sb(es, "h2T", [128, 16, HT], BF16)
                    with contextlib.ExitStack() as es2:
                        norm_phase(es2, h2T, l, t0, HT, A1[:, (l * 2 + 1) * 16:(l * 2 + 2) * 16], mc(48, 64))
                    P.barrier()
                    aT = k.sb(es, "aT", [128, FC, HT], BF16)
                    w1 = [k.sb(es, "w1_%d" % i, [128, 16, 128], BF16) for i in range(2)]
                    w3 = [k.sb(es, "w3_%d" % i, [128, 16, 128], BF16) for i in range(2)]
                    w2 = [k.sb(es, "w2_%d" % i, [128, FC, 128], BF16) for i in range(2)]
                    sg = [k.sb(es, "sg%d" % i, [128, 512]) for i in range(2)]
                    xst = [k.sb(es, "fxst%d" % i, [128, 512]) for i in range(4)]

                    def ld13(f):
                        wload(w1[f % 2], ffn_w13[l][:, f * 128:(f + 1) * 128], 16, 128)
                        wload(w3[f % 2], ffn_w13[l][:, DFF + f * 128: DFF + (f + 1) * 128], 16, 128)
                    ld13(0)
                    nps = 0
                    for f in range(FC):
                        if f + 1 < FC:
                            ld13(f + 1)
                        for tcn in range(HT // 512):
                            pg, pu = PS[(nps * 2) % 8], PS[(nps * 2 + 1) % 8]
                            nps += 1
                            ts_ = slice(tcn * 512, (tcn + 1) * 512)
                            for kc in range(16):
                                k.mm(pg[:, :], w1[f % 2][:, kc, :], h2T[:, kc, ts_], start=(kc == 0), stop=(kc == 15))
                            for kc in range(16):
                                k.mm(pu[:, :], w3[f % 2][:, kc, :], h2T[:, kc, ts_], start=(kc == 0), stop=(kc == 15))
                            s_ = sg[nps % 2]
                            k.act(s_[:, :], pg[:, :], AF.Silu)
                            k.tt("dve", aT[:, f, ts_], s_[:, :], pu[:, :], ALU.mult)

                    def ld2(dc):
                        src = ffn_w2[l][:, dc * 128:(dc + 1) * 128].rearrange("(c p) n -> p c n", p=128)
                        wt = w2[dc % 2]
                        for a in range(0, FC, 11):
                            k.dma("pool", wt[:, a:a + 11, :], src[:, a:a + 11, :])
                    ld2(0)
                    for dc in range(16):
                        if dc + 1 < 16:
                            ld2(dc + 1)
                        for tcn in range(HT // 512):
                            ps = PS[nps % 8]
                            nps += 1
                            ts_ = slice(tcn * 512, (tcn + 1) * 512)
                            for f in range(FC):
                                k.mm(ps[:, :], w2[dc % 2][:, f, :], aT[:, f, ts_], start=(f == 0), stop=(f == FC - 1))
                            resid_epilogue(xst, ps, dc * 128, slice(t0 + tcn * 512, t0 + (tcn + 1) * 512),
                                           mc(80 + dc, 81 + dc))
                P.barrier()

        with contextlib.ExitStack() as es:
            hn = k.sb(es, "hn", [128, 16, 512])
            ost = [k.sb(es, "ost%d" % i, [128, D]) for i in range(2)]

            for tcn in range(NT):
                with contextlib.ExitStack() as es2:
                    norm_phase(es2, hn, 0, tcn * 512, 512, gcol[:, 64:80], None)
                P.barrier()
                for q in range(4):
                    tb = tcn * 4 + q
                    o_ = ost[tb % 2]
                    for g in range(4):
                        ps = PS[(tb * 4 + g) % 8]
                        for j in range(4):
                            kc = g * 4 + j
                            k.tr(ps[:, j * 128:(j + 1) * 128], hn[:, kc, q * 128:(q + 1) * 128], ident)
                        k.cpa(o_[:, g * 512:(g + 1) * 512], ps[:, :])
                    k.dma("sp", out_d[tb * 128:(tb + 1) * 128, :], o_[:, :])
                P.barrier()
    except StopBuild:
        pass
    P.barrier()
    P.emit(es0)
    ncd.close()
    try:
        es0.close()
    except AssertionError:
        pass
    return nc


_CACHE = {}


def make_in_maps(inputs, S, nb):
    consts = host_consts(S)
    shared = {}
    for name, arr in inputs.items():
        if name in ("x", "c"):
            continue
        a = np.asarray(arr, dtype=np.float32)
        if name == "diff_lambda":
            a = a.reshape(2, 256)
        if name == "rwkv_r_k":
            a = a.reshape(2, 512)
        shared[name] = np.ascontiguousarray(a)
    shared.update(consts)
    x = np.asarray(inputs["x"], dtype=np.float32)
    c = np.asarray(inputs["c"], dtype=np.float32)
    maps = []
    for b in range(nb):
        m = dict(shared)
        m["x"] = np.ascontiguousarray(x[b, :S])
        m["c"] = np.ascontiguousarray(c[b])
        maps.append(m)
    return maps


def kernel(**inputs):
    S = 2048
    nc = build(S, 2)
    maps = make_in_maps(inputs, S, 8)
    res = run_bass_kernel_spmd(nc, maps, core_ids=list(range(8)))
    out = np.stack([np.asarray(r["out"], dtype=np.float32) for r in res.results], axis=0)
    return out
```

```python
import math
import contextlib
import numpy as np
import concourse.bass as bass
import concourse.mybir as mybir
from concourse.bass_utils import run_bass_kernel_spmd

F32 = mybir.dt.float32
BF16 = mybir.dt.bfloat16
AF = mybir.ActivationFunctionType
ALU = mybir.AluOpType

D = 2048
KC = 16
DFF = 5632
FC = 44
NIN = 6784
NEG = -30000.0
C_R, C_K, C_V, C_WLO, C_ALO, C_GLO = 0, 512, 1024, 1536, 1632, 1728
C_QB, C_KB, C_VB = 2176, 2688, 3200
C_QC, C_KC, C_VC = 3712, 4224, 4736
C_QD, C_KD, C_VD = 5248, 5760, 6272
C_PV = 6784
TBW = 2432


class View:
    __slots__ = ("t", "ap")

    def __init__(self, t, ap):
        self.t = t
        self.ap = ap

    def m(self, f):
        return View(self.t, f(self.ap))

    def __getitem__(self, k):
        return View(self.t, self.ap[k])


class T:
    __slots__ = ("h", "lw", "rd", "name", "psum")

    def __init__(self, h, name="", psum=False):
        self.h = h
        self.lw = None
        self.rd = []
        self.name = name
        self.psum = psum

    def __getitem__(self, k):
        return View(self, self.h[k])

    def v(self, ap):
        return View(self, ap)


def _ap(x):
    return x.ap if isinstance(x, View) else x


def _ts(*xs):
    return [x.t for x in xs if isinstance(x, View)]


class Prog:
    NQ = 8

    def __init__(self, nc):
        self.nc = nc
        self.ops = []
        self.pos = {}
        self.last = {}
        self.bar = None
        self.bar_seen = set()
        self.unconsumed = set()

    def op(self, eng, fn, R=(), W=(), dma=False, extra=()):
        idx = len(self.ops)
        deps = set(extra)
        for t in R:
            if t.lw is not None:
                deps.add(t.lw)
            if t.psum:
                deps.update(r for r in t.rd if self.ops[r]["eng"] != eng)
        for t in W:
            if t.lw is not None:
                deps.add(t.lw)
            deps.update(t.rd)
        if self.bar is not None and eng not in self.bar_seen:
            deps.add(self.bar)
            self.bar_seen.add(eng)
        pos = self.pos.get(eng, 0)
        keep = set()
        for d in deps:
            o = self.ops[d]
            if o["eng"] == eng and not o["dma"] and not dma:
                if eng == "pe":
                    continue
                if pos - o["pos"] > 3:
                    continue
            keep.add(d)
            if o["dma"]:
                self.unconsumed.discard(d)
        best = {}
        keep2 = set()
        for d in keep:
            o = self.ops[d]
            if o["dma"]:
                keep2.add(d)
            elif best.get(o["eng"], -1) < d:
                best[o["eng"]] = d
        keep2.update(best.values())
        keep = keep2
        self.ops.append(dict(eng=eng, fn=fn, deps=keep, dma=dma, pos=pos))
        self.pos[eng] = pos + 1
        self.last[eng] = idx
        if dma:
            self.unconsumed.add(idx)
        for t in R:
            t.rd.append(idx)
        for t in W:
            t.lw = idx
            t.rd = []
        return idx

    def barrier(self):
        deps = set(self.last.values()) | set(self.unconsumed)
        self.unconsumed = set()
        nc = self.nc
        self.bar = None
        idx = self.op("sp", lambda: nc.sync.nop(nofuse=True), extra=deps)
        self.bar = idx
        self.bar_seen = {"sp"}
        return idx

    def emit(self, es):
        nc = self.nc
        engs = {"pe": nc.tensor, "act": nc.scalar, "dve": nc.vector, "pool": nc.gpsimd, "sp": nc.sync}
        ops = self.ops
        flagged = [False] * len(ops)
        for o in ops:
            for d in o["deps"]:
                flagged[d] = True
        sems = {}
        for e in engs:
            sems[e] = es.enter_context(nc.semaphore("s_" + e))
        dsems = {}
        for e in ("sp", "pool"):
            dsems[e] = [es.enter_context(nc.semaphore("d_%s%d" % (e, i))) for i in range(self.NQ)]
        cnt = {e: 0 for e in engs}
        dcnt = {"sp": 0, "pool": 0}
        known = {e: {} for e in engs}
        ev = [None] * len(ops)
        for idx, o in enumerate(ops):
            e = o["eng"]
            E = engs[e]
            need = {}
            for d in o["deps"]:
                sm, val = ev[d]
                if need.get(sm, (None, 0))[1] < val:
                    need[sm] = (sm, val)
            if o["dma"]:
                k = dcnt[e]
                sm = dsems[e][k % self.NQ]
                prev = 16 * (k // self.NQ)
                if prev > 0 and need.get(sm, (None, 0))[1] < prev:
                    need[sm] = (sm, prev)
            for sm, val in need.values():
                key = id(sm)
                if known[e].get(key, 0) >= val:
                    continue
                E.wait_ge(sm, val)
                known[e][key] = val
            ins = o["fn"]()
            if o["dma"]:
                k = dcnt[e]
                sm = dsems[e][k % self.NQ]
                ins.then_inc(sm, 16)
                ev[idx] = (sm, 16 * (k // self.NQ + 1))
                dcnt[e] = k + 1
            elif flagged[idx]:
                cnt[e] += 1
                ins.then_inc(sems[e], 1)
                ev[idx] = (sems[e], cnt[e])


class K:
    def __init__(self, nc, P, es):
        self.nc = nc
        self.P = P
        self.es = es
        self.eng = {"act": nc.scalar, "dve": nc.vector, "pool": nc.gpsimd}
        self.rr = 0

    def sb(self, es, name, shape, dt=F32):
        self.uid = getattr(self, "uid", 0) + 1
        name = "t%d_%s" % (self.uid, name)
        return T(es.enter_context(self.nc.sbuf_tensor(name, list(shape), dt)), name)

    def dma(self, q, out, in_):
        nc = self.nc
        E = nc.sync if q == "sp" else nc.gpsimd
        o, i = _ap(out), _ap(in_)
        return self.P.op(q, lambda: E.dma_start(out=o, in_=i), R=_ts(in_), W=_ts(out), dma=True)

    def mm(self, out, lhsT, rhs, start=True, stop=True):
        nc = self.nc
        o, l, r = out.ap, lhsT.ap, rhs.ap
        return self.P.op("pe", lambda: nc.tensor.matmul(o, l, r, start=start, stop=stop),
                         R=_ts(lhsT, rhs), W=_ts(out))

    def tr(self, out, in_, ident):
        nc = self.nc
        o, i, d = out.ap, in_.ap, ident.ap
        return self.P.op("pe", lambda: nc.tensor.transpose(o, i, d), R=_ts(in_, ident), W=_ts(out))

    def tt(self, eng, out, in0, in1, op):
        E = self.eng[eng]
        o, a, b = out.ap, _ap(in0), _ap(in1)
        return self.P.op(eng, lambda: E.tensor_tensor(out=o, in0=a, in1=b, op=op), R=_ts(in0, in1), W=_ts(out))

    def ts(self, eng, out, in0, s1, op0, s2=None, op1=None):
        E = self.eng[eng]
        o, a = out.ap, _ap(in0)
        x1, x2 = _ap(s1), _ap(s2)
        if op1 is None:
            f = lambda: E.tensor_scalar(out=o, in0=a, scalar1=x1, scalar2=None, op0=op0)
        else:
            f = lambda: E.tensor_scalar(out=o, in0=a, scalar1=x1, scalar2=x2, op0=op0, op1=op1)
        return self.P.op(eng, f, R=_ts(in0, s1, s2), W=_ts(out))

    def stt(self, out, in0, sc, in1, op0, op1):
        E = self.nc.vector
        o, a, s, b = out.ap, _ap(in0), _ap(sc), _ap(in1)
        return self.P.op("dve", lambda: E.scalar_tensor_tensor(out=o, in0=a, scalar=s, in1=b, op0=op0, op1=op1),
                         R=_ts(in0, sc, in1), W=_ts(out))

    def cp(self, eng, out, in_):
        o, a = out.ap, _ap(in_)
        if eng == "act":
            E = self.nc.scalar
            return self.P.op(eng, lambda: E.copy(out=o, in_=a), R=_ts(in_), W=_ts(out))
        E = self.eng[eng]
        return self.P.op(eng, lambda: E.tensor_copy(out=o, in_=a), R=_ts(in_), W=_ts(out))

    def cpa(self, out, in_):
        self.rr ^= 1
        return self.cp("act" if self.rr else "dve", out, in_)

    def act(self, out, in_, func, bias=None, scale=1.0):
        E = self.nc.scalar
        o, a, b = out.ap, _ap(in_), _ap(bias)
        sc = _ap(scale)
        kw = {}
        if bias is not None:
            kw["bias"] = b
        f = lambda: E.activation(out=o, in_=a, func=func, scale=sc, **kw)
        return self.P.op("act", f, R=_ts(in_, bias, scale), W=_ts(out))

    def recip(self, out, in_):
        E = self.nc.vector
        o, a = out.ap, _ap(in_)
        return self.P.op("dve", lambda: E.reciprocal(out=o, in_=a), R=_ts(in_), W=_ts(out))

    def memset(self, eng, out, val):
        E = self.eng[eng]
        o = out.ap
        return self.P.op(eng, lambda: E.memset(o, val), W=_ts(out))

    def scan(self, out, d0, d1):
        E = self.nc.vector
        o, a, b = out.ap, _ap(d0), _ap(d1)
        return self.P.op("dve", lambda: E.tensor_tensor_scan(out=o, data0=a, data1=b, initial=0.0,
                                                             op0=ALU.mult, op1=ALU.add),
                         R=_ts(d0, d1), W=_ts(out))


def host_consts(S):
    c = {}
    i = np.arange(128)
    ident = np.eye(128, dtype=np.float32)
    ones = np.ones((128, 128), np.float32)
    blk = np.zeros((128, 128), np.float32)
    blk[:64, :64] = 1
    blk[64:, 64:] = 1
    MU = (i[None, :] > i[:, None]).astype(np.float32)
    MU0 = (i[None, :] >= i[:, None]).astype(np.float32)
    ML = (i[:, None] > i[None, :]).astype(np.float32)
    c["cst"] = np.concatenate([ident, ones, blk, MU, MU0, ML, ident[::-1].copy(), ident, ident, MU, MU, MU0, MU0, ML, ML], axis=1)
    r = np.arange(-512, 2048)
    rp = np.maximum(r, 0)
    d_f = np.maximum(rp, 1).astype(np.float32)
    large = 16 + (np.log(d_f / np.float32(16)) / np.float32(math.log(2048 / 16)) * np.float32(16)).astype(np.int32)
    large = np.minimum(large, 31)
    bucket = np.where(rp < 16, rp, large)
    E = np.zeros((34, 2560), np.float32)
    E[bucket, np.arange(2560)] = 1.0
    E[:32, r < 0] = 0.0
    cnt = ((rp <= 128).astype(np.int32) + ((rp % 4 == 0) & (rp <= 512)).astype(np.int32)
           + ((rp % 16 == 0) & (rp <= 2048)).astype(np.int32))
    exC = np.where(cnt > 0, np.log(np.maximum(cnt, 1).astype(np.float64)), NEG).astype(np.float32)
    exC[r < 0] = NEG
    exD = np.where(r < 0, NEG, 0.0).astype(np.float32)
    E[32] = exC
    E[33] = exD
    c["e34"] = E
    ind = np.zeros((2, 12), np.float32)
    ind[0, :8] = 1
    ind[1, 8:] = 1
    c["ind2"] = ind
    xx = np.arange(TBW)[None, :] - 384 - i[:, None]
    c["m0"] = np.where(xx > 0, 0.0, NEG).astype(np.float32)
    c["m1"] = (xx > 0).astype(np.float32)
    rm = np.ones((128, S), np.float32)
    rm[:, ::128] = 0.0
    c["rmask"] = rm
    return c


class StopBuild(Exception):
    pass


def build(S=2048, NL=2, dbg=False, stop_after=None):
    nc = bass.Bass("TRN2", target_bir_lowering=False)
    NT = S // 512
    NB = S // 128
    es0 = contextlib.ExitStack()
    P = Prog(nc)
    k = K(nc, P, es0)

    def din(name, shape):
        return nc.dram_tensor(name, list(shape), F32, kind="ExternalInput").ap()

    x_d = din("x", [S, D])
    c_d = din("c", [D])
    w_ada = din("w_ada", [2, D, 6 * D])
    b_ada = din("b_ada", [2, 6 * D])
    norm_gain = din("norm_gain", [2, 2, D])
    w_in = din("w_in", [2, D, NIN])
    w_out = din("w_out", [2, D, D])
    rel_bias = din("rel_bias", [32, 12])
    rwkv_mu = din("rwkv_mu", [2, 2176])
    rwkv_w0 = din("rwkv_w0", [2, 512])
    rwkv_w_up = din("rwkv_w_up", [2, 96, 512])
    rwkv_a0 = din("rwkv_a0", [2, 512])
    rwkv_a_up = din("rwkv_a_up", [2, 96, 512])
    rwkv_g_up = din("rwkv_g_up", [2, 448, 512])
    rwkv_k_k = din("rwkv_k_k", [2, 512])
    rwkv_k_a = din("rwkv_k_a", [2, 512])
    rwkv_r_k = din("rwkv_r_k", [2, 512])
    rwkv_ln_w = din("rwkv_ln_w", [2, 512])
    rwkv_ln_b = din("rwkv_ln_b", [2, 512])
    vres_down = din("vres_down", [1, D, 64])
    vres_mu = din("vres_mu", [1, 64])
    vres_up = din("vres_up", [1, 64, 512])
    vres_bias = din("vres_bias", [1, 512])
    diff_lambda = din("diff_lambda", [2, 256])
    diff_subln = din("diff_subln", [2, 128])
    ffn_w13 = din("ffn_w13", [2, D, 2 * DFF])
    ffn_w2 = din("ffn_w2", [2, DFF, D])
    final_gain = din("final_gain", [D])
    cst_d = din("cst", [128, 1920])
    e34_d = din("e34", [34, 2560])
    ind2_d = din("ind2", [2, 12])
    m0_d = din("m0", [128, TBW])
    m1_d = din("m1", [128, TBW])
    rmask_d = din("rmask", [128, S])
    out_d = nc.dram_tensor("out", [S, D], F32, kind="ExternalOutput").ap()

    okind = "ExternalOutput" if dbg else "Internal"
    xT_h = nc.dram_tensor("xT", [D, S], F32, kind=okind)
    pT_h = nc.dram_tensor("pT", [NIN + 64, S], F32, kind=okind)
    ym_h = nc.dram_tensor("ymT", [D, S], BF16, kind=okind)
    vtok_h = nc.dram_tensor("vtok", [3, S, 512], BF16, kind="Internal")
    bias_h = nc.dram_tensor("biasd", [12, 2560], F32, kind=okind)
    vf_h = nc.dram_tensor("vfT", [512, S], F32, kind="Internal")
    mod_h = nc.dram_tensor("modd", [128, 192], F32, kind=okind)
    xT_d, pT_d, ym_d, vtok_d, bias_d, vf_d = (h.ap() for h in (xT_h, pT_h, ym_h, vtok_h, bias_h, vf_h))
    xT3 = xT_d.rearrange("(c p) s -> p c s", p=128)
    ym3 = ym_d.rearrange("(c p) s -> p c s", p=128)

    cst = k.sb(es0, "cst", [128, 1920])
    ident, ones, blk64, MU, MU0, ML, JX = (cst[:, i * 128:(i + 1) * 128] for i in range(7))
    onesb = k.sb(es0, "onesb", [128, 128], BF16)
    modc = k.sb(es0, "modc", [128, 2 * 96])
    gcol = k.sb(es0, "gcol", [128, 4 * 16 + 16])
    A1 = k.sb(es0, "A1", [128, 2 * 2 * 16])
    lamc = k.sb(es0, "lamc", [128, 4])
    PS = [T(es0.enter_context(nc.psum_tensor("ps%d" % i, [128, 512], F32)), "ps%d" % i, psum=True) for i in range(8)]
    ncd = contextlib.ExitStack()
    ncd.enter_context(nc.allow_non_contiguous_dma(reason="small per-channel parameter vectors"))

    def colload(dst, vec, n):
        k.dma("sp", dst, vec.rearrange("(c p) -> p c", p=128))

    k.dma("sp", cst[:, :], cst_d)
    k.cp("dve", onesb[:, :], ones)

    def done(tag):
        if stop_after == tag:
            raise StopBuild()
    try:

        with contextlib.ExitStack() as es:
            condT = k.sb(es, "condT", [128, 16])
            colload(condT[:, :], c_d, 16)
            k.act(condT[:, :], condT[:, :], AF.Silu)
            wb = [k.sb(es, "wada%d" % i, [128, 16, 512]) for i in range(2)]
            bcol = k.sb(es, "bcol", [128, 192])
            for l in range(NL):
                colload(bcol[:, l * 96:(l + 1) * 96], b_ada[l], 96)
                for i in range(2):
                    colload(gcol[:, (l * 2 + i) * 16:(l * 2 + i + 1) * 16], norm_gain[l, i], 16)
            colload(gcol[:, 64:80], final_gain, 16)
            for l in range(NL):
                for nb in range(24):
                    w = wb[nb % 2]
                    src = w_ada[l][:, nb * 512:(nb + 1) * 512].rearrange("(c p) n -> p c n", p=128)
                    k.dma("sp", w[:, 0:8, :], src[:, 0:8, :])
                    k.dma("sp", w[:, 8:16, :], src[:, 8:16, :])
                    for m in range(4):
                        j = nb * 4 + m
                        for kc in range(16):
                            k.mm(PS[0][:, j:j + 1], w[:, kc, m * 128:(m + 1) * 128], condT[:, kc:kc + 1],
                                 start=(kc == 0), stop=(kc == 15))
                k.tt("dve", modc[:, l * 96:(l + 1) * 96], PS[0][:, 0:96], bcol[:, l * 96:(l + 1) * 96], ALU.add)
                for i in range(2):
                    sc = modc[:, l * 96 + i * 48 + 16: l * 96 + i * 48 + 32]
                    k.stt(A1[:, (l * 2 + i) * 16:(l * 2 + i + 1) * 16], sc, 1.0,
                          gcol[:, (l * 2 + i) * 16:(l * 2 + i + 1) * 16], ALU.add, ALU.mult)
            if dbg:
                k.dma("sp", mod_h.ap()[:, 0:NL * 96], modc[:, 0:NL * 96])
            lp = k.sb(es, "lp", [1, 512])
            lsum = k.sb(es, "lsum", [1, 8])
            for l in range(NL):
                k.dma("sp", lp[0:1, 0:256], diff_lambda[l:l + 1, :])
                k.tt("dve", lp[0:1, 256:320], lp[0:1, 0:64], lp[0:1, 64:128], ALU.mult)
                k.tt("dve", lp[0:1, 320:384], lp[0:1, 128:192], lp[0:1, 192:256], ALU.mult)
                nc_ = nc
                o1, i1 = lsum[0:1, 0:2].ap, lp[0:1, 256:384].ap.rearrange("p (a b) -> p a b", a=2)
                P.op("dve", lambda o1=o1, i1=i1: nc_.vector.tensor_reduce(out=o1, in_=i1, axis=mybir.AxisListType.X,
                                                                         op=ALU.add), R=[lp], W=[lsum])
                k.act(lsum[0:1, 2:4], lsum[0:1, 0:2], AF.Exp)
                lam_init = 0.8 - 0.6 * math.exp(-0.3 * l)
                k.tt("dve", lsum[0:1, 4:5], lsum[0:1, 3:4], lsum[0:1, 2:3], ALU.subtract)
                k.ts("dve", lsum[0:1, 5:6], lsum[0:1, 4:5], -lam_init, ALU.add)
                k.mm(PS[1][:, l:l + 1], ones[0:1, :], lsum[0:1, 5:6])
                k.cp("dve", lamc[:, l:l + 1], PS[1][:, l:l + 1])
        P.barrier()
        done("mod")

        with contextlib.ExitStack() as es:
            l34 = k.sb(es, "l34", [34, 12])
            e34 = k.sb(es, "e34", [34, 2560])
            bf = k.sb(es, "bf", [12, 2560])
            k.dma("sp", l34[0:32, :], rel_bias)
            k.dma("sp", l34[32:34, :], ind2_d)
            k.dma("sp", e34[:, :], e34_d)
            for i in range(5):
                k.mm(PS[i][0:12, :], l34[:, :], e34[:, i * 512:(i + 1) * 512])
                k.cp("dve", bf[:, i * 512:(i + 1) * 512], PS[i][0:12, :])
            k.dma("sp", bias_d, bf[:, :])
        P.barrier()
        done("bias")

        with contextlib.ExitStack() as es:
            xin = [k.sb(es, "xin%d" % i, [128, D]) for i in range(2)]
            stg = [k.sb(es, "xstg%d" % i, [128, 16, 128]) for i in range(2)]
            for tb in range(NB):
                xi, st = xin[tb % 2], stg[tb % 2]
                k.dma("sp", xi[:, :], x_d[tb * 128:(tb + 1) * 128, :])
                for g in range(4):
                    ps = PS[(tb * 4 + g) % 8]
                    for q in range(4):
                        kc = g * 4 + q
                        k.tr(ps[:, q * 128:(q + 1) * 128], xi[:, kc * 128:(kc + 1) * 128], ident)
                    k.cpa(st[:, g * 4:(g + 1) * 4, :], ps[:, :].m(lambda a: a.rearrange("p (q t) -> p q t", q=4)))
                k.dma("sp", xT3[:, :, tb * 128:(tb + 1) * 128], st[:, :, :])
        P.barrier()
        done("x0")

        def norm_phase(es, hT, li, t0, ntok, acol, bcolv):
            xt = k.sb(es, "nx", [128, 16, 512])
            sq = [k.sb(es, "nsq%d" % i, [128, 512]) for i in range(2)]
            rs = k.sb(es, "nrs", [128, 512])
            tmp = [k.sb(es, "ntmp%d" % i, [128, 512]) for i in range(2)]
            for tcn in range(ntok // 512):
                k.dma("sp", xt[:, :, :], xT3[:, :, t0 + tcn * 512: t0 + (tcn + 1) * 512])
                ps = PS[tcn % 2]
                for kc in range(16):
                    s_ = sq[kc % 2]
                    k.act(s_[:, :], xt[:, kc, :], AF.Square)
                    k.mm(ps[:, :], ones, s_[:, :], start=(kc == 0), stop=(kc == 15))
                k.ts("dve", rs[:, :], ps[:, :], 1.0 / D, ALU.mult, 1e-6, ALU.add)
                k.act(rs[:, :], rs[:, :], AF.Sqrt)
                k.recip(rs[:, :], rs[:, :])
                for kc in range(16):
                    t_ = tmp[kc % 2]
                    k.tt("dve", t_[:, :], xt[:, kc, :], rs[:, :], ALU.mult)
                    dst = hT[:, kc, tcn * 512:(tcn + 1) * 512]
                    if bcolv is None:
                        k.ts("dve", dst, t_[:, :], acol.m(lambda a, kc=kc: a[:, kc:kc + 1]), ALU.mult)
                    else:
                        k.act(dst, t_[:, :], AF.Identity, bias=bcolv.m(lambda a, kc=kc: a[:, kc:kc + 1]),
                              scale=acol.m(lambda a, kc=kc: a[:, kc:kc + 1]))

        def wload(wt, src, kcs, width):
            s3 = src.rearrange("(c p) n -> p c n", p=128)
            step = 8
            for a in range(0, kcs, step):
                b = min(kcs, a + step)
                k.dma("pool", wt[:, a:b, 0:width], s3[:, a:b, :])

        def resid_epilogue(es_stage, ps, c0, tcs, gc):
            xs = es_stage[resid_epilogue.n % len(es_stage)]
            resid_epilogue.n += 1
            k.dma("sp", xs[:, :], xT_d[c0:c0 + 128, tcs])
            k.stt(xs[:, :], ps[:, :], gc, xs[:, :], ALU.mult, ALU.add)
            k.dma("sp", xT_d[c0:c0 + 128, tcs], xs[:, :])
        resid_epilogue.n = 0

        def diag_ap(t):
            base = t.h[:, :, :]
            return View(t, bass.AP(tensor=base.tensor, offset=base.offset, ap=[list(base.ap[0]), [192, 2], [1, 64]]))

        def bc2(v):
            return v.m(lambda a: a.unsqueeze(1).to_broadcast([128, 2, 128]))

        def shift_into(dst, rows, praw, tmp, mucol, r0):
            k.dma("sp", praw[0:rows, 16:S + 16], pT_d[r0:r0 + rows, :])
            k.tt("dve", tmp[0:rows, 0:S], praw[0:rows, 15:S + 15], praw[0:rows, 16:S + 16], ALU.subtract)
            k.stt(dst[0:rows, 0:S], tmp[0:rows, 0:S], mucol, praw[0:rows, 16:S + 16], ALU.mult, ALU.add)

        def rwkv_phase(es, l):
            SW = S + 16
            names = ["raw", "ta", "tb", "KS", "RS", "VS", "LW", "AA", "KK", "KP", "BB", "CU", "BT"]
            Wt = {n: k.sb(es, "rw_" + n, [128, SW]) for n in names}
            raw, ta, tb = Wt["raw"], Wt["ta"], Wt["tb"]
            KS, RS, VS, LW, AA, KK, KP, BB, CU, BT = (Wt[n] for n in names[3:])
            AT, KT, RT, KH, BH, BON, G, YT = KS, LW, AA, KK, tb, CU, raw, BB
            TWt = k.sb(es, "TWt", [96, S])
            TAt = k.sb(es, "TAt", [96, S])
            TG = k.sb(es, "TG", [128, 4, S], BF16)
            wup = k.sb(es, "wup", [96, 512])
            aup = k.sb(es, "aup", [96, 512])
            gup = k.sb(es, "gup", [128, 4, 512], BF16)
            cols = k.sb(es, "rwcols", [128, 64])
            c5 = [k.sb(es, "c5_%d" % i, [128, 512]) for i in range(3)]
            ostg = [k.sb(es, "rwo%d" % i, [128, 512], BF16) for i in range(2)]
            k.memset("dve", raw[:, 15:16], 0.0)
            colload(cols[:, 0:12], rwkv_mu[l][0:1536], 12)
            k.dma("sp", cols[0:96, 12:13], rwkv_mu[l][C_WLO:C_WLO + 96].rearrange("(c p) -> p c", p=96))
            k.dma("sp", cols[0:96, 13:14], rwkv_mu[l][C_ALO:C_ALO + 96].rearrange("(c p) -> p c", p=96))
            colload(cols[:, 14:17], rwkv_mu[l][C_GLO:C_GLO + 384], 3)
            k.dma("sp", cols[0:64, 17:18], rwkv_mu[l][C_GLO + 384:C_GLO + 448].rearrange("(c p) -> p c", p=64))
            colload(cols[:, 18:22], rwkv_w0[l], 4)
            colload(cols[:, 22:26], rwkv_a0[l], 4)
            colload(cols[:, 26:30], rwkv_k_k[l], 4)
            colload(cols[:, 30:34], rwkv_k_a[l], 4)
            k.ts("dve", cols[:, 34:38], cols[:, 30:34], -1.0, ALU.mult, 1.0, ALU.add)
            colload(cols[:, 38:42], rwkv_r_k[l], 4)
            colload(cols[:, 42:46], rwkv_ln_w[l], 4)
            colload(cols[:, 46:50], rwkv_ln_b[l], 4)
            k.dma("sp", wup[:, :], rwkv_w_up[l])
            k.dma("sp", aup[:, :], rwkv_a_up[l])
            k.memset("dve", gup[:, 3, :], 0.0)
            k.memset("dve", TG[:, 3, :], 0.0)
            for gi in range(4):
                rows = 128 if gi < 3 else 64
                k.dma("pool", gup[0:rows, gi, :], rwkv_g_up[l][gi * 128: gi * 128 + rows, :])
            if l == 1:
                PVs = k.sb(es, "PVs", [64, S])
                vup = k.sb(es, "vup", [64, 512])
                k.dma("sp", cols[0:64, 50:51], vres_mu[0].rearrange("(c p) -> p c", p=64))
                colload(cols[:, 51:55], vres_bias[0], 4)
                k.dma("sp", vup[:, :], vres_up[0])
                shift_into(PVs, 64, raw, ta, cols[0:64, 50:51], C_PV)
            shift_into(TWt, 96, raw, ta, cols[0:96, 12:13], C_WLO)
            k.act(TWt[:, :], TWt[:, :], AF.Tanh)
            shift_into(TAt, 96, raw, ta, cols[0:96, 13:14], C_ALO)
            for gi in range(4):
                rows = 128 if gi < 3 else 64
                shift_into(tb, rows, raw, ta, cols[0:rows, 14 + gi:15 + gi], C_GLO + gi * 128)
                k.act(TG[0:rows, gi, :], tb[0:rows, 0:S], AF.Sigmoid)
            done("rw1")
            def pair(name):
                return k.sb(es, name, [128, 2, 128])
            Nn = [pair("Nn0"), pair("Nn1")]
            Nt = [pair("Nt0"), pair("Nt1")]
            MTt, ARK, ARB = pair("MTt"), pair("ARK"), pair("ARB")
            PP = [pair("PP0"), pair("PP1")]
            RH, WU, U0P, VP, KHP, BHP = (pair(n) for n in ("RH", "WU", "U0P", "VP", "KHP", "BHP"))
            for t_ in (WU, U0P, VP, KHP, BHP):
                k.memset("dve", t_[:, :, :], 0.0)
            GyT = k.sb(es, "GyT", [128, 128])
            GhT = k.sb(es, "GhT", [128, 128])
            Hs = k.sb(es, "Hs", [128, 64])
            Hbd = k.sb(es, "Hbd", [128, 128])
            gC = k.sb(es, "gC", [128, NB])
            identv = ident
            id2, MU2, MU02, ML2 = (cst[:, 896 + i * 256: 896 + (i + 1) * 256].m(lambda a_: a_.rearrange("p (h w) -> p h w", h=2)) for i in range(4))
            Lz = k.sb(es, "Lz", [128, 6, 128])
            hmc = (blk64[:, 0:1], blk64[:, 64:65])

            def p3(ps, c0, w):
                return ps[:, c0:c0 + 2 * w].m(lambda a: a.rearrange("p (h w) -> p h w", h=2))

            for ct in range(4):
                cc = lambda j: cols[:, j + ct:j + ct + 1]
                r0 = ct * 128
                k.memset("dve", raw[:, 15:16], 0.0)
                shift_into(KS, 128, raw, ta, cc(4), C_K + r0)
                shift_into(RS, 128, raw, ta, cc(0), C_R + r0)
                shift_into(VS, 128, raw, ta, cc(8), C_V + r0)
                for tcn in range(NT):
                    sl = slice(tcn * 512, (tcn + 1) * 512)
                    ps = PS[tcn % 2]
                    k.mm(ps[:, :], wup[:, r0:r0 + 128], TWt[:, sl])
                    k.act(LW[:, sl], ps[:, :], AF.Sigmoid, bias=cc(18))
                    ps = PS[2 + tcn % 2]
                    k.mm(ps[:, :], aup[:, r0:r0 + 128], TAt[:, sl])
                    k.act(AA[:, sl], ps[:, :], AF.Sigmoid, bias=cc(22))
                k.ts("dve", LW[:, 0:S], LW[:, 0:S], -0.6065306597126334, ALU.mult)
                k.ts("dve", ta[:, 0:S], KS[:, 0:S], cc(26), ALU.mult)
                k.act(tb[:, 0:S], ta[:, 0:S], AF.Square)
                for tcn in range(NT):
                    sl = slice(tcn * 512, (tcn + 1) * 512)
                    ps = PS[4 + tcn % 2]
                    c_ = c5[tcn % 3]
                    k.mm(ps[:, :], blk64, tb[:, sl])
                    k.ts("dve", c_[:, :], ps[:, :], 1e-24, ALU.max)
                    k.act(c_[:, :], c_[:, :], AF.Sqrt)
                    k.recip(c_[:, :], c_[:, :])
                    k.tt("dve", KK[:, sl], ta[:, sl], c_[:, :], ALU.mult)
                k.ts("dve", ta[:, 0:S], AA[:, 0:S], cc(30), ALU.mult, cc(34), ALU.add)
                k.tt("dve", KP[:, 0:S], KS[:, 0:S], ta[:, 0:S], ALU.mult)
                k.tt("dve", BB[:, 0:S], KK[:, 0:S], AA[:, 0:S], ALU.mult)
                if l == 0:
                    k.dma("sp", vf_d[r0:r0 + 128, :], VS[:, 0:S])
                else:
                    k.dma("sp", ta[:, 0:S], vf_d[r0:r0 + 128, :])
                    for tcn in range(NT):
                        sl = slice(tcn * 512, (tcn + 1) * 512)
                        ps = PS[6 + tcn % 2]
                        c_, c2 = c5[tcn % 2], c5[2]
                        k.mm(ps[:, :], vup[:, r0:r0 + 128], PVs[:, sl])
                        k.act(c_[:, :], ps[:, :], AF.Sigmoid, bias=cc(51))
                        k.tt("dve", c2[:, :], ta[:, sl], VS[:, sl], ALU.subtract)
                        k.tt("dve", c2[:, :], c2[:, :], c_[:, :], ALU.mult)
                        k.tt("dve", VS[:, sl], VS[:, sl], c2[:, :], ALU.add)
                for c in range(NB):
                    k.scan(CU[:, c * 128:(c + 1) * 128], ones, LW[:, c * 128:(c + 1) * 128])
                k.tt("dve", ta[:, 0:S], CU[:, 0:S], LW[:, 0:S], ALU.subtract)
                k.act(ta[:, 0:S], ta[:, 0:S], AF.Exp)
                k.stt(AT[:, 0:S], KK[:, 0:S], -1.0, ta[:, 0:S], ALU.mult, ALU.mult)
                k.act(ta[:, 0:S], CU[:, 0:S], AF.Exp, scale=-1.0)
                k.tt("dve", KT[:, 0:S], KP[:, 0:S], ta[:, 0:S], ALU.mult)
                k.tt("dve", BT[:, 0:S], BB[:, 0:S], ta[:, 0:S], ALU.mult)
                k.act(ta[:, 0:S], CU[:, 0:S], AF.Exp)
                k.tt("dve", RT[:, 0:S], RS[:, 0:S], ta[:, 0:S], ALU.mult)
                cu3 = CU[:, 0:S].m(lambda a: a.rearrange("p (c t) -> p c t", t=128))
                cuC = cu3.m(lambda a: a[:, :, 127:128].to_broadcast([128, NB, 128]))
                ta3 = ta[:, 0:S].m(lambda a: a.rearrange("p (c t) -> p c t", t=128))
                k.tt("dve", ta3, cuC, cu3, ALU.subtract)
                k.act(ta[:, 0:S], ta[:, 0:S], AF.Exp)
                k.act(gC[:, :], cu3.m(lambda a: a[:, :, 127]), AF.Exp)
                k.tt("dve", KH[:, 0:S], KP[:, 0:S], ta[:, 0:S], ALU.mult)
                k.tt("dve", BH[:, 0:S], BB[:, 0:S], ta[:, 0:S], ALU.mult)
                k.tt("dve", ta[:, 0:S], RS[:, 0:S], KP[:, 0:S], ALU.mult)
                k.ts("dve", ta[:, 0:S], ta[:, 0:S], cc(38), ALU.mult)
                for tcn in range(NT):
                    sl = slice(tcn * 512, (tcn + 1) * 512)
                    ps = PS[tcn % 2]
                    k.mm(ps[:, :], blk64, ta[:, sl])
                    k.tt("dve", BON[:, sl], ps[:, :], VS[:, sl], ALU.mult)
                for tcn in range(NT):
                    sl = slice(tcn * 512, (tcn + 1) * 512)
                    ps = PS[2 + tcn % 2]
                    for gi in range(4):
                        k.mm(ps[:, :], gup[:, gi, r0:r0 + 128], TG[:, gi, sl], start=(gi == 0), stop=(gi == 3))
                    k.cpa(G[:, sl], ps[:, :])
                done("rw2")
                import os
                if os.environ.get("KVAR", "") != "nomem":
                    k.memset("dve", Hs[:, :], 0.0)
                    k.memset("dve", Hbd[:, :], 0.0)
                for c in range(NB):
                    sl = slice(c * 128, (c + 1) * 128)
                    hp = lambda t_, hh: t_[hh * 64:(hh + 1) * 64, sl]
                    psT = PS[0]
                    import os
                    KVAR = os.environ.get("KVAR", "")
                    if KVAR != "notr":
                        for i_, src in enumerate((AT, VS, KH, BH)):
                            k.tr(psT[:, i_ * 128:(i_ + 1) * 128], src[:, sl], ident)
                    if KVAR != "noev":
                        k.cpa(RH[:, :, 0:64], p3(psT, 0, 64))
                        k.cpa(diag_ap(VP), p3(psT, 128, 64))
                        k.cpa(diag_ap(KHP), p3(psT, 256, 64))
                        k.cpa(diag_ap(BHP), p3(psT, 384, 64))
                    done("rw3a")
                    for i_, src_ in enumerate((BT, AT, KT)):
                        for hh in range(2):
                            k.ts("dve", Lz[:, i_ * 2 + hh, :], src_[:, sl], hmc[hh], ALU.mult)
                    specs = [(PS[1], 0, 0, AT, Nn[0], MU2), (PS[1], 256, 1, BT, Nt[0], ML2),
                             (PS[2], 0, 2, AT, MTt, MU2), (PS[2], 256, 2, RT, ARK, MU02),
                             (PS[3], 0, 0, RT, ARB, MU02)]
                    for ps, c0, li_, rt, dst, msk in specs:
                        for hh in range(2):
                            k.mm(ps[:, c0 + hh * 128:c0 + (hh + 1) * 128], Lz[:, li_ * 2 + hh, :], rt[:, sl])
                        k.tt("dve", dst[:, :, :], p3(ps, c0, 128), msk, ALU.mult)
                    k.tt("dve", PP[0][:, :, :], Nn[0][:, :, :], id2, ALU.add)
                    done("rw3")
                    a = 0
                    b = 0
                    for lev in range(1, 7):
                        psq = PS[4 + 2 * (lev % 2)]
                        psc = PS[5 + 2 * (lev % 2)]
                        for hh in range(2):
                            k.mm(psq[:, 256 + hh * 128:256 + (hh + 1) * 128], Nn[a][:, hh, :], Nt[a][:, hh, :])
                        if lev < 6:
                            for hh in range(2):
                                k.mm(psq[:, hh * 128:(hh + 1) * 128], Nt[a][:, hh, :], Nn[a][:, hh, :])
                        k.cpa(Nt[1 - a][:, :, :], p3(psq, 256, 128))
                        if lev < 6:
                            k.cpa(Nn[1 - a][:, :, :], p3(psq, 0, 128))
                        for hh in range(2):
                            k.mm(psc[:, hh * 128:(hh + 1) * 128], identv, PP[b][:, hh, :], start=True, stop=False)
                            k.mm(psc[:, hh * 128:(hh + 1) * 128], Nt[1 - a][:, hh, :], PP[b][:, hh, :],
                                 start=False, stop=True)
                        k.cpa(PP[1 - b][:, :, :], p3(psc, 0, 128))
                        a, b = 1 - a, 1 - b
                    Pf = PP[b]
                    done("rw4")
                    psM = PS[1]
                    for hh in range(2):
                        k.mm(psM[:, hh * 64:(hh + 1) * 64], MTt[:, hh, :], VP[:, hh, hh * 64:(hh + 1) * 64])
                    k.cpa(RH[:, :, 64:128], p3(psM, 0, 64))
                    psW = PS[2]
                    for hh in range(2):
                        k.mm(psW[:, hh * 128:(hh + 1) * 128], Pf[:, hh, :], RH[:, hh, :])
                    pw3 = p3(psW, 0, 128)
                    k.cpa(diag_ap(WU), pw3.m(lambda a_: a_[:, :, 0:64]))
                    k.cpa(diag_ap(U0P), pw3.m(lambda a_: a_[:, :, 64:128]))
                    psG = PS[3]
                    for hh in range(2):
                        k.mm(psG[:, 0:128], WU[:, hh, :], ARB[:, hh, :], start=(hh == 0), stop=(hh == 1))
                    k.tt("dve", GyT[:, :], psG[:, 0:128], RT[:, sl], ALU.add)
                    psY = PS[1]
                    seq = [(VP, ARK, 0), (U0P, ARB, 0), (VP, ARK, 1), (U0P, ARB, 1)]
                    for i_, (lt, rt, hh) in enumerate(seq):
                        k.mm(psY[:, 256:384], lt[:, hh, :], rt[:, hh, :], start=(i_ == 0), stop=False)
                    k.mm(psY[:, 256:384], Hbd[:, :], GyT[:, :], start=False, stop=True)
                    k.cpa(YT[:, sl], psY[:, 256:384])
                    psH = PS[2]
                    for hh in range(2):
                        k.mm(psH[:, 256:384], WU[:, hh, :], BHP[:, hh, :], start=(hh == 0), stop=(hh == 1))
                    k.stt(GhT[:, :], ident, gC[:, c:c + 1], psH[:, 256:384], ALU.mult, ALU.add)
                    psS = PS[3]
                    seq = [(KHP, VP, 0), (KHP, VP, 1), (BHP, U0P, 0), (BHP, U0P, 1)]
                    for i_, (lt, rt, hh) in enumerate(seq):
                        k.mm(psS[:, 256:320], lt[:, hh, :], rt[:, hh, hh * 64:(hh + 1) * 64], start=(i_ == 0), stop=False)
                    k.mm(psS[:, 256:320], GhT[:, :], Hs[:, :], start=False, stop=True)
                    k.cp("dve", Hs[:, :], psS[:, 256:320])
                    k.cp("dve", Hbd[0:64, 0:64], Hs[0:64, :])
                    k.cp("act", Hbd[64:128, 64:128], Hs[64:128, :])
                    done("rw5")
                for tcn in range(NT):
                    sl = slice(tcn * 512, (tcn + 1) * 512)
                    ps1, ps2 = PS[4 + tcn % 2], PS[6 + tcn % 2]
                    d_, q_ = c5[0], c5[1]
                    k.mm(ps1[:, :], blk64, YT[:, sl])
                    k.stt(d_[:, :], ps1[:, :], -1.0 / 64, YT[:, sl], ALU.mult, ALU.add)
                    k.act(q_[:, :], d_[:, :], AF.Square)
                    k.mm(ps2[:, :], blk64, q_[:, :])
                    k.ts("dve", q_[:, :], ps2[:, :], 1.0 / 64, ALU.mult, 64e-5, ALU.add)
                    k.act(q_[:, :], q_[:, :], AF.Sqrt)
                    k.recip(q_[:, :], q_[:, :])
                    k.tt("dve", d_[:, :], d_[:, :], q_[:, :], ALU.mult)
                    k.ts("dve", d_[:, :], d_[:, :], cc(42), ALU.mult, cc(46), ALU.add)
                    k.tt("dve", d_[:, :], d_[:, :], BON[:, sl], ALU.add)
                    o_ = ostg[tcn % 2]
                    k.tt("dve", o_[:, :], d_[:, :], G[:, sl], ALU.mult)
                    k.dma("sp", ym_d[r0:r0 + 128, sl], o_[:, :])

        def attn_phase(es, l):
            qT = [k.sb(es, "qT%d" % i, [64, S], BF16) for i in range(4)]
            kT = [k.sb(es, "kT%d" % i, [64, S], BF16) for i in range(4)]
            vt = [k.sb(es, "vt%d" % i, [128, NB, 128], BF16) for i in range(2)]
            TB = [k.sb(es, "TB%d" % i, [128, TBW]) for i in range(2)]
            M0 = k.sb(es, "M0", [128, TBW])
            M1 = k.sb(es, "M1", [128, TBW])
            tS = [k.sb(es, "tS%d" % i, [128, 512]) for i in range(4)]
            eB = [k.sb(es, "eB%d" % i, [128, 512], BF16) for i in range(2)]
            Rr = k.sb(es, "Rr", [128, 512])
            O0 = k.sb(es, "O0", [128, 512])
            O1 = k.sb(es, "O1", [128, 512])
            ostg = [k.sb(es, "aostg%d" % i, [128, 512], BF16) for i in range(2)]
            sub = k.sb(es, "subc", [128, 2])
            k.dma("sp", M0[:, :], m0_d)
            k.dma("sp", M1[:, :], m1_d)
            colload(sub[:, 0:1], diff_subln[l], 1)
            lam_init = 0.8 - 0.6 * math.exp(-0.3 * l)
            k.ts("dve", sub[:, 1:2], sub[:, 0:1], 1.0 - lam_init, ALU.mult)
            vtok3 = [vtok_d[i].rearrange("(b p) c -> p b c", p=128) for i in range(3)]
            cnt = {"u": 0, "a": 0, "t": 0, "o": 0}

            def load_qk(slot, qrow, krow):
                k.dma("pool", qT[slot][:, :], pT_d[qrow:qrow + 64, :])
                k.dma("pool", kT[slot][:, :], pT_d[krow:krow + 64, :])

            Hk = k.sb(es, "Hk", [128, TBW])

            def load_tb(slot, hb):
                src = bass.AP(tensor=bias_h, offset=hb * 2560 + 1, ap=[[1, 128], [1, TBW]])
                k.dma("sp", Hk[:, :], src)
                for i_, c0 in enumerate(range(0, TBW, 512)):
                    w_ = min(512, TBW - c0)
                    ps = PS[i_ % 4]
                    k.mm(ps[:, 0:w_], JX, Hk[:, c0:c0 + w_])
                    k.cpa(TB[slot][:, c0:c0 + w_], ps[:, 0:w_])

            def softmax_pass(qv, kv, vv, dv, tb, qc, dst):
                u = cnt["u"]
                cnt["u"] += 1
                psN, psD = PS[4 + u % 2], PS[6 + u % 2]
                jl = 4 * qc + 4
                for j in range(jl):
                    off = 384 + 512 * qc - 128 * j
                    psA = PS[cnt["a"] % 4]
                    cnt["a"] += 1
                    t_ = tS[cnt["t"] % 2]
                    e_ = eB[cnt["t"] % 2]
                    cnt["t"] += 1
                    k.mm(psA[:, :], kv[:, j * 128:(j + 1) * 128], qv[:, qc * 512:(qc + 1) * 512])
                    k.stt(t_[:, :], psA[:, :], 0.125, tb[:, off:off + 512], ALU.mult, ALU.add)
                    k.act(e_[:, :], t_[:, :], AF.Exp)
                    k.mm(psN[0:dv, :], vv[:, j, 0:dv], e_[:, :], start=(j == 0), stop=(j == jl - 1))
                    k.mm(psD[:, :], onesb[:, :], e_[:, :], start=(j == 0), stop=(j == jl - 1))
                rd = tS[2]
                k.recip(rd[:, :], psD[:, :])
                k.tt("dve", dst[0:dv, :], psN[0:dv, :], rd[0:dv, :], ALU.mult)

            load_qk(0, C_QC, C_KC)
            load_tb(0, 0)
            k.dma("sp", vt[0][:, :, 0:64], vtok3[1][:, :, 0:64])
            for h in range(8):
                s_ = h % 2
                if h + 1 < 8:
                    load_qk(1 - s_, C_QC + (h + 1) * 64, C_KC + (h + 1) * 64)
                    load_tb(1 - s_, h + 1)
                    k.dma("sp", vt[1 - s_][:, :, 0:64], vtok3[1][:, :, (h + 1) * 64:(h + 2) * 64])
                for qc in range(NT):
                    softmax_pass(qT[s_], kT[s_], vt[s_], 64, TB[s_], qc, O0)
                    o_ = ostg[cnt["o"] % 2]
                    cnt["o"] += 1
                    k.cp("act", o_[0:64, :], O0[0:64, :])
                    k.dma("sp", ym_d[1024 + h * 64:1024 + (h + 1) * 64, qc * 512:(qc + 1) * 512], o_[0:64, :])
            for hd in range(4):
                s_ = hd % 2
                for c in range(2):
                    load_qk(2 * s_ + c, C_QD + hd * 128 + c * 64, C_KD + hd * 128 + c * 64)
                load_tb(s_, 8 + hd)
                k.dma("sp", vt[s_][:, :, :], vtok3[2][:, :, hd * 128:(hd + 1) * 128])
                for qc in range(NT):
                    softmax_pass(qT[2 * s_], kT[2 * s_], vt[s_], 128, TB[s_], qc, O0)
                    softmax_pass(qT[2 * s_ + 1], kT[2 * s_ + 1], vt[s_], 128, TB[s_], qc, O1)
                    k.stt(O0[:, :], O1[:, :], lamc[:, l:l + 1], O0[:, :], ALU.mult, ALU.add)
                    sq = tS[3]
                    k.act(sq[:, :], O0[:, :], AF.Square)
                    psX = PS[cnt["a"] % 4]
                    cnt["a"] += 1
                    k.mm(psX[:, :], ones, sq[:, :])
                    k.ts("dve", sq[:, :], psX[:, :], 1.0 / 128, ALU.mult, 1e-5, ALU.add)
                    k.act(sq[:, :], sq[:, :], AF.Sqrt)
                    k.recip(sq[:, :], sq[:, :])
                    k.tt("dve", O0[:, :], O0[:, :], sq[:, :], ALU.mult)
                    o_ = ostg[cnt["o"] % 2]
                    cnt["o"] += 1
                    k.ts("dve", o_[:, :], O0[:, :], sub[:, 1:2], ALU.mult)
                    k.dma("sp", ym_d[1536 + hd * 128:1536 + (hd + 1) * 128, qc * 512:(qc + 1) * 512], o_[:, :])
            load_qk(0, C_QB, C_KB)
            k.dma("sp", vt[0][:, :, 0:64], vtok3[0][:, :, 0:64])
            for h in range(8):
                s_ = h % 2
                if h + 1 < 8:
                    load_qk(1 - s_, C_QB + (h + 1) * 64, C_KB + (h + 1) * 64)
                    k.dma("sp", vt[1 - s_][:, :, 0:64], vtok3[0][:, :, (h + 1) * 64:(h + 2) * 64])
                qv, kv, vv = qT[s_], kT[s_], vt[s_]
                for qc in range(NT):
                    u = cnt["u"]
                    cnt["u"] += 1
                    psN = PS[6 + u % 2]
                    first = True
                    jl = 4 * qc + 4
                    for j in range(jl - 1, -1, -1):
                        off = 384 + 512 * qc - 128 * j
                        diag = j >= 4 * qc
                        a_ = cnt["a"]
                        cnt["a"] += 1
                        psA, psB, psC = PS[a_ % 2], PS[2 + a_ % 2], PS[4 + a_ % 2]
                        e1, sp, u_ = tS[0], tS[1], tS[2]
                        k.mm(psA[:, :], kv[:, j * 128:(j + 1) * 128], qv[:, qc * 512:(qc + 1) * 512])
                        k.act(e1[:, :], psA[:, :], AF.Exp, scale=0.125)
                        k.act(sp[:, :], e1[:, :], AF.Ln, bias=1.0)
                        if diag:
                            k.tt("dve", sp[:, :], sp[:, :], M1[:, off:off + 512], ALU.mult)
                        k.mm(psB[:, :], ML, sp[:, :])
                        if j > 0:
                            k.mm(psC[:, :], ones, sp[:, :])
                        k.stt(u_[:, :], psA[:, :], 0.125, sp[:, :], ALU.mult, ALU.subtract)
                        k.tt("dve", u_[:, :], u_[:, :], psB[:, :], ALU.subtract)
                        if not first:
                            k.tt("dve", u_[:, :], u_[:, :], Rr[:, :], ALU.subtract)
                        if diag:
                            k.tt("dve", u_[:, :], u_[:, :], M0[:, off:off + 512], ALU.add)
                        e_ = eB[a_ % 2]
                        k.act(e_[:, :], u_[:, :], AF.Exp)
                        k.mm(psN[0:64, :], vv[:, j, 0:64], e_[:, :], start=first, stop=(j == 0))
                        if j > 0:
                            if first:
                                k.cp("dve", Rr[:, :], psC[:, :])
                            else:
                                k.tt("dve", Rr[:, :], Rr[:, :], psC[:, :], ALU.add)
                        first = False
                    o_ = ostg[cnt["o"] % 2]
                    cnt["o"] += 1
                    k.cp("act", o_[0:64, :], psN[0:64, :])
                    k.dma("sp", ym_d[512 + h * 64:512 + (h + 1) * 64, qc * 512:(qc + 1) * 512], o_[0:64, :])

        for l in range(NL):
            mc = lambda a, b, l=l: modc[:, l * 96 + a: l * 96 + b]
            ncols_in = NIN + (64 if l == 1 else 0)
            with contextlib.ExitStack() as es:
                hT = k.sb(es, "hT", [128, 16, S], BF16)
                with contextlib.ExitStack() as es2:
                    norm_phase(es2, hT, l, 0, S, A1[:, (l * 2) * 16:(l * 2 + 1) * 16], mc(0, 16))
                P.barrier()
                wts = [k.sb(es, "win%d" % i, [128, 16, 512], BF16) for i in range(2)]
                stg = [k.sb(es, "pstg%d" % i, [128, 512]) for i in range(4)]
                stgb = [k.sb(es, "vstg%d" % i, [128, 512], BF16) for i in range(2)]
                segs = [(0, C_VB), (C_QC, C_VC), (C_QD, C_VD)]
                blocks = []
                for (a, b) in segs:
                    for c0 in range(a, b, 512):
                        blocks.append(("fm", c0, min(512, b - c0)))
                if l == 1:
                    blocks.append(("pv", C_PV, 64))
                for i, c0 in enumerate((C_VB, C_VC, C_VD)):
                    blocks.append(("tm", c0, 512, i))

                def load_block(bi):
                    blk = blocks[bi]
                    wt = wts[bi % 2]
                    if blk[0] == "pv":
                        wload(wt, vres_down[0], 16, 64)
                    else:
                        wload(wt, w_in[l][:, blk[1]:blk[1] + blk[2]], 16, blk[2])
                load_block(0)
                nps = 0
                for bi, blk in enumerate(blocks):
                    if bi + 1 < len(blocks):
                        load_block(bi + 1)
                    wt = wts[bi % 2]
                    if blk[0] in ("fm", "pv"):
                        c0, wd = blk[1], blk[2]
                        for m in range(0, wd, 128):
                            mw = min(128, wd - m)
                            for tcn in range(NT):
                                ps = PS[nps % 4]
                                nps += 1
                                for kc in range(16):
                                    k.mm(ps[0:mw, :], wt[:, kc, m:m + mw], hT[:, kc, tcn * 512:(tcn + 1) * 512],
                                         start=(kc == 0), stop=(kc == 15))
                                st = stg[nps % 4]
                                k.cpa(st[0:mw, :], ps[0:mw, :])
                                k.dma("sp", pT_d[c0 + m:c0 + m + mw, tcn * 512:(tcn + 1) * 512], st[0:mw, :])
                    else:
                        vi = blk[3]
                        for tb in range(NB):
                            ps = PS[nps % 4]
                            nps += 1
                            for kc in range(16):
                                k.mm(ps[:, :], hT[:, kc, tb * 128:(tb + 1) * 128], wt[:, kc, :],
                                     start=(kc == 0), stop=(kc == 15))
                            st = stgb[nps % 2]
                            k.cpa(st[:, :], ps[:, :])
                            k.dma("sp", vtok_d[vi, tb * 128:(tb + 1) * 128, :], st[:, :])
            P.barrier()
            done("gemm1")

            with contextlib.ExitStack() as es:
                rwkv_phase(es, l)
            P.barrier()
            done("rwkv")
            with contextlib.ExitStack() as es:
                attn_phase(es, l)
            P.barrier()
            done("attn")

            with contextlib.ExitStack() as es:
                ymT = k.sb(es, "ymT", [128, 16, S], BF16)
                for kc in range(16):
                    k.dma("sp", ymT[:, kc, :], ym3[:, kc, :])
                wts = [k.sb(es, "wout%d" % i, [128, 16, 512], BF16) for i in range(2)]
                xst = [k.sb(es, "xst%d" % i, [128, 512]) for i in range(4)]
                wload(wts[0], w_out[l][:, 0:512], 16, 512)
                nps = 0
                for bi in range(4):
                    if bi + 1 < 4:
                        wload(wts[(bi + 1) % 2], w_out[l][:, (bi + 1) * 512:(bi + 2) * 512], 16, 512)
                    wt = wts[bi % 2]
                    for m in range(4):
                        dc = bi * 4 + m
                        for tcn in range(NT):
                            ps = PS[nps % 4]
                            nps += 1
                            for kc in range(16):
                                k.mm(ps[:, :], wt[:, kc, m * 128:(m + 1) * 128], ymT[:, kc, tcn * 512:(tcn + 1) * 512],
                                     start=(kc == 0), stop=(kc == 15))
                            resid_epilogue(xst, ps, dc * 128, slice(tcn * 512, (tcn + 1) * 512), mc(32 + dc, 33 + dc))
            P.barrier()
            done("wout")

            HT = min(S, 1024)
            for half in range(S // HT):
                t0 = half * HT
                with contextlib.ExitStack() as es:
                    aT = [k.sb(es, "aT%d" % f, [128, HT], BF16) for f in range(FC)]
                    with contextlib.ExitStack() as esA:
                        h2T = k.sb(esA, "h2T", [128, 16, HT], BF16)
                        with contextlib.ExitStack() as es2:
                            norm_phase(es2, h2T, l, t0, HT, A1[:, (l * 2 + 1) * 16:(l * 2 + 2) * 16], mc(48, 64))
                        P.barrier()
                        w13 = [k.sb(esA, "w13_%d" % i, [128, 16, 512], BF16) for i in range(2)]
                        nblk = 2 * DFF // 512

                        def ld13(b_):
                            wload(w13[b_ % 2], ffn_w13[l][:, b_ * 512:(b_ + 1) * 512], 16, 512)
                        ld13(0)
                        nps = 0
                        for b_ in range(nblk):
                            if b_ + 1 < nblk:
                                ld13(b_ + 1)
                            for m in range(4):
                                col = b_ * 512 + m * 128
                                is_up = col >= DFF
                                f = (col - DFF) // 128 if is_up else col // 128
                                for tcn in range(HT // 512):
                                    ps = PS[nps % 8]
                                    nps += 1
                                    ts_ = slice(tcn * 512, (tcn + 1) * 512)
                                    for kc in range(16):
                                        k.mm(ps[:, :], w13[b_ % 2][:, kc, m * 128:(m + 1) * 128], h2T[:, kc, ts_],
                                             start=(kc == 0), stop=(kc == 15))
                                    if not is_up:
                                        k.act(aT[f][:, ts_], ps[:, :], AF.Silu)
                                    else:
                                        k.tt("dve", aT[f][:, ts_], aT[f][:, ts_], ps[:, :], ALU.mult)
                    P.barrier()
                    with contextlib.ExitStack() as esB:
                        w2 = [k.sb(esB, "w2_%d" % i, [128, FC, 256], BF16) for i in range(2)]
                        xst = [k.sb(esB, "fxst%d" % i, [128, 512]) for i in range(4)]

                        def ld2(b_):
                            src_ = ffn_w2[l][:, b_ * 256:(b_ + 1) * 256].rearrange("(c p) n -> p c n", p=128)
                            wt = w2[b_ % 2]
                            for a_ in range(0, FC, 11):
                                k.dma("pool", wt[:, a_:a_ + 11, :], src_[:, a_:a_ + 11, :])
                        ld2(0)
                        nps = 0
                        for b_ in range(8):
                            if b_ + 1 < 8:
                                ld2(b_ + 1)
                            for m in range(2):
                                dc = b_ * 2 + m
                                for tcn in range(HT // 512):
                                    ps = PS[nps % 8]
                                    nps += 1
                                    ts_ = slice(tcn * 512, (tcn + 1) * 512)
                                    for f in range(FC):
                                        k.mm(ps[:, :], w2[b_ % 2][:, f, m * 128:(m + 1) * 128], aT[f][:, ts_],
                                             start=(f == 0), stop=(f == FC - 1))
                                    resid_epilogue(xst, ps, dc * 128, slice(t0 + tcn * 512, t0 + (tcn + 1) * 512),
                                                   mc(80 + dc, 81 + dc))
                P.barrier()

        with contextlib.ExitStack() as es:
            hn = k.sb(es, "hn", [128, 16, 512])
            ost = [k.sb(es, "ost%d" % i, [128, D]) for i in range(2)]

            for tcn in range(NT):
                with contextlib.ExitStack() as es2:
                    norm_phase(es2, hn, 0, tcn * 512, 512, gcol[:, 64:80], None)
                P.barrier()
                for q in range(4):
                    tb = tcn * 4 + q
                    o_ = ost[tb % 2]
                    for g in range(4):
                        ps = PS[(tb * 4 + g) % 8]
                        for j in range(4):
                            kc = g * 4 + j
                            k.tr(ps[:, j * 128:(j + 1) * 128], hn[:, kc, q * 128:(q + 1) * 128], ident)
                        k.cpa(o_[:, g * 512:(g + 1) * 512], ps[:, :])
                    k.dma("sp", out_d[tb * 128:(tb + 1) * 128, :], o_[:, :])
                P.barrier()
    except StopBuild:
        pass
    P.barrier()
    P.emit(es0)
    ncd.close()
    try:
        es0.close()
    except AssertionError:
        pass
    return nc


_CACHE = {}


def make_in_maps(inputs, S, nb):
    consts = host_consts(S)
    shared = {}
    for name, arr in inputs.items():
        if name in ("x", "c"):
            continue
        a = np.asarray(arr, dtype=np.float32)
        if name == "diff_lambda":
            a = a.reshape(2, 256)
        if name == "rwkv_r_k":
            a = a.reshape(2, 512)
        shared[name] = np.ascontiguousarray(a)
    shared.update(consts)
    x = np.asarray(inputs["x"], dtype=np.float32)
    c = np.asarray(inputs["c"], dtype=np.float32)
    maps = []
    for b in range(nb):
        m = dict(shared)
        m["x"] = np.ascontiguousarray(x[b, :S])
        m["c"] = np.ascontiguousarray(c[b])
        maps.append(m)
    return maps


def kernel(**inputs):
    S = 2048
    nc = build(S, 2)
    maps = make_in_maps(inputs, S, 8)
    res = run_bass_kernel_spmd(nc, maps, core_ids=list(range(8)))
    out = np.stack([np.asarray(r["out"], dtype=np.float32) for r in res.results], axis=0)
    return out
```

```python
import math
import contextlib
import numpy as np
import concourse.bass as bass
import concourse.mybir as mybir
from concourse.bass_utils import run_bass_kernel_spmd

F32 = mybir.dt.float32
BF16 = mybir.dt.bfloat16
AF = mybir.ActivationFunctionType
ALU = mybir.AluOpType

D = 2048
KC = 16
DFF = 5632
FC = 44
NIN = 6784
NEG = -30000.0
C_R, C_K, C_V, C_WLO, C_ALO, C_GLO = 0, 512, 1024, 1536, 1632, 1728
C_QB, C_KB, C_VB = 2176, 2688, 3200
C_QC, C_KC, C_VC = 3712, 4224, 4736
C_QD, C_KD, C_VD = 5248, 5760, 6272
C_PV = 6784
TBW = 2432


class View:
    __slots__ = ("t", "ap")

    def __init__(self, t, ap):
        self.t = t
        self.ap = ap

    def m(self, f):
        return View(self.t, f(self.ap))

    def __getitem__(self, k):
        return View(self.t, self.ap[k])


class T:
    __slots__ = ("h", "lw", "rd", "name", "psum")

    def __init__(self, h, name="", psum=False):
        self.h = h
        self.lw = None
        self.rd = []
        self.name = name
        self.psum = psum

    def __getitem__(self, k):
        return View(self, self.h[k])

    def v(self, ap):
        return View(self, ap)


def _ap(x):
    return x.ap if isinstance(x, View) else x


def _ts(*xs):
    return [x.t for x in xs if isinstance(x, View)]


class Prog:
    NQ = 8

    def __init__(self, nc):
        self.nc = nc
        self.ops = []
        self.pos = {}
        self.last = {}
        self.bar = None
        self.bar_seen = set()
        self.unconsumed = set()

    def op(self, eng, fn, R=(), W=(), dma=False, extra=()):
        idx = len(self.ops)
        deps = set(extra)
        for t in R:
            if t.lw is not None:
                deps.add(t.lw)
            if t.psum:
                deps.update(r for r in t.rd if self.ops[r]["eng"] != eng)
        for t in W:
            if t.lw is not None:
                deps.add(t.lw)
            deps.update(t.rd)
        if self.bar is not None and eng not in self.bar_seen:
            deps.add(self.bar)
            self.bar_seen.add(eng)
        pos = self.pos.get(eng, 0)
        keep = set()
        for d in deps:
            o = self.ops[d]
            if o["eng"] == eng and not o["dma"] and not dma:
                if eng == "pe":
                    continue
                if pos - o["pos"] > 3:
                    continue
            keep.add(d)
            if o["dma"]:
                self.unconsumed.discard(d)
        best = {}
        keep2 = set()
        for d in keep:
            o = self.ops[d]
            if o["dma"]:
                keep2.add(d)
            elif best.get(o["eng"], -1) < d:
                best[o["eng"]] = d
        keep2.update(best.values())
        keep = keep2
        self.ops.append(dict(eng=eng, fn=fn, deps=keep, dma=dma, pos=pos))
        self.pos[eng] = pos + 1
        self.last[eng] = idx
        if dma:
            self.unconsumed.add(idx)
        for t in R:
            t.rd.append(idx)
        for t in W:
            t.lw = idx
            t.rd = []
        return idx

    def barrier(self):
        deps = set(self.last.values()) | set(self.unconsumed)
        self.unconsumed = set()
        nc = self.nc
        self.bar = None
        idx = self.op("sp", lambda: nc.sync.nop(nofuse=True), extra=deps)
        self.bar = idx
        self.bar_seen = {"sp"}
        return idx

    def emit(self, es):
        nc = self.nc
        engs = {"pe": nc.tensor, "act": nc.scalar, "dve": nc.vector, "pool": nc.gpsimd, "sp": nc.sync}
        ops = self.ops
        flagged = [False] * len(ops)
        for o in ops:
            for d in o["deps"]:
                flagged[d] = True
        sems = {}
        for e in engs:
            sems[e] = es.enter_context(nc.semaphore("s_" + e))
        dsems = {}
        for e in ("sp", "pool"):
            dsems[e] = [es.enter_context(nc.semaphore("d_%s%d" % (e, i))) for i in range(self.NQ)]
        cnt = {e: 0 for e in engs}
        dcnt = {"sp": 0, "pool": 0}
        known = {e: {} for e in engs}
        ev = [None] * len(ops)
        for idx, o in enumerate(ops):
            e = o["eng"]
            E = engs[e]
            need = {}
            for d in o["deps"]:
                sm, val = ev[d]
                if need.get(sm, (None, 0))[1] < val:
                    need[sm] = (sm, val)
            if o["dma"]:
                k = dcnt[e]
                sm = dsems[e][k % self.NQ]
                prev = 16 * (k // self.NQ)
                if prev > 0 and need.get(sm, (None, 0))[1] < prev:
                    need[sm] = (sm, prev)
            for sm, val in need.values():
                key = id(sm)
                if known[e].get(key, 0) >= val:
                    continue
                E.wait_ge(sm, val)
                known[e][key] = val
            ins = o["fn"]()
            if o["dma"]:
                k = dcnt[e]
                sm = dsems[e][k % self.NQ]
                ins.then_inc(sm, 16)
                ev[idx] = (sm, 16 * (k // self.NQ + 1))
                dcnt[e] = k + 1
            elif flagged[idx]:
                cnt[e] += 1
                ins.then_inc(sems[e], 1)
                ev[idx] = (sems[e], cnt[e])


class K:
    def __init__(self, nc, P, es):
        self.nc = nc
        self.P = P
        self.es = es
        self.eng = {"act": nc.scalar, "dve": nc.vector, "pool": nc.gpsimd}
        self.rr = 0

    def sb(self, es, name, shape, dt=F32):
        self.uid = getattr(self, "uid", 0) + 1
        name = "t%d_%s" % (self.uid, name)
        return T(es.enter_context(self.nc.sbuf_tensor(name, list(shape), dt)), name)

    def dma(self, q, out, in_):
        nc = self.nc
        E = nc.sync if q == "sp" else nc.gpsimd
        o, i = _ap(out), _ap(in_)
        return self.P.op(q, lambda: E.dma_start(out=o, in_=i), R=_ts(in_), W=_ts(out), dma=True)

    def mm(self, out, lhsT, rhs, start=True, stop=True):
        nc = self.nc
        o, l, r = out.ap, lhsT.ap, rhs.ap
        return self.P.op("pe", lambda: nc.tensor.matmul(o, l, r, start=start, stop=stop),
                         R=_ts(lhsT, rhs), W=_ts(out))

    def tr(self, out, in_, ident):
        nc = self.nc
        o, i, d = out.ap, in_.ap, ident.ap
        return self.P.op("pe", lambda: nc.tensor.transpose(o, i, d), R=_ts(in_, ident), W=_ts(out))

    def tt(self, eng, out, in0, in1, op):
        E = self.eng[eng]
        o, a, b = out.ap, _ap(in0), _ap(in1)
        return self.P.op(eng, lambda: E.tensor_tensor(out=o, in0=a, in1=b, op=op), R=_ts(in0, in1), W=_ts(out))

    def ts(self, eng, out, in0, s1, op0, s2=None, op1=None):
        E = self.eng[eng]
        o, a = out.ap, _ap(in0)
        x1, x2 = _ap(s1), _ap(s2)
        if op1 is None:
            f = lambda: E.tensor_scalar(out=o, in0=a, scalar1=x1, scalar2=None, op0=op0)
        else:
            f = lambda: E.tensor_scalar(out=o, in0=a, scalar1=x1, scalar2=x2, op0=op0, op1=op1)
        return self.P.op(eng, f, R=_ts(in0, s1, s2), W=_ts(out))

    def stt(self, out, in0, sc, in1, op0, op1):
        E = self.nc.vector
        o, a, s, b = out.ap, _ap(in0), _ap(sc), _ap(in1)
        return self.P.op("dve", lambda: E.scalar_tensor_tensor(out=o, in0=a, scalar=s, in1=b, op0=op0, op1=op1),
                         R=_ts(in0, sc, in1), W=_ts(out))

    def cp(self, eng, out, in_):
        o, a = out.ap, _ap(in_)
        if eng == "act":
            E = self.nc.scalar
            return self.P.op(eng, lambda: E.copy(out=o, in_=a), R=_ts(in_), W=_ts(out))
        E = self.eng[eng]
        return self.P.op(eng, lambda: E.tensor_copy(out=o, in_=a), R=_ts(in_), W=_ts(out))

    def cpa(self, out, in_):
        self.rr ^= 1
        return self.cp("act" if self.rr else "dve", out, in_)

    def act(self, out, in_, func, bias=None, scale=1.0):
        E = self.nc.scalar
        o, a, b = out.ap, _ap(in_), _ap(bias)
        sc = _ap(scale)
        kw = {}
        if bias is not None:
            kw["bias"] = b
        f = lambda: E.activation(out=o, in_=a, func=func, scale=sc, **kw)
        return self.P.op("act", f, R=_ts(in_, bias, scale), W=_ts(out))

    def recip(self, out, in_):
        E = self.nc.vector
        o, a = out.ap, _ap(in_)
        return self.P.op("dve", lambda: E.reciprocal(out=o, in_=a), R=_ts(in_), W=_ts(out))

    def memset(self, eng, out, val):
        E = self.eng[eng]
        o = out.ap
        return self.P.op(eng, lambda: E.memset(o, val), W=_ts(out))

    def scan(self, out, d0, d1):
        E = self.nc.vector
        o, a, b = out.ap, _ap(d0), _ap(d1)
        return self.P.op("dve", lambda: E.tensor_tensor_scan(out=o, data0=a, data1=b, initial=0.0,
                                                             op0=ALU.mult, op1=ALU.add),
                         R=_ts(d0, d1), W=_ts(out))


def host_consts(S):
    c = {}
    i = np.arange(128)
    ident = np.eye(128, dtype=np.float32)
    ones = np.ones((128, 128), np.float32)
    blk = np.zeros((128, 128), np.float32)
    blk[:64, :64] = 1
    blk[64:, 64:] = 1
    MU = (i[None, :] > i[:, None]).astype(np.float32)
    MU0 = (i[None, :] >= i[:, None]).astype(np.float32)
    ML = (i[:, None] > i[None, :]).astype(np.float32)
    c["cst"] = np.concatenate([ident, ones, blk, MU, MU0, ML, ident[::-1].copy(), ident, ident, MU, MU, MU0, MU0, ML, ML], axis=1)
    r = np.arange(-512, 2048)
    rp = np.maximum(r, 0)
    d_f = np.maximum(rp, 1).astype(np.float32)
    large = 16 + (np.log(d_f / np.float32(16)) / np.float32(math.log(2048 / 16)) * np.float32(16)).astype(np.int32)
    large = np.minimum(large, 31)
    bucket = np.where(rp < 16, rp, large)
    E = np.zeros((34, 2560), np.float32)
    E[bucket, np.arange(2560)] = 1.0
    E[:32, r < 0] = 0.0
    cnt = ((rp <= 128).astype(np.int32) + ((rp % 4 == 0) & (rp <= 512)).astype(np.int32)
           + ((rp % 16 == 0) & (rp <= 2048)).astype(np.int32))
    exC = np.where(cnt > 0, np.log(np.maximum(cnt, 1).astype(np.float64)), NEG).astype(np.float32)
    exC[r < 0] = NEG
    exD = np.where(r < 0, NEG, 0.0).astype(np.float32)
    E[32] = exC
    E[33] = exD
    c["e34"] = E
    ind = np.zeros((2, 12), np.float32)
    ind[0, :8] = 1
    ind[1, 8:] = 1
    c["ind2"] = ind
    xx = np.arange(TBW)[None, :] - 384 - i[:, None]
    c["m0"] = np.where(xx > 0, 0.0, NEG).astype(np.float32)
    c["m1"] = (xx > 0).astype(np.float32)
    rm = np.ones((128, S), np.float32)
    rm[:, ::128] = 0.0
    c["rmask"] = rm
    return c


class StopBuild(Exception):
    pass


def build(S=2048, NL=2, dbg=False, stop_after=None):
    nc = bass.Bass("TRN2", target_bir_lowering=False)
    NT = S // 512
    NB = S // 128
    es0 = contextlib.ExitStack()
    P = Prog(nc)
    k = K(nc, P, es0)

    def din(name, shape):
        return nc.dram_tensor(name, list(shape), F32, kind="ExternalInput").ap()

    x_d = din("x", [S, D])
    c_d = din("c", [D])
    w_ada = din("w_ada", [2, D, 6 * D])
    b_ada = din("b_ada", [2, 6 * D])
    norm_gain = din("norm_gain", [2, 2, D])
    w_in = din("w_in", [2, D, NIN])
    w_out = din("w_out", [2, D, D])
    rel_bias = din("rel_bias", [32, 12])
    rwkv_mu = din("rwkv_mu", [2, 2176])
    rwkv_w0 = din("rwkv_w0", [2, 512])
    rwkv_w_up = din("rwkv_w_up", [2, 96, 512])
    rwkv_a0 = din("rwkv_a0", [2, 512])
    rwkv_a_up = din("rwkv_a_up", [2, 96, 512])
    rwkv_g_up = din("rwkv_g_up", [2, 448, 512])
    rwkv_k_k = din("rwkv_k_k", [2, 512])
    rwkv_k_a = din("rwkv_k_a", [2, 512])
    rwkv_r_k = din("rwkv_r_k", [2, 512])
    rwkv_ln_w = din("rwkv_ln_w", [2, 512])
    rwkv_ln_b = din("rwkv_ln_b", [2, 512])
    vres_down = din("vres_down", [1, D, 64])
    vres_mu = din("vres_mu", [1, 64])
    vres_up = din("vres_up", [1, 64, 512])
    vres_bias = din("vres_bias", [1, 512])
    diff_lambda = din("diff_lambda", [2, 256])
    diff_subln = din("diff_subln", [2, 128])
    ffn_w13 = din("ffn_w13", [2, D, 2 * DFF])
    ffn_w2 = din("ffn_w2", [2, DFF, D])
    final_gain = din("final_gain", [D])
    cst_d = din("cst", [128, 1920])
    e34_d = din("e34", [34, 2560])
    ind2_d = din("ind2", [2, 12])
    m0_d = din("m0", [128, TBW])
    m1_d = din("m1", [128, TBW])
    rmask_d = din("rmask", [128, S])
    out_d = nc.dram_tensor("out", [S, D], F32, kind="ExternalOutput").ap()

    okind = "ExternalOutput" if dbg else "Internal"
    xT_h = nc.dram_tensor("xT", [D, S], F32, kind=okind)
    pT_h = nc.dram_tensor("pT", [NIN + 64, S], F32, kind=okind)
    ym_h = nc.dram_tensor("ymT", [D, S], BF16, kind=okind)
    vtok_h = nc.dram_tensor("vtok", [3, S, 512], BF16, kind="Internal")
    bias_h = nc.dram_tensor("biasd", [12, 2560], F32, kind=okind)
    vf_h = nc.dram_tensor("vfT", [512, S], F32, kind="Internal")
    mod_h = nc.dram_tensor("modd", [128, 192], F32, kind=okind)
    xT_d, pT_d, ym_d, vtok_d, bias_d, vf_d = (h.ap() for h in (xT_h, pT_h, ym_h, vtok_h, bias_h, vf_h))
    xT3 = xT_d.rearrange("(c p) s -> p c s", p=128)
    ym3 = ym_d.rearrange("(c p) s -> p c s", p=128)

    cst = k.sb(es0, "cst", [128, 1920])
    ident, ones, blk64, MU, MU0, ML, JX = (cst[:, i * 128:(i + 1) * 128] for i in range(7))
    onesb = k.sb(es0, "onesb", [128, 128], BF16)
    modc = k.sb(es0, "modc", [128, 2 * 96])
    gcol = k.sb(es0, "gcol", [128, 4 * 16 + 16])
    A1 = k.sb(es0, "A1", [128, 2 * 2 * 16])
    lamc = k.sb(es0, "lamc", [128, 4])
    PS = [T(es0.enter_context(nc.psum_tensor("ps%d" % i, [128, 512], F32)), "ps%d" % i, psum=True) for i in range(8)]
    ncd = contextlib.ExitStack()
    ncd.enter_context(nc.allow_non_contiguous_dma(reason="small per-channel parameter vectors"))

    def colload(dst, vec, n):
        k.dma("sp", dst, vec.rearrange("(c p) -> p c", p=128))

    k.dma("sp", cst[:, :], cst_d)
    k.cp("dve", onesb[:, :], ones)

    def done(tag):
        if stop_after == tag:
            raise StopBuild()
    try:

        with contextlib.ExitStack() as es:
            condT = k.sb(es, "condT", [128, 16])
            colload(condT[:, :], c_d, 16)
            k.act(condT[:, :], condT[:, :], AF.Silu)
            wb = [k.sb(es, "wada%d" % i, [128, 16, 512]) for i in range(4)]
            bcol = k.sb(es, "bcol", [128, 192])
            for l in range(NL):
                colload(bcol[:, l * 96:(l + 1) * 96], b_ada[l], 96)
                for i in range(2):
                    colload(gcol[:, (l * 2 + i) * 16:(l * 2 + i + 1) * 16], norm_gain[l, i], 16)
            colload(gcol[:, 64:80], final_gain, 16)
            for l in range(NL):
                for nb in range(24):
                    w = wb[nb % 4]
                    src = w_ada[l][:, nb * 512:(nb + 1) * 512].rearrange("(c p) n -> p c n", p=128)
                    k.dma("sp", w[:, 0:8, :], src[:, 0:8, :])
                    k.dma("pool", w[:, 8:16, :], src[:, 8:16, :])
                    for m in range(4):
                        j = nb * 4 + m
                        for kc in range(16):
                            k.mm(PS[0][:, j:j + 1], w[:, kc, m * 128:(m + 1) * 128], condT[:, kc:kc + 1],
                                 start=(kc == 0), stop=(kc == 15))
                k.tt("dve", modc[:, l * 96:(l + 1) * 96], PS[0][:, 0:96], bcol[:, l * 96:(l + 1) * 96], ALU.add)
                for i in range(2):
                    sc = modc[:, l * 96 + i * 48 + 16: l * 96 + i * 48 + 32]
                    k.stt(A1[:, (l * 2 + i) * 16:(l * 2 + i + 1) * 16], sc, 1.0,
                          gcol[:, (l * 2 + i) * 16:(l * 2 + i + 1) * 16], ALU.add, ALU.mult)
            if dbg:
                k.dma("sp", mod_h.ap()[:, 0:NL * 96], modc[:, 0:NL * 96])
            lp = k.sb(es, "lp", [1, 512])
            lsum = k.sb(es, "lsum", [1, 8])
            for l in range(NL):
                k.dma("sp", lp[0:1, 0:256], diff_lambda[l:l + 1, :])
                k.tt("dve", lp[0:1, 256:320], lp[0:1, 0:64], lp[0:1, 64:128], ALU.mult)
                k.tt("dve", lp[0:1, 320:384], lp[0:1, 128:192], lp[0:1, 192:256], ALU.mult)
                nc_ = nc
                o1, i1 = lsum[0:1, 0:2].ap, lp[0:1, 256:384].ap.rearrange("p (a b) -> p a b", a=2)
                P.op("dve", lambda o1=o1, i1=i1: nc_.vector.tensor_reduce(out=o1, in_=i1, axis=mybir.AxisListType.X,
                                                                         op=ALU.add), R=[lp], W=[lsum])
                k.act(lsum[0:1, 2:4], lsum[0:1, 0:2], AF.Exp)
                lam_init = 0.8 - 0.6 * math.exp(-0.3 * l)
                k.tt("dve", lsum[0:1, 4:5], lsum[0:1, 3:4], lsum[0:1, 2:3], ALU.subtract)
                k.ts("dve", lsum[0:1, 5:6], lsum[0:1, 4:5], -lam_init, ALU.add)
                k.mm(PS[1][:, l:l + 1], ones[0:1, :], lsum[0:1, 5:6])
                k.cp("dve", lamc[:, l:l + 1], PS[1][:, l:l + 1])
        P.barrier()
        done("mod")

        with contextlib.ExitStack() as es:
            l34 = k.sb(es, "l34", [34, 12])
            e34 = k.sb(es, "e34", [34, 2560])
            bf = k.sb(es, "bf", [12, 2560])
            k.dma("sp", l34[0:32, :], rel_bias)
            k.dma("sp", l34[32:34, :], ind2_d)
            k.dma("sp", e34[:, :], e34_d)
            for i in range(5):
                k.mm(PS[i][0:12, :], l34[:, :], e34[:, i * 512:(i + 1) * 512])
                k.cp("dve", bf[:, i * 512:(i + 1) * 512], PS[i][0:12, :])
            k.dma("sp", bias_d, bf[:, :])
        P.barrier()
        done("bias")

        with contextlib.ExitStack() as es:
            xin = [k.sb(es, "xin%d" % i, [128, D]) for i in range(2)]
            stg = [k.sb(es, "xstg%d" % i, [128, 16, 128]) for i in range(2)]
            for tb in range(NB):
                xi, st = xin[tb % 2], stg[tb % 2]
                k.dma("sp", xi[:, :], x_d[tb * 128:(tb + 1) * 128, :])
                for g in range(4):
                    ps = PS[(tb * 4 + g) % 8]
                    for q in range(4):
                        kc = g * 4 + q
                        k.tr(ps[:, q * 128:(q + 1) * 128], xi[:, kc * 128:(kc + 1) * 128], ident)
                    k.cpa(st[:, g * 4:(g + 1) * 4, :], ps[:, :].m(lambda a: a.rearrange("p (q t) -> p q t", q=4)))
                k.dma("sp", xT3[:, :, tb * 128:(tb + 1) * 128], st[:, :, :])
        P.barrier()
        done("x0")

        def norm_phase(es, hT, li, t0, ntok, acol, bcolv):
            xt = k.sb(es, "nx", [128, 16, 512])
            sq = [k.sb(es, "nsq%d" % i, [128, 512]) for i in range(2)]
            rs = k.sb(es, "nrs", [128, 512])
            tmp = [k.sb(es, "ntmp%d" % i, [128, 512]) for i in range(2)]
            for tcn in range(ntok // 512):
                k.dma("sp", xt[:, :, :], xT3[:, :, t0 + tcn * 512: t0 + (tcn + 1) * 512])
                ps = PS[tcn % 2]
                for kc in range(16):
                    s_ = sq[kc % 2]
                    k.act(s_[:, :], xt[:, kc, :], AF.Square)
                    k.mm(ps[:, :], ones, s_[:, :], start=(kc == 0), stop=(kc == 15))
                k.ts("dve", rs[:, :], ps[:, :], 1.0 / D, ALU.mult, 1e-6, ALU.add)
                k.act(rs[:, :], rs[:, :], AF.Sqrt)
                k.recip(rs[:, :], rs[:, :])
                for kc in range(16):
                    t_ = tmp[kc % 2]
                    k.tt("dve", t_[:, :], xt[:, kc, :], rs[:, :], ALU.mult)
                    dst = hT[:, kc, tcn * 512:(tcn + 1) * 512]
                    if bcolv is None:
                        k.ts("dve", dst, t_[:, :], acol.m(lambda a, kc=kc: a[:, kc:kc + 1]), ALU.mult)
                    else:
                        k.act(dst, t_[:, :], AF.Identity, bias=bcolv.m(lambda a, kc=kc: a[:, kc:kc + 1]),
                              scale=acol.m(lambda a, kc=kc: a[:, kc:kc + 1]))

        def wload(wt, src, kcs, width):
            s3 = src.rearrange("(c p) n -> p c n", p=128)
            step = 8
            for a in range(0, kcs, step):
                b = min(kcs, a + step)
                k.dma("pool", wt[:, a:b, 0:width], s3[:, a:b, :])

        def resid_epilogue(es_stage, ps, c0, tcs, gc):
            xs = es_stage[resid_epilogue.n % len(es_stage)]
            resid_epilogue.n += 1
            k.dma("sp", xs[:, :], xT_d[c0:c0 + 128, tcs])
            k.stt(xs[:, :], ps[:, :], gc, xs[:, :], ALU.mult, ALU.add)
            k.dma("sp", xT_d[c0:c0 + 128, tcs], xs[:, :])
        resid_epilogue.n = 0

        def diag_ap(t):
            base = t.h[:, :, :]
            return View(t, bass.AP(tensor=base.tensor, offset=base.offset, ap=[list(base.ap[0]), [192, 2], [1, 64]]))

        def bc2(v):
            return v.m(lambda a: a.unsqueeze(1).to_broadcast([128, 2, 128]))

        def shift_into(dst, rows, praw, tmp, mucol, r0):
            k.dma("sp", praw[0:rows, 16:S + 16], pT_d[r0:r0 + rows, :])
            k.tt("dve", tmp[0:rows, 0:S], praw[0:rows, 15:S + 15], praw[0:rows, 16:S + 16], ALU.subtract)
            k.stt(dst[0:rows, 0:S], tmp[0:rows, 0:S], mucol, praw[0:rows, 16:S + 16], ALU.mult, ALU.add)

        def rwkv_phase(es, l):
            SW = S + 16
            names = ["raw", "ta", "tb", "KS", "RS", "VS", "LW", "AA", "KK", "KP", "BB", "CU", "BT"]
            Wt = {n: k.sb(es, "rw_" + n, [128, SW]) for n in names}
            raw, ta, tb = Wt["raw"], Wt["ta"], Wt["tb"]
            KS, RS, VS, LW, AA, KK, KP, BB, CU, BT = (Wt[n] for n in names[3:])
            AT, KT, RT, KH, BH, BON, G, YT = KS, LW, AA, KK, tb, CU, raw, BB
            TWt = k.sb(es, "TWt", [96, S])
            TAt = k.sb(es, "TAt", [96, S])
            TG = k.sb(es, "TG", [128, 4, S], BF16)
            wup = k.sb(es, "wup", [96, 512])
            aup = k.sb(es, "aup", [96, 512])
            gup = k.sb(es, "gup", [128, 4, 512], BF16)
            cols = k.sb(es, "rwcols", [128, 64])
            c5 = [k.sb(es, "c5_%d" % i, [128, 512]) for i in range(3)]
            ostg = [k.sb(es, "rwo%d" % i, [128, 512], BF16) for i in range(2)]
            k.memset("dve", raw[:, 15:16], 0.0)
            colload(cols[:, 0:12], rwkv_mu[l][0:1536], 12)
            k.dma("sp", cols[0:96, 12:13], rwkv_mu[l][C_WLO:C_WLO + 96].rearrange("(c p) -> p c", p=96))
            k.dma("sp", cols[0:96, 13:14], rwkv_mu[l][C_ALO:C_ALO + 96].rearrange("(c p) -> p c", p=96))
            colload(cols[:, 14:17], rwkv_mu[l][C_GLO:C_GLO + 384], 3)
            k.dma("sp", cols[0:64, 17:18], rwkv_mu[l][C_GLO + 384:C_GLO + 448].rearrange("(c p) -> p c", p=64))
            colload(cols[:, 18:22], rwkv_w0[l], 4)
            colload(cols[:, 22:26], rwkv_a0[l], 4)
            colload(cols[:, 26:30], rwkv_k_k[l], 4)
            colload(cols[:, 30:34], rwkv_k_a[l], 4)
            k.ts("dve", cols[:, 34:38], cols[:, 30:34], -1.0, ALU.mult, 1.0, ALU.add)
            colload(cols[:, 38:42], rwkv_r_k[l], 4)
            colload(cols[:, 42:46], rwkv_ln_w[l], 4)
            colload(cols[:, 46:50], rwkv_ln_b[l], 4)
            k.dma("sp", wup[:, :], rwkv_w_up[l])
            k.dma("sp", aup[:, :], rwkv_a_up[l])
            k.memset("dve", gup[:, 3, :], 0.0)
            k.memset("dve", TG[:, 3, :], 0.0)
            for gi in range(4):
                rows = 128 if gi < 3 else 64
                k.dma("pool", gup[0:rows, gi, :], rwkv_g_up[l][gi * 128: gi * 128 + rows, :])
            if l == 1:
                PVs = k.sb(es, "PVs", [64, S])
                vup = k.sb(es, "vup", [64, 512])
                k.dma("sp", cols[0:64, 50:51], vres_mu[0].rearrange("(c p) -> p c", p=64))
                colload(cols[:, 51:55], vres_bias[0], 4)
                k.dma("sp", vup[:, :], vres_up[0])
                shift_into(PVs, 64, raw, ta, cols[0:64, 50:51], C_PV)
            shift_into(TWt, 96, raw, ta, cols[0:96, 12:13], C_WLO)
            k.act(TWt[:, :], TWt[:, :], AF.Tanh)
            shift_into(TAt, 96, raw, ta, cols[0:96, 13:14], C_ALO)
            for gi in range(4):
                rows = 128 if gi < 3 else 64
                shift_into(tb, rows, raw, ta, cols[0:rows, 14 + gi:15 + gi], C_GLO + gi * 128)
                k.act(TG[0:rows, gi, :], tb[0:rows, 0:S], AF.Sigmoid)
            done("rw1")
            def pair(name):
                return k.sb(es, name, [128, 2, 128])
            Nn = [pair("Nn0"), pair("Nn1")]
            Nt = [pair("Nt0"), pair("Nt1")]
            MTt, ARK, ARB = pair("MTt"), pair("ARK"), pair("ARB")
            PP = [pair("PP0"), pair("PP1")]
            RH, WU, U0P, VP, KHP, BHP = (pair(n) for n in ("RH", "WU", "U0P", "VP", "KHP", "BHP"))
            for t_ in (WU, U0P, VP, KHP, BHP):
                k.memset("dve", t_[:, :, :], 0.0)
            GyT = k.sb(es, "GyT", [128, 128])
            GhT = k.sb(es, "GhT", [128, 128])
            Hs = k.sb(es, "Hs", [128, 64])
            Hbd = k.sb(es, "Hbd", [128, 128])
            gC = k.sb(es, "gC", [128, NB])
            identv = ident
            id2, MU2, MU02, ML2 = (cst[:, 896 + i * 256: 896 + (i + 1) * 256].m(lambda a_: a_.rearrange("p (h w) -> p h w", h=2)) for i in range(4))
            Lz = k.sb(es, "Lz", [128, 6, 128])
            hmc = (blk64[:, 0:1], blk64[:, 64:65])

            def p3(ps, c0, w):
                return ps[:, c0:c0 + 2 * w].m(lambda a: a.rearrange("p (h w) -> p h w", h=2))

            for ct in range(4):
                cc = lambda j: cols[:, j + ct:j + ct + 1]
                r0 = ct * 128
                k.memset("dve", raw[:, 15:16], 0.0)
                shift_into(KS, 128, raw, ta, cc(4), C_K + r0)
                shift_into(RS, 128, raw, ta, cc(0), C_R + r0)
                shift_into(VS, 128, raw, ta, cc(8), C_V + r0)
                for tcn in range(NT):
                    sl = slice(tcn * 512, (tcn + 1) * 512)
                    ps = PS[tcn % 2]
                    k.mm(ps[:, :], wup[:, r0:r0 + 128], TWt[:, sl])
                    k.act(LW[:, sl], ps[:, :], AF.Sigmoid, bias=cc(18))
                    ps = PS[2 + tcn % 2]
                    k.mm(ps[:, :], aup[:, r0:r0 + 128], TAt[:, sl])
                    k.act(AA[:, sl], ps[:, :], AF.Sigmoid, bias=cc(22))
                k.ts("dve", LW[:, 0:S], LW[:, 0:S], -0.6065306597126334, ALU.mult)
                k.ts("dve", ta[:, 0:S], KS[:, 0:S], cc(26), ALU.mult)
                k.act(tb[:, 0:S], ta[:, 0:S], AF.Square)
                for tcn in range(NT):
                    sl = slice(tcn * 512, (tcn + 1) * 512)
                    ps = PS[4 + tcn % 2]
                    c_ = c5[tcn % 3]
                    k.mm(ps[:, :], blk64, tb[:, sl])
                    k.ts("dve", c_[:, :], ps[:, :], 1e-24, ALU.max)
                    k.act(c_[:, :], c_[:, :], AF.Sqrt)
                    k.recip(c_[:, :], c_[:, :])
                    k.tt("dve", KK[:, sl], ta[:, sl], c_[:, :], ALU.mult)
                k.ts("dve", ta[:, 0:S], AA[:, 0:S], cc(30), ALU.mult, cc(34), ALU.add)
                k.tt("dve", KP[:, 0:S], KS[:, 0:S], ta[:, 0:S], ALU.mult)
                k.tt("dve", BB[:, 0:S], KK[:, 0:S], AA[:, 0:S], ALU.mult)
                if l == 0:
                    k.dma("sp", vf_d[r0:r0 + 128, :], VS[:, 0:S])
                else:
                    k.dma("sp", ta[:, 0:S], vf_d[r0:r0 + 128, :])
                    for tcn in range(NT):
                        sl = slice(tcn * 512, (tcn + 1) * 512)
                        ps = PS[6 + tcn % 2]
                        c_, c2 = c5[tcn % 2], c5[2]
                        k.mm(ps[:, :], vup[:, r0:r0 + 128], PVs[:, sl])
                        k.act(c_[:, :], ps[:, :], AF.Sigmoid, bias=cc(51))
                        k.tt("dve", c2[:, :], ta[:, sl], VS[:, sl], ALU.subtract)
                        k.tt("dve", c2[:, :], c2[:, :], c_[:, :], ALU.mult)
                        k.tt("dve", VS[:, sl], VS[:, sl], c2[:, :], ALU.add)
                for c in range(NB):
                    k.scan(CU[:, c * 128:(c + 1) * 128], ones, LW[:, c * 128:(c + 1) * 128])
                k.tt("dve", ta[:, 0:S], CU[:, 0:S], LW[:, 0:S], ALU.subtract)
                k.act(ta[:, 0:S], ta[:, 0:S], AF.Exp)
                k.stt(AT[:, 0:S], KK[:, 0:S], -1.0, ta[:, 0:S], ALU.mult, ALU.mult)
                k.act(ta[:, 0:S], CU[:, 0:S], AF.Exp, scale=-1.0)
                k.tt("dve", KT[:, 0:S], KP[:, 0:S], ta[:, 0:S], ALU.mult)
                k.tt("dve", BT[:, 0:S], BB[:, 0:S], ta[:, 0:S], ALU.mult)
                k.act(ta[:, 0:S], CU[:, 0:S], AF.Exp)
                k.tt("dve", RT[:, 0:S], RS[:, 0:S], ta[:, 0:S], ALU.mult)
                cu3 = CU[:, 0:S].m(lambda a: a.rearrange("p (c t) -> p c t", t=128))
                cuC = cu3.m(lambda a: a[:, :, 127:128].to_broadcast([128, NB, 128]))
                ta3 = ta[:, 0:S].m(lambda a: a.rearrange("p (c t) -> p c t", t=128))
                k.tt("dve", ta3, cuC, cu3, ALU.subtract)
                k.act(ta[:, 0:S], ta[:, 0:S], AF.Exp)
                k.act(gC[:, :], cu3.m(lambda a: a[:, :, 127]), AF.Exp)
                k.tt("dve", KH[:, 0:S], KP[:, 0:S], ta[:, 0:S], ALU.mult)
                k.tt("dve", BH[:, 0:S], BB[:, 0:S], ta[:, 0:S], ALU.mult)
                k.tt("dve", ta[:, 0:S], RS[:, 0:S], KP[:, 0:S], ALU.mult)
                k.ts("dve", ta[:, 0:S], ta[:, 0:S], cc(38), ALU.mult)
                for tcn in range(NT):
                    sl = slice(tcn * 512, (tcn + 1) * 512)
                    ps = PS[tcn % 2]
                    k.mm(ps[:, :], blk64, ta[:, sl])
                    k.tt("dve", BON[:, sl], ps[:, :], VS[:, sl], ALU.mult)
                for tcn in range(NT):
                    sl = slice(tcn * 512, (tcn + 1) * 512)
                    ps = PS[2 + tcn % 2]
                    for gi in range(4):
                        k.mm(ps[:, :], gup[:, gi, r0:r0 + 128], TG[:, gi, sl], start=(gi == 0), stop=(gi == 3))
                    k.cpa(G[:, sl], ps[:, :])
                done("rw2")
                import os
                if os.environ.get("KVAR", "") != "nomem":
                    k.memset("dve", Hs[:, :], 0.0)
                    k.memset("dve", Hbd[:, :], 0.0)
                for c in range(NB):
                    sl = slice(c * 128, (c + 1) * 128)
                    hp = lambda t_, hh: t_[hh * 64:(hh + 1) * 64, sl]
                    psT = PS[0]
                    import os
                    KVAR = os.environ.get("KVAR", "")
                    if KVAR != "notr":
                        for i_, src in enumerate((AT, VS, KH, BH)):
                            k.tr(psT[:, i_ * 128:(i_ + 1) * 128], src[:, sl], ident)
                    if KVAR != "noev":
                        k.cpa(RH[:, :, 0:64], p3(psT, 0, 64))
                        k.cpa(diag_ap(VP), p3(psT, 128, 64))
                        k.cpa(diag_ap(KHP), p3(psT, 256, 64))
                        k.cpa(diag_ap(BHP), p3(psT, 384, 64))
                    done("rw3a")
                    for i_, src_ in enumerate((BT, AT, KT)):
                        for hh in range(2):
                            k.ts("dve", Lz[:, i_ * 2 + hh, :], src_[:, sl], hmc[hh], ALU.mult)
                    specs = [(PS[1], 0, 0, AT, Nn[0], MU2), (PS[1], 256, 1, BT, Nt[0], ML2),
                             (PS[2], 0, 2, AT, MTt, MU2), (PS[2], 256, 2, RT, ARK, MU02),
                             (PS[3], 0, 0, RT, ARB, MU02)]
                    for ps, c0, li_, rt, dst, msk in specs:
                        for hh in range(2):
                            k.mm(ps[:, c0 + hh * 128:c0 + (hh + 1) * 128], Lz[:, li_ * 2 + hh, :], rt[:, sl])
                        k.tt("dve", dst[:, :, :], p3(ps, c0, 128), msk, ALU.mult)
                    k.tt("dve", PP[0][:, :, :], Nn[0][:, :, :], id2, ALU.add)
                    done("rw3")
                    a = 0
                    b = 0
                    for lev in range(1, 7):
                        psq = PS[4 + 2 * (lev % 2)]
                        psq2 = PS[lev % 2]
                        psc = PS[5 + 2 * (lev % 2)]
                        for hh in range(2):
                            k.mm(psq[:, 256 + hh * 128:256 + (hh + 1) * 128], Nn[a][:, hh, :], Nt[a][:, hh, :])
                        if lev < 6:
                            for hh in range(2):
                                k.mm(psq2[:, hh * 128:(hh + 1) * 128], Nt[a][:, hh, :], Nn[a][:, hh, :])
                        k.cp("act", Nt[1 - a][:, :, :], p3(psq, 256, 128))
                        if lev < 6:
                            k.cp("dve", Nn[1 - a][:, :, :], p3(psq2, 0, 128))
                        for hh in range(2):
                            k.mm(psc[:, hh * 128:(hh + 1) * 128], identv, PP[b][:, hh, :], start=True, stop=False)
                            k.mm(psc[:, hh * 128:(hh + 1) * 128], Nt[1 - a][:, hh, :], PP[b][:, hh, :],
                                 start=False, stop=True)
                        k.cpa(PP[1 - b][:, :, :], p3(psc, 0, 128))
                        a, b = 1 - a, 1 - b
                    Pf = PP[b]
                    done("rw4")
                    psM = PS[1]
                    for hh in range(2):
                        k.mm(psM[:, hh * 64:(hh + 1) * 64], MTt[:, hh, :], VP[:, hh, hh * 64:(hh + 1) * 64])
                    k.cpa(RH[:, :, 64:128], p3(psM, 0, 64))
                    psW = PS[2]
                    for hh in range(2):
                        k.mm(psW[:, hh * 128:(hh + 1) * 128], Pf[:, hh, :], RH[:, hh, :])
                    pw3 = p3(psW, 0, 128)
                    k.cpa(diag_ap(WU), pw3.m(lambda a_: a_[:, :, 0:64]))
                    k.cpa(diag_ap(U0P), pw3.m(lambda a_: a_[:, :, 64:128]))
                    psG = PS[3]
                    for hh in range(2):
                        k.mm(psG[:, 0:128], WU[:, hh, :], ARB[:, hh, :], start=(hh == 0), stop=(hh == 1))
                    k.tt("dve", GyT[:, :], psG[:, 0:128], RT[:, sl], ALU.add)
                    psY = PS[1]
                    seq = [(VP, ARK, 0), (U0P, ARB, 0), (VP, ARK, 1), (U0P, ARB, 1)]
                    for i_, (lt, rt, hh) in enumerate(seq):
                        k.mm(psY[:, 256:384], lt[:, hh, :], rt[:, hh, :], start=(i_ == 0), stop=False)
                    k.mm(psY[:, 256:384], Hbd[:, :], GyT[:, :], start=False, stop=True)
                    k.cpa(YT[:, sl], psY[:, 256:384])
                    psH = PS[2]
                    for hh in range(2):
                        k.mm(psH[:, 256:384], WU[:, hh, :], BHP[:, hh, :], start=(hh == 0), stop=(hh == 1))
                    k.stt(GhT[:, :], ident, gC[:, c:c + 1], psH[:, 256:384], ALU.mult, ALU.add)
                    psS = PS[3]
                    seq = [(KHP, VP, 0), (KHP, VP, 1), (BHP, U0P, 0), (BHP, U0P, 1)]
                    for i_, (lt, rt, hh) in enumerate(seq):
                        k.mm(psS[:, 256:320], lt[:, hh, :], rt[:, hh, hh * 64:(hh + 1) * 64], start=(i_ == 0), stop=False)
                    k.mm(psS[:, 256:320], GhT[:, :], Hs[:, :], start=False, stop=True)
                    k.cp("dve", Hs[:, :], psS[:, 256:320])
                    k.cp("dve", Hbd[0:64, 0:64], Hs[0:64, :])
                    k.cp("act", Hbd[64:128, 64:128], Hs[64:128, :])
                    done("rw5")
                for tcn in range(NT):
                    sl = slice(tcn * 512, (tcn + 1) * 512)
                    ps1, ps2 = PS[4 + tcn % 2], PS[6 + tcn % 2]
                    d_, q_ = c5[0], c5[1]
                    k.mm(ps1[:, :], blk64, YT[:, sl])
                    k.stt(d_[:, :], ps1[:, :], -1.0 / 64, YT[:, sl], ALU.mult, ALU.add)
                    k.act(q_[:, :], d_[:, :], AF.Square)
                    k.mm(ps2[:, :], blk64, q_[:, :])
                    k.ts("dve", q_[:, :], ps2[:, :], 1.0 / 64, ALU.mult, 64e-5, ALU.add)
                    k.act(q_[:, :], q_[:, :], AF.Sqrt)
                    k.recip(q_[:, :], q_[:, :])
                    k.tt("dve", d_[:, :], d_[:, :], q_[:, :], ALU.mult)
                    k.ts("dve", d_[:, :], d_[:, :], cc(42), ALU.mult, cc(46), ALU.add)
                    k.tt("dve", d_[:, :], d_[:, :], BON[:, sl], ALU.add)
                    o_ = ostg[tcn % 2]
                    k.tt("dve", o_[:, :], d_[:, :], G[:, sl], ALU.mult)
                    k.dma("sp", ym_d[r0:r0 + 128, sl], o_[:, :])

        def attn_phase(es, l):
            qT = [k.sb(es, "qT%d" % i, [64, S], BF16) for i in range(4)]
            kT = [k.sb(es, "kT%d" % i, [64, S], BF16) for i in range(4)]
            vt = [k.sb(es, "vt%d" % i, [128, NB, 128], BF16) for i in range(2)]
            TB = [k.sb(es, "TB%d" % i, [128, TBW]) for i in range(2)]
            M0 = k.sb(es, "M0", [128, TBW])
            M1 = k.sb(es, "M1", [128, TBW])
            tS = [k.sb(es, "tS%d" % i, [128, 512]) for i in range(4)]
            eB = [k.sb(es, "eB%d" % i, [128, 512], BF16) for i in range(2)]
            Rr = k.sb(es, "Rr", [128, 512])
            O0 = k.sb(es, "O0", [128, 512])
            O1 = k.sb(es, "O1", [128, 512])
            ostg = [k.sb(es, "aostg%d" % i, [128, 512], BF16) for i in range(2)]
            sub = k.sb(es, "subc", [128, 2])
            k.dma("sp", M0[:, :], m0_d)
            k.dma("sp", M1[:, :], m1_d)
            colload(sub[:, 0:1], diff_subln[l], 1)
            lam_init = 0.8 - 0.6 * math.exp(-0.3 * l)
            k.ts("dve", sub[:, 1:2], sub[:, 0:1], 1.0 - lam_init, ALU.mult)
            vtok3 = [vtok_d[i].rearrange("(b p) c -> p b c", p=128) for i in range(3)]
            cnt = {"u": 0, "a": 0, "t": 0, "o": 0}

            def load_qk(slot, qrow, krow):
                k.dma("pool", qT[slot][:, :], pT_d[qrow:qrow + 64, :])
                k.dma("pool", kT[slot][:, :], pT_d[krow:krow + 64, :])

            Hk = k.sb(es, "Hk", [128, TBW])

            def load_tb(slot, hb):
                src = bass.AP(tensor=bias_h, offset=hb * 2560 + 1, ap=[[1, 128], [1, TBW]])
                k.dma("sp", Hk[:, :], src)
                for i_, c0 in enumerate(range(0, TBW, 512)):
                    w_ = min(512, TBW - c0)
                    ps = PS[i_ % 4]
                    k.mm(ps[:, 0:w_], JX, Hk[:, c0:c0 + w_])
                    k.cpa(TB[slot][:, c0:c0 + w_], ps[:, 0:w_])

            def softmax_pass(qv, kv, vv, dv, tb, qc, dst):
                u = cnt["u"]
                cnt["u"] += 1
                psN, psD = PS[4 + u % 2], PS[6 + u % 2]
                jl = 4 * qc + 4
                for j in range(jl):
                    off = 384 + 512 * qc - 128 * j
                    psA = PS[cnt["a"] % 4]
                    cnt["a"] += 1
                    t_ = tS[cnt["t"] % 2]
                    e_ = eB[cnt["t"] % 2]
                    cnt["t"] += 1
                    k.mm(psA[:, :], kv[:, j * 128:(j + 1) * 128], qv[:, qc * 512:(qc + 1) * 512])
                    k.stt(t_[:, :], psA[:, :], 0.125, tb[:, off:off + 512], ALU.mult, ALU.add)
                    k.act(e_[:, :], t_[:, :], AF.Exp)
                    k.mm(psN[0:dv, :], vv[:, j, 0:dv], e_[:, :], start=(j == 0), stop=(j == jl - 1))
                    k.mm(psD[:, :], onesb[:, :], e_[:, :], start=(j == 0), stop=(j == jl - 1))
                rd = tS[2]
                k.recip(rd[:, :], psD[:, :])
                k.tt("dve", dst[0:dv, :], psN[0:dv, :], rd[0:dv, :], ALU.mult)

            load_qk(0, C_QC, C_KC)
            load_tb(0, 0)
            k.dma("sp", vt[0][:, :, 0:64], vtok3[1][:, :, 0:64])
            for h in range(8):
                s_ = h % 2
                if h + 1 < 8:
                    load_qk(1 - s_, C_QC + (h + 1) * 64, C_KC + (h + 1) * 64)
                    load_tb(1 - s_, h + 1)
                    k.dma("sp", vt[1 - s_][:, :, 0:64], vtok3[1][:, :, (h + 1) * 64:(h + 2) * 64])
                for qc in range(NT):
                    softmax_pass(qT[s_], kT[s_], vt[s_], 64, TB[s_], qc, O0)
                    o_ = ostg[cnt["o"] % 2]
                    cnt["o"] += 1
                    k.cp("act", o_[0:64, :], O0[0:64, :])
                    k.dma("sp", ym_d[1024 + h * 64:1024 + (h + 1) * 64, qc * 512:(qc + 1) * 512], o_[0:64, :])
            for hd in range(4):
                s_ = hd % 2
                for c in range(2):
                    load_qk(2 * s_ + c, C_QD + hd * 128 + c * 64, C_KD + hd * 128 + c * 64)
                load_tb(s_, 8 + hd)
                k.dma("sp", vt[s_][:, :, :], vtok3[2][:, :, hd * 128:(hd + 1) * 128])
                for qc in range(NT):
                    softmax_pass(qT[2 * s_], kT[2 * s_], vt[s_], 128, TB[s_], qc, O0)
                    softmax_pass(qT[2 * s_ + 1], kT[2 * s_ + 1], vt[s_], 128, TB[s_], qc, O1)
                    k.stt(O0[:, :], O1[:, :], lamc[:, l:l + 1], O0[:, :], ALU.mult, ALU.add)
                    sq = tS[3]
                    k.act(sq[:, :], O0[:, :], AF.Square)
                    psX = PS[cnt["a"] % 4]
                    cnt["a"] += 1
                    k.mm(psX[:, :], ones, sq[:, :])
                    k.ts("dve", sq[:, :], psX[:, :], 1.0 / 128, ALU.mult, 1e-5, ALU.add)
                    k.act(sq[:, :], sq[:, :], AF.Sqrt)
                    k.recip(sq[:, :], sq[:, :])
                    k.tt("dve", O0[:, :], O0[:, :], sq[:, :], ALU.mult)
                    o_ = ostg[cnt["o"] % 2]
                    cnt["o"] += 1
                    k.ts("dve", o_[:, :], O0[:, :], sub[:, 1:2], ALU.mult)
                    k.dma("sp", ym_d[1536 + hd * 128:1536 + (hd + 1) * 128, qc * 512:(qc + 1) * 512], o_[:, :])
            load_qk(0, C_QB, C_KB)
            k.dma("sp", vt[0][:, :, 0:64], vtok3[0][:, :, 0:64])
            for h in range(8):
                s_ = h % 2
                if h + 1 < 8:
                    load_qk(1 - s_, C_QB + (h + 1) * 64, C_KB + (h + 1) * 64)
                    k.dma("sp", vt[1 - s_][:, :, 0:64], vtok3[0][:, :, (h + 1) * 64:(h + 2) * 64])
                qv, kv, vv = qT[s_], kT[s_], vt[s_]
                for qc in range(NT):
                    u = cnt["u"]
                    cnt["u"] += 1
                    psN = PS[6 + u % 2]
                    first = True
                    jl = 4 * qc + 4
                    for j in range(jl - 1, -1, -1):
                        off = 384 + 512 * qc - 128 * j
                        diag = j >= 4 * qc
                        a_ = cnt["a"]
                        cnt["a"] += 1
                        psA, psB, psC = PS[a_ % 2], PS[2 + a_ % 2], PS[4 + a_ % 2]
                        e1, sp, u_ = tS[0], tS[1], tS[2]
                        k.mm(psA[:, :], kv[:, j * 128:(j + 1) * 128], qv[:, qc * 512:(qc + 1) * 512])
                        k.act(e1[:, :], psA[:, :], AF.Exp, scale=0.125)
                        k.act(sp[:, :], e1[:, :], AF.Ln, bias=1.0)
                        if diag:
                            k.tt("dve", sp[:, :], sp[:, :], M1[:, off:off + 512], ALU.mult)
                        k.mm(psB[:, :], ML, sp[:, :])
                        if j > 0:
                            k.mm(psC[:, :], ones, sp[:, :])
                        k.stt(u_[:, :], psA[:, :], 0.125, sp[:, :], ALU.mult, ALU.subtract)
                        k.tt("dve", u_[:, :], u_[:, :], psB[:, :], ALU.subtract)
                        if not first:
                            k.tt("dve", u_[:, :], u_[:, :], Rr[:, :], ALU.subtract)
                        if diag:
                            k.tt("dve", u_[:, :], u_[:, :], M0[:, off:off + 512], ALU.add)
                        e_ = eB[a_ % 2]
                        k.act(e_[:, :], u_[:, :], AF.Exp)
                        k.mm(psN[0:64, :], vv[:, j, 0:64], e_[:, :], start=first, stop=(j == 0))
                        if j > 0:
                            if first:
                                k.cp("dve", Rr[:, :], psC[:, :])
                            else:
                                k.tt("dve", Rr[:, :], Rr[:, :], psC[:, :], ALU.add)
                        first = False
                    o_ = ostg[cnt["o"] % 2]
                    cnt["o"] += 1
                    k.cp("act", o_[0:64, :], psN[0:64, :])
                    k.dma("sp", ym_d[512 + h * 64:512 + (h + 1) * 64, qc * 512:(qc + 1) * 512], o_[0:64, :])

        for l in range(NL):
            mc = lambda a, b, l=l: modc[:, l * 96 + a: l * 96 + b]
            ncols_in = NIN + (64 if l == 1 else 0)
            with contextlib.ExitStack() as es:
                hT = k.sb(es, "hT", [128, 16, S], BF16)
                with contextlib.ExitStack() as es2:
                    norm_phase(es2, hT, l, 0, S, A1[:, (l * 2) * 16:(l * 2 + 1) * 16], mc(0, 16))
                P.barrier()
                wts = [k.sb(es, "win%d" % i, [128, 16, 512], BF16) for i in range(2)]
                stg = [k.sb(es, "pstg%d" % i, [128, 512]) for i in range(4)]
                stgb = [k.sb(es, "vstg%d" % i, [128, 512], BF16) for i in range(2)]
                segs = [(0, C_VB), (C_QC, C_VC), (C_QD, C_VD)]
                blocks = []
                for (a, b) in segs:
                    for c0 in range(a, b, 512):
                        blocks.append(("fm", c0, min(512, b - c0)))
                if l == 1:
                    blocks.append(("pv", C_PV, 64))
                for i, c0 in enumerate((C_VB, C_VC, C_VD)):
                    blocks.append(("tm", c0, 512, i))

                def load_block(bi):
                    blk = blocks[bi]
                    wt = wts[bi % 2]
                    if blk[0] == "pv":
                        wload(wt, vres_down[0], 16, 64)
                    else:
                        wload(wt, w_in[l][:, blk[1]:blk[1] + blk[2]], 16, blk[2])
                load_block(0)
                nps = 0
                for bi, blk in enumerate(blocks):
                    if bi + 1 < len(blocks):
                        load_block(bi + 1)
                    wt = wts[bi % 2]
                    if blk[0] in ("fm", "pv"):
                        c0, wd = blk[1], blk[2]
                        for m in range(0, wd, 128):
                            mw = min(128, wd - m)
                            for tcn in range(NT):
                                ps = PS[nps % 4]
                                nps += 1
                                for kc in range(16):
                                    k.mm(ps[0:mw, :], wt[:, kc, m:m + mw], hT[:, kc, tcn * 512:(tcn + 1) * 512],
                                         start=(kc == 0), stop=(kc == 15))
                                st = stg[nps % 4]
                                k.cpa(st[0:mw, :], ps[0:mw, :])
                                k.dma("sp", pT_d[c0 + m:c0 + m + mw, tcn * 512:(tcn + 1) * 512], st[0:mw, :])
                    else:
                        vi = blk[3]
                        for tb in range(NB):
                            ps = PS[nps % 4]
                            nps += 1
                            for kc in range(16):
                                k.mm(ps[:, :], hT[:, kc, tb * 128:(tb + 1) * 128], wt[:, kc, :],
                                     start=(kc == 0), stop=(kc == 15))
                            st = stgb[nps % 2]
                            k.cpa(st[:, :], ps[:, :])
                            k.dma("sp", vtok_d[vi, tb * 128:(tb + 1) * 128, :], st[:, :])
            P.barrier()
            done("gemm1")

            with contextlib.ExitStack() as es:
                rwkv_phase(es, l)
            P.barrier()
            done("rwkv")
            with contextlib.ExitStack() as es:
                attn_phase(es, l)
            P.barrier()
            done("attn")

            with contextlib.ExitStack() as es:
                ymT = k.sb(es, "ymT", [128, 16, S], BF16)
                for kc in range(16):
                    k.dma("sp", ymT[:, kc, :], ym3[:, kc, :])
                wts = [k.sb(es, "wout%d" % i, [128, 16, 512], BF16) for i in range(2)]
                xst = [k.sb(es, "xst%d" % i, [128, 512]) for i in range(4)]
                wload(wts[0], w_out[l][:, 0:512], 16, 512)
                nps = 0
                for bi in range(4):
                    if bi + 1 < 4:
                        wload(wts[(bi + 1) % 2], w_out[l][:, (bi + 1) * 512:(bi + 2) * 512], 16, 512)
                    wt = wts[bi % 2]
                    for m in range(4):
                        dc = bi * 4 + m
                        for tcn in range(NT):
                            ps = PS[nps % 4]
                            nps += 1
                            for kc in range(16):
                                k.mm(ps[:, :], wt[:, kc, m * 128:(m + 1) * 128], ymT[:, kc, tcn * 512:(tcn + 1) * 512],
                                     start=(kc == 0), stop=(kc == 15))
                            resid_epilogue(xst, ps, dc * 128, slice(tcn * 512, (tcn + 1) * 512), mc(32 + dc, 33 + dc))
            P.barrier()
            done("wout")

            HT = min(S, 1024)
            for half in range(S // HT):
                t0 = half * HT
                with contextlib.ExitStack() as es:
                    aT = [k.sb(es, "aT%d" % f, [128, HT], BF16) for f in range(FC)]
                    with contextlib.ExitStack() as esA:
                        h2T = k.sb(esA, "h2T", [128, 16, HT], BF16)
                        with contextlib.ExitStack() as es2:
                            norm_phase(es2, h2T, l, t0, HT, A1[:, (l * 2 + 1) * 16:(l * 2 + 2) * 16], mc(48, 64))
                        P.barrier()
                        w13 = [k.sb(esA, "w13_%d" % i, [128, 16, 512], BF16) for i in range(2)]
                        nblk = 2 * DFF // 512

                        def ld13(b_):
                            wload(w13[b_ % 2], ffn_w13[l][:, b_ * 512:(b_ + 1) * 512], 16, 512)
                        ld13(0)
                        nps = 0
                        for b_ in range(nblk):
                            if b_ + 1 < nblk:
                                ld13(b_ + 1)
                            for m in range(4):
                                col = b_ * 512 + m * 128
                                is_up = col >= DFF
                                f = (col - DFF) // 128 if is_up else col // 128
                                for tcn in range(HT // 512):
                                    ps = PS[nps % 8]
                                    nps += 1
                                    ts_ = slice(tcn * 512, (tcn + 1) * 512)
                                    for kc in range(16):
                                        k.mm(ps[:, :], w13[b_ % 2][:, kc, m * 128:(m + 1) * 128], h2T[:, kc, ts_],
                                             start=(kc == 0), stop=(kc == 15))
                                    if not is_up:
                                        k.act(aT[f][:, ts_], ps[:, :], AF.Silu)
                                    else:
                                        k.tt("dve", aT[f][:, ts_], aT[f][:, ts_], ps[:, :], ALU.mult)
                    P.barrier()
                    with contextlib.ExitStack() as esB:
                        w2 = [k.sb(esB, "w2_%d" % i, [128, FC, 256], BF16) for i in range(2)]
                        xst = [k.sb(esB, "fxst%d" % i, [128, 512]) for i in range(4)]

                        def ld2(b_):
                            src_ = ffn_w2[l][:, b_ * 256:(b_ + 1) * 256].rearrange("(c p) n -> p c n", p=128)
                            wt = w2[b_ % 2]
                            for a_ in range(0, FC, 11):
                                k.dma("pool", wt[:, a_:a_ + 11, :], src_[:, a_:a_ + 11, :])
                        ld2(0)
                        nps = 0
                        for b_ in range(8):
                            if b_ + 1 < 8:
                                ld2(b_ + 1)
                            for m in range(2):
                                dc = b_ * 2 + m
                                for tcn in range(HT // 512):
                                    ps = PS[nps % 8]
                                    nps += 1
                                    ts_ = slice(tcn * 512, (tcn + 1) * 512)
                                    for f in range(FC):
                                        k.mm(ps[:, :], w2[b_ % 2][:, f, m * 128:(m + 1) * 128], aT[f][:, ts_],
                                             start=(f == 0), stop=(f == FC - 1))
                                    resid_epilogue(xst, ps, dc * 128, slice(t0 + tcn * 512, t0 + (tcn + 1) * 512),
                                                   mc(80 + dc, 81 + dc))
                P.barrier()

        with contextlib.ExitStack() as es:
            hn = k.sb(es, "hn", [128, 16, 512])
            ost = [k.sb(es, "ost%d" % i, [128, D]) for i in range(2)]

            for tcn in range(NT):
                with contextlib.ExitStack() as es2:
                    norm_phase(es2, hn, 0, tcn * 512, 512, gcol[:, 64:80], None)
                P.barrier()
                for q in range(4):
                    tb = tcn * 4 + q
                    o_ = ost[tb % 2]
                    for g in range(4):
                        ps = PS[(tb * 4 + g) % 8]
                        for j in range(4):
                            kc = g * 4 + j
                            k.tr(ps[:, j * 128:(j + 1) * 128], hn[:, kc, q * 128:(q + 1) * 128], ident)
                        k.cpa(o_[:, g * 512:(g + 1) * 512], ps[:, :])
                    k.dma("sp", out_d[tb * 128:(tb + 1) * 128, :], o_[:, :])
                P.barrier()
    except StopBuild:
        pass
    P.barrier()
    P.emit(es0)
    ncd.close()
    try:
        es0.close()
    except AssertionError:
        pass
    return nc


_CACHE = {}


def make_in_maps(inputs, S, nb):
    consts = host_consts(S)
    shared = {}
    for name, arr in inputs.items():
        if name in ("x", "c"):
            continue
        a = np.asarray(arr, dtype=np.float32)
        if name == "diff_lambda":
            a = a.reshape(2, 256)
        if name == "rwkv_r_k":
            a = a.reshape(2, 512)
        shared[name] = np.ascontiguousarray(a)
    shared.update(consts)
    x = np.asarray(inputs["x"], dtype=np.float32)
    c = np.asarray(inputs["c"], dtype=np.float32)
    maps = []
    for b in range(nb):
        m = dict(shared)
        m["x"] = np.ascontiguousarray(x[b, :S])
        m["c"] = np.ascontiguousarray(c[b])
        maps.append(m)
    return maps


def kernel(**inputs):
    S = 2048
    nc = build(S, 2)
    maps = make_in_maps(inputs, S, 8)
    res = run_bass_kernel_spmd(nc, maps, core_ids=list(range(8)))
    out = np.stack([np.asarray(r["out"], dtype=np.float32) for r in res.results], axis=0)
    return out
```

```python
import math
import contextlib
import numpy as np
import concourse.bass as bass
import concourse.mybir as mybir
from concourse.bass_utils import run_bass_kernel_spmd

F32 = mybir.dt.float32
BF16 = mybir.dt.bfloat16
AF = mybir.ActivationFunctionType
ALU = mybir.AluOpType

D = 2048
KC = 16
DFF = 5632
FC = 44
NIN = 6784
NEG = -30000.0
C_R, C_K, C_V, C_WLO, C_ALO, C_GLO = 0, 512, 1024, 1536, 1632, 1728
C_QB, C_KB, C_VB = 2176, 2688, 3200
C_QC, C_KC, C_VC = 3712, 4224, 4736
C_QD, C_KD, C_VD = 5248, 5760, 6272
C_PV = 6784
TBW = 2432


class View:
    __slots__ = ("t", "ap")

    def __init__(self, t, ap):
        self.t = t
        self.ap = ap

    def m(self, f):
        return View(self.t, f(self.ap))

    def __getitem__(self, k):
        return View(self.t, self.ap[k])


class T:
    __slots__ = ("h", "lw", "rd", "name", "psum")

    def __init__(self, h, name="", psum=False):
        self.h = h
        self.lw = None
        self.rd = []
        self.name = name
        self.psum = psum

    def __getitem__(self, k):
        return View(self, self.h[k])

    def v(self, ap):
        return View(self, ap)


def _ap(x):
    return x.ap if isinstance(x, View) else x


def _ts(*xs):
    return [x.t for x in xs if isinstance(x, View)]


class Prog:
    NQ = 8

    def __init__(self, nc):
        self.nc = nc
        self.ops = []
        self.pos = {}
        self.last = {}
        self.bar = None
        self.bar_seen = set()
        self.unconsumed = set()

    def op(self, eng, fn, R=(), W=(), dma=False, extra=()):
        idx = len(self.ops)
        deps = set(extra)
        for t in R:
            if t.lw is not None:
                deps.add(t.lw)
            if t.psum:
                deps.update(r for r in t.rd if self.ops[r]["eng"] != eng)
        for t in W:
            if t.lw is not None:
                deps.add(t.lw)
            deps.update(t.rd)
        if self.bar is not None and eng not in self.bar_seen:
            deps.add(self.bar)
            self.bar_seen.add(eng)
        pos = self.pos.get(eng, 0)
        keep = set()
        for d in deps:
            o = self.ops[d]
            if o["eng"] == eng and not o["dma"] and not dma:
                if eng == "pe":
                    continue
                if pos - o["pos"] > 3:
                    continue
            keep.add(d)
            if o["dma"]:
                self.unconsumed.discard(d)
        best = {}
        keep2 = set()
        for d in keep:
            o = self.ops[d]
            if o["dma"]:
                keep2.add(d)
            elif best.get(o["eng"], -1) < d:
                best[o["eng"]] = d
        keep2.update(best.values())
        keep = keep2
        self.ops.append(dict(eng=eng, fn=fn, deps=keep, dma=dma, pos=pos))
        self.pos[eng] = pos + 1
        self.last[eng] = idx
        if dma:
            self.unconsumed.add(idx)
        for t in R:
            t.rd.append(idx)
        for t in W:
            t.lw = idx
            t.rd = []
        return idx

    def barrier(self):
        deps = set(self.last.values()) | set(self.unconsumed)
        self.unconsumed = set()
        nc = self.nc
        self.bar = None
        idx = self.op("sp", lambda: nc.sync.nop(nofuse=True), extra=deps)
        self.bar = idx
        self.bar_seen = {"sp"}
        return idx

    def emit(self, es):
        nc = self.nc
        engs = {"pe": nc.tensor, "act": nc.scalar, "dve": nc.vector, "pool": nc.gpsimd, "sp": nc.sync}
        ops = self.ops
        flagged = [False] * len(ops)
        for o in ops:
            for d in o["deps"]:
                flagged[d] = True
        sems = {}
        for e in engs:
            sems[e] = es.enter_context(nc.semaphore("s_" + e))
        dsems = {}
        for e in ("sp", "pool"):
            dsems[e] = [es.enter_context(nc.semaphore("d_%s%d" % (e, i))) for i in range(self.NQ)]
        cnt = {e: 0 for e in engs}
        dcnt = {"sp": 0, "pool": 0}
        known = {e: {} for e in engs}
        ev = [None] * len(ops)
        for idx, o in enumerate(ops):
            e = o["eng"]
            E = engs[e]
            need = {}
            for d in o["deps"]:
                sm, val = ev[d]
                if need.get(sm, (None, 0))[1] < val:
                    need[sm] = (sm, val)
            if o["dma"]:
                k = dcnt[e]
                sm = dsems[e][k % self.NQ]
                prev = 16 * (k // self.NQ)
                if prev > 0 and need.get(sm, (None, 0))[1] < prev:
                    need[sm] = (sm, prev)
            for sm, val in need.values():
                key = id(sm)
                if known[e].get(key, 0) >= val:
                    continue
                E.wait_ge(sm, val)
                known[e][key] = val
            ins = o["fn"]()
            if o["dma"]:
                k = dcnt[e]
                sm = dsems[e][k % self.NQ]
                ins.then_inc(sm, 16)
                ev[idx] = (sm, 16 * (k // self.NQ + 1))
                dcnt[e] = k + 1
            elif flagged[idx]:
                cnt[e] += 1
                ins.then_inc(sems[e], 1)
                ev[idx] = (sems[e], cnt[e])


class K:
    def __init__(self, nc, P, es):
        self.nc = nc
        self.P = P
        self.es = es
        self.eng = {"act": nc.scalar, "dve": nc.vector, "pool": nc.gpsimd}
        self.rr = 0

    def sb(self, es, name, shape, dt=F32):
        self.uid = getattr(self, "uid", 0) + 1
        name = "t%d_%s" % (self.uid, name)
        return T(es.enter_context(self.nc.sbuf_tensor(name, list(shape), dt)), name)

    def dma(self, q, out, in_):
        nc = self.nc
        E = nc.sync if q == "sp" else nc.gpsimd
        o, i = _ap(out), _ap(in_)
        return self.P.op(q, lambda: E.dma_start(out=o, in_=i), R=_ts(in_), W=_ts(out), dma=True)

    def mm(self, out, lhsT, rhs, start=True, stop=True):
        nc = self.nc
        o, l, r = out.ap, lhsT.ap, rhs.ap
        return self.P.op("pe", lambda: nc.tensor.matmul(o, l, r, start=start, stop=stop),
                         R=_ts(lhsT, rhs), W=_ts(out))

    def tr(self, out, in_, ident):
        nc = self.nc
        o, i, d = out.ap, in_.ap, ident.ap
        return self.P.op("pe", lambda: nc.tensor.transpose(o, i, d), R=_ts(in_, ident), W=_ts(out))

    def tt(self, eng, out, in0, in1, op):
        E = self.eng[eng]
        o, a, b = out.ap, _ap(in0), _ap(in1)
        return self.P.op(eng, lambda: E.tensor_tensor(out=o, in0=a, in1=b, op=op), R=_ts(in0, in1), W=_ts(out))

    def ts(self, eng, out, in0, s1, op0, s2=None, op1=None):
        E = self.eng[eng]
        o, a = out.ap, _ap(in0)
        x1, x2 = _ap(s1), _ap(s2)
        if op1 is None:
            f = lambda: E.tensor_scalar(out=o, in0=a, scalar1=x1, scalar2=None, op0=op0)
        else:
            f = lambda: E.tensor_scalar(out=o, in0=a, scalar1=x1, scalar2=x2, op0=op0, op1=op1)
        return self.P.op(eng, f, R=_ts(in0, s1, s2), W=_ts(out))

    def stt(self, out, in0, sc, in1, op0, op1):
        E = self.nc.vector
        o, a, s, b = out.ap, _ap(in0), _ap(sc), _ap(in1)
        return self.P.op("dve", lambda: E.scalar_tensor_tensor(out=o, in0=a, scalar=s, in1=b, op0=op0, op1=op1),
                         R=_ts(in0, sc, in1), W=_ts(out))

    def cp(self, eng, out, in_):
        o, a = out.ap, _ap(in_)
        if eng == "act":
            E = self.nc.scalar
            return self.P.op(eng, lambda: E.copy(out=o, in_=a), R=_ts(in_), W=_ts(out))
        E = self.eng[eng]
        return self.P.op(eng, lambda: E.tensor_copy(out=o, in_=a), R=_ts(in_), W=_ts(out))

    def cpa(self, out, in_):
        self.rr ^= 1
        return self.cp("act" if self.rr else "dve", out, in_)

    def act(self, out, in_, func, bias=None, scale=1.0):
        E = self.nc.scalar
        o, a, b = out.ap, _ap(in_), _ap(bias)
        sc = _ap(scale)
        kw = {}
        if bias is not None:
            kw["bias"] = b
        f = lambda: E.activation(out=o, in_=a, func=func, scale=sc, **kw)
        return self.P.op("act", f, R=_ts(in_, bias, scale), W=_ts(out))

    def recip(self, out, in_):
        E = self.nc.vector
        o, a = out.ap, _ap(in_)
        return self.P.op("dve", lambda: E.reciprocal(out=o, in_=a), R=_ts(in_), W=_ts(out))

    def memset(self, eng, out, val):
        E = self.eng[eng]
        o = out.ap
        return self.P.op(eng, lambda: E.memset(o, val), W=_ts(out))

    def scan(self, out, d0, d1):
        E = self.nc.vector
        o, a, b = out.ap, _ap(d0), _ap(d1)
        return self.P.op("dve", lambda: E.tensor_tensor_scan(out=o, data0=a, data1=b, initial=0.0,
                                                             op0=ALU.mult, op1=ALU.add),
                         R=_ts(d0, d1), W=_ts(out))


def host_consts(S):
    c = {}
    i = np.arange(128)
    ident = np.eye(128, dtype=np.float32)
    ones = np.ones((128, 128), np.float32)
    blk = np.zeros((128, 128), np.float32)
    blk[:64, :64] = 1
    blk[64:, 64:] = 1
    MU = (i[None, :] > i[:, None]).astype(np.float32)
    MU0 = (i[None, :] >= i[:, None]).astype(np.float32)
    ML = (i[:, None] > i[None, :]).astype(np.float32)
    c["cst"] = np.concatenate([ident, ones, blk, MU, MU0, ML, ident[::-1].copy(), ident, ident, MU, MU, MU0, MU0, ML, ML], axis=1)
    r = np.arange(-512, 2048)
    rp = np.maximum(r, 0)
    d_f = np.maximum(rp, 1).astype(np.float32)
    large = 16 + (np.log(d_f / np.float32(16)) / np.float32(math.log(2048 / 16)) * np.float32(16)).astype(np.int32)
    large = np.minimum(large, 31)
    bucket = np.where(rp < 16, rp, large)
    E = np.zeros((34, 2560), np.float32)
    E[bucket, np.arange(2560)] = 1.0
    E[:32, r < 0] = 0.0
    cnt = ((rp <= 128).astype(np.int32) + ((rp % 4 == 0) & (rp <= 512)).astype(np.int32)
           + ((rp % 16 == 0) & (rp <= 2048)).astype(np.int32))
    exC = np.where(cnt > 0, np.log(np.maximum(cnt, 1).astype(np.float64)), NEG).astype(np.float32)
    exC[r < 0] = NEG
    exD = np.where(r < 0, NEG, 0.0).astype(np.float32)
    E[32] = exC
    E[33] = exD
    c["e34"] = E
    ind = np.zeros((2, 12), np.float32)
    ind[0, :8] = 1
    ind[1, 8:] = 1
    c["ind2"] = ind
    xx = np.arange(TBW)[None, :] - 384 - i[:, None]
    c["m0"] = np.where(xx > 0, 0.0, NEG).astype(np.float32)
    c["m1"] = (xx > 0).astype(np.float32)
    rm = np.ones((128, S), np.float32)
    rm[:, ::128] = 0.0
    c["rmask"] = rm
    return c


class StopBuild(Exception):
    pass


def build(S=2048, NL=2, dbg=False, stop_after=None):
    nc = bass.Bass("TRN2", target_bir_lowering=False)
    NT = S // 512
    NB = S // 128
    es0 = contextlib.ExitStack()
    P = Prog(nc)
    k = K(nc, P, es0)

    def din(name, shape):
        return nc.dram_tensor(name, list(shape), F32, kind="ExternalInput").ap()

    x_d = din("x", [S, D])
    c_d = din("c", [D])
    w_ada = din("w_ada", [2, D, 6 * D])
    b_ada = din("b_ada", [2, 6 * D])
    norm_gain = din("norm_gain", [2, 2, D])
    w_in = din("w_in", [2, D, NIN])
    w_out = din("w_out", [2, D, D])
    rel_bias = din("rel_bias", [32, 12])
    rwkv_mu = din("rwkv_mu", [2, 2176])
    rwkv_w0 = din("rwkv_w0", [2, 512])
    rwkv_w_up = din("rwkv_w_up", [2, 96, 512])
    rwkv_a0 = din("rwkv_a0", [2, 512])
    rwkv_a_up = din("rwkv_a_up", [2, 96, 512])
    rwkv_g_up = din("rwkv_g_up", [2, 448, 512])
    rwkv_k_k = din("rwkv_k_k", [2, 512])
    rwkv_k_a = din("rwkv_k_a", [2, 512])
    rwkv_r_k = din("rwkv_r_k", [2, 512])
    rwkv_ln_w = din("rwkv_ln_w", [2, 512])
    rwkv_ln_b = din("rwkv_ln_b", [2, 512])
    vres_down = din("vres_down", [1, D, 64])
    vres_mu = din("vres_mu", [1, 64])
    vres_up = din("vres_up", [1, 64, 512])
    vres_bias = din("vres_bias", [1, 512])
    diff_lambda = din("diff_lambda", [2, 256])
    diff_subln = din("diff_subln", [2, 128])
    ffn_w13 = din("ffn_w13", [2, D, 2 * DFF])
    ffn_w2 = din("ffn_w2", [2, DFF, D])
    final_gain = din("final_gain", [D])
    cst_d = din("cst", [128, 1920])
    e34_d = din("e34", [34, 2560])
    ind2_d = din("ind2", [2, 12])
    m0_d = din("m0", [128, TBW])
    m1_d = din("m1", [128, TBW])
    rmask_d = din("rmask", [128, S])
    out_d = nc.dram_tensor("out", [S, D], F32, kind="ExternalOutput").ap()

    okind = "ExternalOutput" if dbg else "Internal"
    xT_h = nc.dram_tensor("xT", [D, S], F32, kind=okind)
    pT_h = nc.dram_tensor("pT", [NIN + 64, S], F32, kind=okind)
    ym_h = nc.dram_tensor("ymT", [D, S], BF16, kind=okind)
    vtok_h = nc.dram_tensor("vtok", [3, S, 512], BF16, kind="Internal")
    bias_h = nc.dram_tensor("biasd", [12, 2560], F32, kind=okind)
    vf_h = nc.dram_tensor("vfT", [512, S], F32, kind="Internal")
    mod_h = nc.dram_tensor("modd", [128, 192], F32, kind=okind)
    xT_d, pT_d, ym_d, vtok_d, bias_d, vf_d = (h.ap() for h in (xT_h, pT_h, ym_h, vtok_h, bias_h, vf_h))
    xT3 = xT_d.rearrange("(c p) s -> p c s", p=128)
    ym3 = ym_d.rearrange("(c p) s -> p c s", p=128)

    cst = k.sb(es0, "cst", [128, 1920])
    ident, ones, blk64, MU, MU0, ML, JX = (cst[:, i * 128:(i + 1) * 128] for i in range(7))
    onesb = k.sb(es0, "onesb", [128, 128], BF16)
    modc = k.sb(es0, "modc", [128, 2 * 96])
    gcol = k.sb(es0, "gcol", [128, 4 * 16 + 16])
    A1 = k.sb(es0, "A1", [128, 2 * 2 * 16])
    lamc = k.sb(es0, "lamc", [128, 4])
    PS = [T(es0.enter_context(nc.psum_tensor("ps%d" % i, [128, 512], F32)), "ps%d" % i, psum=True) for i in range(8)]
    ncd = contextlib.ExitStack()
    ncd.enter_context(nc.allow_non_contiguous_dma(reason="small per-channel parameter vectors"))

    def colload(dst, vec, n):
        k.dma("sp", dst, vec.rearrange("(c p) -> p c", p=128))

    k.dma("sp", cst[:, :], cst_d)
    k.cp("dve", onesb[:, :], ones)

    def done(tag):
        if stop_after == tag:
            raise StopBuild()
    try:

        with contextlib.ExitStack() as es:
            condT = k.sb(es, "condT", [128, 16])
            colload(condT[:, :], c_d, 16)
            k.act(condT[:, :], condT[:, :], AF.Silu)
            wb = [k.sb(es, "wada%d" % i, [128, 16, 512]) for i in range(4)]
            bcol = k.sb(es, "bcol", [128, 192])
            for l in range(NL):
                colload(bcol[:, l * 96:(l + 1) * 96], b_ada[l], 96)
                for i in range(2):
                    colload(gcol[:, (l * 2 + i) * 16:(l * 2 + i + 1) * 16], norm_gain[l, i], 16)
            colload(gcol[:, 64:80], final_gain, 16)
            for l in range(NL):
                for nb in range(24):
                    w = wb[nb % 4]
                    src = w_ada[l][:, nb * 512:(nb + 1) * 512].rearrange("(c p) n -> p c n", p=128)
                    k.dma("sp", w[:, 0:8, :], src[:, 0:8, :])
                    k.dma("pool", w[:, 8:16, :], src[:, 8:16, :])
                    for m in range(4):
                        j = nb * 4 + m
                        for kc in range(16):
                            k.mm(PS[0][:, j:j + 1], w[:, kc, m * 128:(m + 1) * 128], condT[:, kc:kc + 1],
                                 start=(kc == 0), stop=(kc == 15))
                k.tt("dve", modc[:, l * 96:(l + 1) * 96], PS[0][:, 0:96], bcol[:, l * 96:(l + 1) * 96], ALU.add)
                for i in range(2):
                    sc = modc[:, l * 96 + i * 48 + 16: l * 96 + i * 48 + 32]
                    k.stt(A1[:, (l * 2 + i) * 16:(l * 2 + i + 1) * 16], sc, 1.0,
                          gcol[:, (l * 2 + i) * 16:(l * 2 + i + 1) * 16], ALU.add, ALU.mult)
            if dbg:
                k.dma("sp", mod_h.ap()[:, 0:NL * 96], modc[:, 0:NL * 96])
            lp = k.sb(es, "lp", [1, 512])
            lsum = k.sb(es, "lsum", [1, 8])
            for l in range(NL):
                k.dma("sp", lp[0:1, 0:256], diff_lambda[l:l + 1, :])
                k.tt("dve", lp[0:1, 256:320], lp[0:1, 0:64], lp[0:1, 64:128], ALU.mult)
                k.tt("dve", lp[0:1, 320:384], lp[0:1, 128:192], lp[0:1, 192:256], ALU.mult)
                nc_ = nc
                o1, i1 = lsum[0:1, 0:2].ap, lp[0:1, 256:384].ap.rearrange("p (a b) -> p a b", a=2)
                P.op("dve", lambda o1=o1, i1=i1: nc_.vector.tensor_reduce(out=o1, in_=i1, axis=mybir.AxisListType.X,
                                                                         op=ALU.add), R=[lp], W=[lsum])
                k.act(lsum[0:1, 2:4], lsum[0:1, 0:2], AF.Exp)
                lam_init = 0.8 - 0.6 * math.exp(-0.3 * l)
                k.tt("dve", lsum[0:1, 4:5], lsum[0:1, 3:4], lsum[0:1, 2:3], ALU.subtract)
                k.ts("dve", lsum[0:1, 5:6], lsum[0:1, 4:5], -lam_init, ALU.add)
                k.mm(PS[1][:, l:l + 1], ones[0:1, :], lsum[0:1, 5:6])
                k.cp("dve", lamc[:, l:l + 1], PS[1][:, l:l + 1])
        P.barrier()
        done("mod")

        with contextlib.ExitStack() as es:
            l34 = k.sb(es, "l34", [34, 12])
            e34 = k.sb(es, "e34", [34, 2560])
            bf = k.sb(es, "bf", [12, 2560])
            k.dma("sp", l34[0:32, :], rel_bias)
            k.dma("sp", l34[32:34, :], ind2_d)
            k.dma("sp", e34[:, :], e34_d)
            for i in range(5):
                k.mm(PS[i][0:12, :], l34[:, :], e34[:, i * 512:(i + 1) * 512])
                k.cp("dve", bf[:, i * 512:(i + 1) * 512], PS[i][0:12, :])
            k.dma("sp", bias_d, bf[:, :])
        P.barrier()
        done("bias")

        with contextlib.ExitStack() as es:
            xin = [k.sb(es, "xin%d" % i, [128, D]) for i in range(2)]
            stg = [k.sb(es, "xstg%d" % i, [128, 16, 128]) for i in range(2)]
            for tb in range(NB):
                xi, st = xin[tb % 2], stg[tb % 2]
                k.dma("sp", xi[:, :], x_d[tb * 128:(tb + 1) * 128, :])
                for g in range(4):
                    ps = PS[(tb * 4 + g) % 8]
                    for q in range(4):
                        kc = g * 4 + q
                        k.tr(ps[:, q * 128:(q + 1) * 128], xi[:, kc * 128:(kc + 1) * 128], ident)
                    k.cpa(st[:, g * 4:(g + 1) * 4, :], ps[:, :].m(lambda a: a.rearrange("p (q t) -> p q t", q=4)))
                k.dma("sp", xT3[:, :, tb * 128:(tb + 1) * 128], st[:, :, :])
        P.barrier()
        done("x0")

        def norm_phase(es, hT, li, t0, ntok, acol, bcolv):
            xts = [k.sb(es, "nx%d" % i, [128, 16, 512]) for i in range(2 if ntok > 512 else 1)]
            sq = [k.sb(es, "nsq%d" % i, [128, 512]) for i in range(2)]
            rs = k.sb(es, "nrs", [128, 512])
            tmp = [k.sb(es, "ntmp%d" % i, [128, 512]) for i in range(2)]
            for tcn in range(ntok // 512):
                xt = xts[tcn % len(xts)]
                k.dma("sp", xt[:, :, :], xT3[:, :, t0 + tcn * 512: t0 + (tcn + 1) * 512])
                ps = PS[tcn % 2]
                for kc in range(16):
                    s_ = sq[kc % 2]
                    k.act(s_[:, :], xt[:, kc, :], AF.Square)
                    k.mm(ps[:, :], ones, s_[:, :], start=(kc == 0), stop=(kc == 15))
                k.ts("dve", rs[:, :], ps[:, :], 1.0 / D, ALU.mult, 1e-6, ALU.add)
                k.act(rs[:, :], rs[:, :], AF.Sqrt)
                k.recip(rs[:, :], rs[:, :])
                for kc in range(16):
                    t_ = tmp[kc % 2]
                    k.tt("dve", t_[:, :], xt[:, kc, :], rs[:, :], ALU.mult)
                    dst = hT[:, kc, tcn * 512:(tcn + 1) * 512]
                    if bcolv is None:
                        k.ts("dve", dst, t_[:, :], acol.m(lambda a, kc=kc: a[:, kc:kc + 1]), ALU.mult)
                    else:
                        k.act(dst, t_[:, :], AF.Identity, bias=bcolv.m(lambda a, kc=kc: a[:, kc:kc + 1]),
                              scale=acol.m(lambda a, kc=kc: a[:, kc:kc + 1]))

        def wload(wt, src, kcs, width):
            s3 = src.rearrange("(c p) n -> p c n", p=128)
            step = 8
            for a in range(0, kcs, step):
                b = min(kcs, a + step)
                k.dma("pool", wt[:, a:b, 0:width], s3[:, a:b, :])

        def resid_epilogue(es_stage, ps, c0, tcs, gc):
            xs = es_stage[resid_epilogue.n % len(es_stage)]
            resid_epilogue.n += 1
            k.dma("sp", xs[:, :], xT_d[c0:c0 + 128, tcs])
            k.stt(xs[:, :], ps[:, :], gc, xs[:, :], ALU.mult, ALU.add)
            k.dma("sp", xT_d[c0:c0 + 128, tcs], xs[:, :])
        resid_epilogue.n = 0

        def diag_ap(t):
            base = t.h[:, :, :]
            return View(t, bass.AP(tensor=base.tensor, offset=base.offset, ap=[list(base.ap[0]), [192, 2], [1, 64]]))

        def bc2(v):
            return v.m(lambda a: a.unsqueeze(1).to_broadcast([128, 2, 128]))

        def shift_into(dst, rows, praw, tmp, mucol, r0):
            k.dma("sp", praw[0:rows, 16:S + 16], pT_d[r0:r0 + rows, :])
            k.tt("dve", tmp[0:rows, 0:S], praw[0:rows, 15:S + 15], praw[0:rows, 16:S + 16], ALU.subtract)
            k.stt(dst[0:rows, 0:S], tmp[0:rows, 0:S], mucol, praw[0:rows, 16:S + 16], ALU.mult, ALU.add)

        def rwkv_phase(es, l):
            SW = S + 16
            names = ["raw", "ta", "tb", "KS", "RS", "VS", "LW", "AA", "KK", "KP", "BB", "CU", "BT"]
            Wt = {n: k.sb(es, "rw_" + n, [128, SW]) for n in names}
            raw, ta, tb = Wt["raw"], Wt["ta"], Wt["tb"]
            KS, RS, VS, LW, AA, KK, KP, BB, CU, BT = (Wt[n] for n in names[3:])
            AT, KT, RT, KH, BH, BON, G, YT = KS, LW, AA, KK, tb, CU, raw, BB
            TWt = k.sb(es, "TWt", [96, S])
            TAt = k.sb(es, "TAt", [96, S])
            TG = k.sb(es, "TG", [128, 4, S], BF16)
            wup = k.sb(es, "wup", [96, 512])
            aup = k.sb(es, "aup", [96, 512])
            gup = k.sb(es, "gup", [128, 4, 512], BF16)
            cols = k.sb(es, "rwcols", [128, 64])
            c5 = [k.sb(es, "c5_%d" % i, [128, 512]) for i in range(3)]
            ostg = [k.sb(es, "rwo%d" % i, [128, 512], BF16) for i in range(2)]
            k.memset("dve", raw[:, 15:16], 0.0)
            colload(cols[:, 0:12], rwkv_mu[l][0:1536], 12)
            k.dma("sp", cols[0:96, 12:13], rwkv_mu[l][C_WLO:C_WLO + 96].rearrange("(c p) -> p c", p=96))
            k.dma("sp", cols[0:96, 13:14], rwkv_mu[l][C_ALO:C_ALO + 96].rearrange("(c p) -> p c", p=96))
            colload(cols[:, 14:17], rwkv_mu[l][C_GLO:C_GLO + 384], 3)
            k.dma("sp", cols[0:64, 17:18], rwkv_mu[l][C_GLO + 384:C_GLO + 448].rearrange("(c p) -> p c", p=64))
            colload(cols[:, 18:22], rwkv_w0[l], 4)
            colload(cols[:, 22:26], rwkv_a0[l], 4)
            colload(cols[:, 26:30], rwkv_k_k[l], 4)
            colload(cols[:, 30:34], rwkv_k_a[l], 4)
            k.ts("dve", cols[:, 34:38], cols[:, 30:34], -1.0, ALU.mult, 1.0, ALU.add)
            colload(cols[:, 38:42], rwkv_r_k[l], 4)
            colload(cols[:, 42:46], rwkv_ln_w[l], 4)
            colload(cols[:, 46:50], rwkv_ln_b[l], 4)
            k.dma("sp", wup[:, :], rwkv_w_up[l])
            k.dma("sp", aup[:, :], rwkv_a_up[l])
            k.memset("dve", gup[:, 3, :], 0.0)
            k.memset("dve", TG[:, 3, :], 0.0)
            for gi in range(4):
                rows = 128 if gi < 3 else 64
                k.dma("pool", gup[0:rows, gi, :], rwkv_g_up[l][gi * 128: gi * 128 + rows, :])
            if l == 1:
                PVs = k.sb(es, "PVs", [64, S])
                vup = k.sb(es, "vup", [64, 512])
                k.dma("sp", cols[0:64, 50:51], vres_mu[0].rearrange("(c p) -> p c", p=64))
                colload(cols[:, 51:55], vres_bias[0], 4)
                k.dma("sp", vup[:, :], vres_up[0])
                shift_into(PVs, 64, raw, ta, cols[0:64, 50:51], C_PV)
            shift_into(TWt, 96, raw, ta, cols[0:96, 12:13], C_WLO)
            k.act(TWt[:, :], TWt[:, :], AF.Tanh)
            shift_into(TAt, 96, raw, ta, cols[0:96, 13:14], C_ALO)
            for gi in range(4):
                rows = 128 if gi < 3 else 64
                shift_into(tb, rows, raw, ta, cols[0:rows, 14 + gi:15 + gi], C_GLO + gi * 128)
                k.act(TG[0:rows, gi, :], tb[0:rows, 0:S], AF.Sigmoid)
            done("rw1")
            def pair(name):
                return k.sb(es, name, [128, 2, 128])
            Nn = [pair("Nn0"), pair("Nn1")]
            Nt = [pair("Nt0"), pair("Nt1")]
            MTt, ARK, ARB = pair("MTt"), pair("ARK"), pair("ARB")
            PP = [pair("PP0"), pair("PP1")]
            RH, WU, U0P, VP, KHP, BHP = (pair(n) for n in ("RH", "WU", "U0P", "VP", "KHP", "BHP"))
            for t_ in (WU, U0P, VP, KHP, BHP):
                k.memset("dve", t_[:, :, :], 0.0)
            GyT = k.sb(es, "GyT", [128, 128])
            GhT = k.sb(es, "GhT", [128, 128])
            Hs = k.sb(es, "Hs", [128, 64])
            Hbd = k.sb(es, "Hbd", [128, 128])
            gC = k.sb(es, "gC", [128, NB])
            identv = ident
            id2, MU2, MU02, ML2 = (cst[:, 896 + i * 256: 896 + (i + 1) * 256].m(lambda a_: a_.rearrange("p (h w) -> p h w", h=2)) for i in range(4))
            Lz = k.sb(es, "Lz", [128, 6, 128])
            hmc = (blk64[:, 0:1], blk64[:, 64:65])

            def p3(ps, c0, w):
                return ps[:, c0:c0 + 2 * w].m(lambda a: a.rearrange("p (h w) -> p h w", h=2))

            for ct in range(4):
                cc = lambda j: cols[:, j + ct:j + ct + 1]
                r0 = ct * 128
                k.memset("dve", raw[:, 15:16], 0.0)
                shift_into(KS, 128, raw, ta, cc(4), C_K + r0)
                shift_into(RS, 128, raw, ta, cc(0), C_R + r0)
                shift_into(VS, 128, raw, ta, cc(8), C_V + r0)
                for tcn in range(NT):
                    sl = slice(tcn * 512, (tcn + 1) * 512)
                    ps = PS[tcn % 2]
                    k.mm(ps[:, :], wup[:, r0:r0 + 128], TWt[:, sl])
                    k.act(LW[:, sl], ps[:, :], AF.Sigmoid, bias=cc(18))
                    ps = PS[2 + tcn % 2]
                    k.mm(ps[:, :], aup[:, r0:r0 + 128], TAt[:, sl])
                    k.act(AA[:, sl], ps[:, :], AF.Sigmoid, bias=cc(22))
                k.ts("dve", LW[:, 0:S], LW[:, 0:S], -0.6065306597126334, ALU.mult)
                k.ts("dve", ta[:, 0:S], KS[:, 0:S], cc(26), ALU.mult)
                k.act(tb[:, 0:S], ta[:, 0:S], AF.Square)
                for tcn in range(NT):
                    sl = slice(tcn * 512, (tcn + 1) * 512)
                    ps = PS[4 + tcn % 2]
                    c_ = c5[tcn % 3]
                    k.mm(ps[:, :], blk64, tb[:, sl])
                    k.ts("dve", c_[:, :], ps[:, :], 1e-24, ALU.max)
                    k.act(c_[:, :], c_[:, :], AF.Sqrt)
                    k.recip(c_[:, :], c_[:, :])
                    k.tt("dve", KK[:, sl], ta[:, sl], c_[:, :], ALU.mult)
                k.ts("dve", ta[:, 0:S], AA[:, 0:S], cc(30), ALU.mult, cc(34), ALU.add)
                k.tt("dve", KP[:, 0:S], KS[:, 0:S], ta[:, 0:S], ALU.mult)
                k.tt("dve", BB[:, 0:S], KK[:, 0:S], AA[:, 0:S], ALU.mult)
                if l == 0:
                    k.dma("sp", vf_d[r0:r0 + 128, :], VS[:, 0:S])
                else:
                    k.dma("sp", ta[:, 0:S], vf_d[r0:r0 + 128, :])
                    for tcn in range(NT):
                        sl = slice(tcn * 512, (tcn + 1) * 512)
                        ps = PS[6 + tcn % 2]
                        c_, c2 = c5[tcn % 2], c5[2]
                        k.mm(ps[:, :], vup[:, r0:r0 + 128], PVs[:, sl])
                        k.act(c_[:, :], ps[:, :], AF.Sigmoid, bias=cc(51))
                        k.tt("dve", c2[:, :], ta[:, sl], VS[:, sl], ALU.subtract)
                        k.tt("dve", c2[:, :], c2[:, :], c_[:, :], ALU.mult)
                        k.tt("dve", VS[:, sl], VS[:, sl], c2[:, :], ALU.add)
                for c in range(NB):
                    k.scan(CU[:, c * 128:(c + 1) * 128], ones, LW[:, c * 128:(c + 1) * 128])
                k.tt("dve", ta[:, 0:S], CU[:, 0:S], LW[:, 0:S], ALU.subtract)
                k.act(ta[:, 0:S], ta[:, 0:S], AF.Exp)
                k.stt(AT[:, 0:S], KK[:, 0:S], -1.0, ta[:, 0:S], ALU.mult, ALU.mult)
                k.act(ta[:, 0:S], CU[:, 0:S], AF.Exp, scale=-1.0)
                k.tt("dve", KT[:, 0:S], KP[:, 0:S], ta[:, 0:S], ALU.mult)
                k.tt("dve", BT[:, 0:S], BB[:, 0:S], ta[:, 0:S], ALU.mult)
                k.act(ta[:, 0:S], CU[:, 0:S], AF.Exp)
                k.tt("dve", RT[:, 0:S], RS[:, 0:S], ta[:, 0:S], ALU.mult)
                cu3 = CU[:, 0:S].m(lambda a: a.rearrange("p (c t) -> p c t", t=128))
                cuC = cu3.m(lambda a: a[:, :, 127:128].to_broadcast([128, NB, 128]))
                ta3 = ta[:, 0:S].m(lambda a: a.rearrange("p (c t) -> p c t", t=128))
                k.tt("dve", ta3, cuC, cu3, ALU.subtract)
                k.act(ta[:, 0:S], ta[:, 0:S], AF.Exp)
                k.act(gC[:, :], cu3.m(lambda a: a[:, :, 127]), AF.Exp)
                k.tt("dve", KH[:, 0:S], KP[:, 0:S], ta[:, 0:S], ALU.mult)
                k.tt("dve", BH[:, 0:S], BB[:, 0:S], ta[:, 0:S], ALU.mult)
                k.tt("dve", ta[:, 0:S], RS[:, 0:S], KP[:, 0:S], ALU.mult)
                k.ts("dve", ta[:, 0:S], ta[:, 0:S], cc(38), ALU.mult)
                for tcn in range(NT):
                    sl = slice(tcn * 512, (tcn + 1) * 512)
                    ps = PS[tcn % 2]
                    k.mm(ps[:, :], blk64, ta[:, sl])
                    k.tt("dve", BON[:, sl], ps[:, :], VS[:, sl], ALU.mult)
                for tcn in range(NT):
                    sl = slice(tcn * 512, (tcn + 1) * 512)
                    ps = PS[2 + tcn % 2]
                    for gi in range(4):
                        k.mm(ps[:, :], gup[:, gi, r0:r0 + 128], TG[:, gi, sl], start=(gi == 0), stop=(gi == 3))
                    k.cpa(G[:, sl], ps[:, :])
                done("rw2")
                import os
                if os.environ.get("KVAR", "") != "nomem":
                    k.memset("dve", Hs[:, :], 0.0)
                    k.memset("dve", Hbd[:, :], 0.0)
                for c in range(NB):
                    sl = slice(c * 128, (c + 1) * 128)
                    hp = lambda t_, hh: t_[hh * 64:(hh + 1) * 64, sl]
                    psT = PS[0]
                    import os
                    KVAR = os.environ.get("KVAR", "")
                    if KVAR != "notr":
                        for i_, src in enumerate((AT, VS, KH, BH)):
                            k.tr(psT[:, i_ * 128:(i_ + 1) * 128], src[:, sl], ident)
                    if KVAR != "noev":
                        k.cpa(RH[:, :, 0:64], p3(psT, 0, 64))
                        k.cpa(diag_ap(VP), p3(psT, 128, 64))
                        k.cpa(diag_ap(KHP), p3(psT, 256, 64))
                        k.cpa(diag_ap(BHP), p3(psT, 384, 64))
                    done("rw3a")
                    for i_, src_ in enumerate((BT, AT, KT)):
                        for hh in range(2):
                            k.ts("dve", Lz[:, i_ * 2 + hh, :], src_[:, sl], hmc[hh], ALU.mult)
                    specs = [(PS[1], 0, 0, AT, Nn[0], MU2), (PS[1], 256, 1, BT, Nt[0], ML2),
                             (PS[2], 0, 2, AT, MTt, MU2), (PS[2], 256, 2, RT, ARK, MU02),
                             (PS[3], 0, 0, RT, ARB, MU02)]
                    for ps, c0, li_, rt, dst, msk in specs:
                        for hh in range(2):
                            k.mm(ps[:, c0 + hh * 128:c0 + (hh + 1) * 128], Lz[:, li_ * 2 + hh, :], rt[:, sl])
                        k.tt("dve", dst[:, :, :], p3(ps, c0, 128), msk, ALU.mult)
                    k.tt("dve", PP[0][:, :, :], Nn[0][:, :, :], id2, ALU.add)
                    done("rw3")
                    a = 0
                    b = 0
                    for lev in range(1, 7):
                        psq = PS[4 + 2 * (lev % 2)]
                        psq2 = PS[lev % 2]
                        psc = PS[5 + 2 * (lev % 2)]
                        for hh in range(2):
                            k.mm(psq[:, 256 + hh * 128:256 + (hh + 1) * 128], Nn[a][:, hh, :], Nt[a][:, hh, :])
                        if lev < 6:
                            for hh in range(2):
                                k.mm(psq2[:, hh * 128:(hh + 1) * 128], Nt[a][:, hh, :], Nn[a][:, hh, :])
                        k.cp("act", Nt[1 - a][:, :, :], p3(psq, 256, 128))
                        if lev < 6:
                            k.cp("dve", Nn[1 - a][:, :, :], p3(psq2, 0, 128))
                        for hh in range(2):
                            k.mm(psc[:, hh * 128:(hh + 1) * 128], Nt[1 - a][:, hh, :], PP[b][:, hh, :])
                        k.tt("dve", PP[1 - b][:, :, :], p3(psc, 0, 128), PP[b][:, :, :], ALU.add)
                        a, b = 1 - a, 1 - b
                    Pf = PP[b]
                    done("rw4")
                    psM = PS[1]
                    for hh in range(2):
                        k.mm(psM[:, hh * 64:(hh + 1) * 64], MTt[:, hh, :], VP[:, hh, hh * 64:(hh + 1) * 64])
                    k.cpa(RH[:, :, 64:128], p3(psM, 0, 64))
                    psW = PS[2]
                    for hh in range(2):
                        k.mm(psW[:, hh * 128:(hh + 1) * 128], Pf[:, hh, :], RH[:, hh, :])
                    pw3 = p3(psW, 0, 128)
                    k.cpa(diag_ap(WU), pw3.m(lambda a_: a_[:, :, 0:64]))
                    k.cpa(diag_ap(U0P), pw3.m(lambda a_: a_[:, :, 64:128]))
                    psG = PS[3]
                    for hh in range(2):
                        k.mm(psG[:, 0:128], WU[:, hh, :], ARB[:, hh, :], start=(hh == 0), stop=(hh == 1))
                    k.tt("dve", GyT[:, :], psG[:, 0:128], RT[:, sl], ALU.add)
                    psY = PS[1]
                    seq = [(VP, ARK, 0), (U0P, ARB, 0), (VP, ARK, 1), (U0P, ARB, 1)]
                    for i_, (lt, rt, hh) in enumerate(seq):
                        k.mm(psY[:, 256:384], lt[:, hh, :], rt[:, hh, :], start=(i_ == 0), stop=False)
                    k.mm(psY[:, 256:384], Hbd[:, :], GyT[:, :], start=False, stop=True)
                    k.cpa(YT[:, sl], psY[:, 256:384])
                    psH = PS[2]
                    for hh in range(2):
                        k.mm(psH[:, 256:384], WU[:, hh, :], BHP[:, hh, :], start=(hh == 0), stop=(hh == 1))
                    k.stt(GhT[:, :], ident, gC[:, c:c + 1], psH[:, 256:384], ALU.mult, ALU.add)
                    psS = PS[3]
                    seq = [(KHP, VP, 0), (KHP, VP, 1), (BHP, U0P, 0), (BHP, U0P, 1)]
                    for i_, (lt, rt, hh) in enumerate(seq):
                        k.mm(psS[:, 256:320], lt[:, hh, :], rt[:, hh, hh * 64:(hh + 1) * 64], start=(i_ == 0), stop=False)
                    k.mm(psS[:, 256:320], GhT[:, :], Hs[:, :], start=False, stop=True)
                    k.cp("dve", Hs[:, :], psS[:, 256:320])
                    k.cp("dve", Hbd[0:64, 0:64], Hs[0:64, :])
                    k.cp("act", Hbd[64:128, 64:128], Hs[64:128, :])
                    done("rw5")
                for tcn in range(NT):
                    sl = slice(tcn * 512, (tcn + 1) * 512)
                    ps1, ps2 = PS[4 + tcn % 2], PS[6 + tcn % 2]
                    d_, q_ = c5[0], c5[1]
                    k.mm(ps1[:, :], blk64, YT[:, sl])
                    k.stt(d_[:, :], ps1[:, :], -1.0 / 64, YT[:, sl], ALU.mult, ALU.add)
                    k.act(q_[:, :], d_[:, :], AF.Square)
                    k.mm(ps2[:, :], blk64, q_[:, :])
                    k.ts("dve", q_[:, :], ps2[:, :], 1.0 / 64, ALU.mult, 64e-5, ALU.add)
                    k.act(q_[:, :], q_[:, :], AF.Sqrt)
                    k.recip(q_[:, :], q_[:, :])
                    k.tt("dve", d_[:, :], d_[:, :], q_[:, :], ALU.mult)
                    k.ts("dve", d_[:, :], d_[:, :], cc(42), ALU.mult, cc(46), ALU.add)
                    k.tt("dve", d_[:, :], d_[:, :], BON[:, sl], ALU.add)
                    o_ = ostg[tcn % 2]
                    k.tt("dve", o_[:, :], d_[:, :], G[:, sl], ALU.mult)
                    k.dma("sp", ym_d[r0:r0 + 128, sl], o_[:, :])

        def attn_phase(es, l):
            qT = [k.sb(es, "qT%d" % i, [64, S], BF16) for i in range(4)]
            kT = [k.sb(es, "kT%d" % i, [64, S], BF16) for i in range(4)]
            vt = [k.sb(es, "vt%d" % i, [128, NB, 128], BF16) for i in range(2)]
            TB = [k.sb(es, "TB%d" % i, [128, TBW]) for i in range(2)]
            M0 = k.sb(es, "M0", [128, TBW])
            M1 = k.sb(es, "M1", [128, TBW])
            tS = [k.sb(es, "tS%d" % i, [128, 512]) for i in range(4)]
            eB = [k.sb(es, "eB%d" % i, [128, 512], BF16) for i in range(2)]
            Rr = k.sb(es, "Rr", [128, 512])
            O0 = k.sb(es, "O0", [128, 512])
            O1 = k.sb(es, "O1", [128, 512])
            ostg = [k.sb(es, "aostg%d" % i, [128, 512], BF16) for i in range(2)]
            sub = k.sb(es, "subc", [128, 2])
            k.dma("sp", M0[:, :], m0_d)
            k.dma("sp", M1[:, :], m1_d)
            colload(sub[:, 0:1], diff_subln[l], 1)
            lam_init = 0.8 - 0.6 * math.exp(-0.3 * l)
            k.ts("dve", sub[:, 1:2], sub[:, 0:1], 1.0 - lam_init, ALU.mult)
            vtok3 = [vtok_d[i].rearrange("(b p) c -> p b c", p=128) for i in range(3)]
            cnt = {"u": 0, "a": 0, "t": 0, "o": 0}

            def load_qk(slot, qrow, krow):
                k.dma("pool", qT[slot][:, :], pT_d[qrow:qrow + 64, :])
                k.dma("pool", kT[slot][:, :], pT_d[krow:krow + 64, :])

            Hk = k.sb(es, "Hk", [128, TBW])

            def load_tb(slot, hb):
                src = bass.AP(tensor=bias_h, offset=hb * 2560 + 1, ap=[[1, 128], [1, TBW]])
                k.dma("sp", Hk[:, :], src)
                for i_, c0 in enumerate(range(0, TBW, 512)):
                    w_ = min(512, TBW - c0)
                    ps = PS[i_ % 4]
                    k.mm(ps[:, 0:w_], JX, Hk[:, c0:c0 + w_])
                    k.cpa(TB[slot][:, c0:c0 + w_], ps[:, 0:w_])

            def softmax_pass(qv, kv, vv, dv, tb, qc, dst):
                u = cnt["u"]
                cnt["u"] += 1
                psN, psD = PS[4 + u % 2], PS[6 + u % 2]
                jl = 4 * qc + 4
                for j in range(jl):
                    off = 384 + 512 * qc - 128 * j
                    psA = PS[cnt["a"] % 4]
                    cnt["a"] += 1
                    t_ = tS[cnt["t"] % 2]
                    e_ = eB[cnt["t"] % 2]
                    cnt["t"] += 1
                    k.mm(psA[:, :], kv[:, j * 128:(j + 1) * 128], qv[:, qc * 512:(qc + 1) * 512])
                    k.stt(t_[:, :], psA[:, :], 0.125, tb[:, off:off + 512], ALU.mult, ALU.add)
                    k.act(e_[:, :], t_[:, :], AF.Exp)
                    k.mm(psN[0:dv, :], vv[:, j, 0:dv], e_[:, :], start=(j == 0), stop=(j == jl - 1))
                    k.mm(psD[:, :], onesb[:, :], e_[:, :], start=(j == 0), stop=(j == jl - 1))
                rd = tS[2]
                k.recip(rd[:, :], psD[:, :])
                k.tt("dve", dst[0:dv, :], psN[0:dv, :], rd[0:dv, :], ALU.mult)

            load_qk(0, C_QC, C_KC)
            load_tb(0, 0)
            k.dma("sp", vt[0][:, :, 0:64], vtok3[1][:, :, 0:64])
            for h in range(8):
                s_ = h % 2
                if h + 1 < 8:
                    load_qk(1 - s_, C_QC + (h + 1) * 64, C_KC + (h + 1) * 64)
                    load_tb(1 - s_, h + 1)
                    k.dma("sp", vt[1 - s_][:, :, 0:64], vtok3[1][:, :, (h + 1) * 64:(h + 2) * 64])
                for qc in range(NT):
                    softmax_pass(qT[s_], kT[s_], vt[s_], 64, TB[s_], qc, O0)
                    o_ = ostg[cnt["o"] % 2]
                    cnt["o"] += 1
                    k.cp("act", o_[0:64, :], O0[0:64, :])
                    k.dma("sp", ym_d[1024 + h * 64:1024 + (h + 1) * 64, qc * 512:(qc + 1) * 512], o_[0:64, :])
            for hd in range(4):
                s_ = hd % 2
                for c in range(2):
                    load_qk(2 * s_ + c, C_QD + hd * 128 + c * 64, C_KD + hd * 128 + c * 64)
                load_tb(s_, 8 + hd)
                k.dma("sp", vt[s_][:, :, :], vtok3[2][:, :, hd * 128:(hd + 1) * 128])
                for qc in range(NT):
                    softmax_pass(qT[2 * s_], kT[2 * s_], vt[s_], 128, TB[s_], qc, O0)
                    softmax_pass(qT[2 * s_ + 1], kT[2 * s_ + 1], vt[s_], 128, TB[s_], qc, O1)
                    k.stt(O0[:, :], O1[:, :], lamc[:, l:l + 1], O0[:, :], ALU.mult, ALU.add)
                    sq = tS[3]
                    k.act(sq[:, :], O0[:, :], AF.Square)
                    psX = PS[cnt["a"] % 4]
                    cnt["a"] += 1
                    k.mm(psX[:, :], ones, sq[:, :])
                    k.ts("dve", sq[:, :], psX[:, :], 1.0 / 128, ALU.mult, 1e-5, ALU.add)
                    k.act(sq[:, :], sq[:, :], AF.Sqrt)
                    k.recip(sq[:, :], sq[:, :])
                    k.tt("dve", O0[:, :], O0[:, :], sq[:, :], ALU.mult)
                    o_ = ostg[cnt["o"] % 2]
                    cnt["o"] += 1
                    k.ts("dve", o_[:, :], O0[:, :], sub[:, 1:2], ALU.mult)
                    k.dma("sp", ym_d[1536 + hd * 128:1536 + (hd + 1) * 128, qc * 512:(qc + 1) * 512], o_[:, :])
            load_qk(0, C_QB, C_KB)
            k.dma("sp", vt[0][:, :, 0:64], vtok3[0][:, :, 0:64])
            for h in range(8):
                s_ = h % 2
                if h + 1 < 8:
                    load_qk(1 - s_, C_QB + (h + 1) * 64, C_KB + (h + 1) * 64)
                    k.dma("sp", vt[1 - s_][:, :, 0:64], vtok3[0][:, :, (h + 1) * 64:(h + 2) * 64])
                qv, kv, vv = qT[s_], kT[s_], vt[s_]
                for qc in range(NT):
                    u = cnt["u"]
                    cnt["u"] += 1
                    psN = PS[6 + u % 2]
                    first = True
                    jl = 4 * qc + 4
                    for j in range(jl - 1, -1, -1):
                        off = 384 + 512 * qc - 128 * j
                        diag = j >= 4 * qc
                        a_ = cnt["a"]
                        cnt["a"] += 1
                        psA, psB, psC = PS[a_ % 2], PS[2 + a_ % 2], PS[4 + a_ % 2]
                        e1, sp, u_ = tS[0], tS[1], tS[2]
                        k.mm(psA[:, :], kv[:, j * 128:(j + 1) * 128], qv[:, qc * 512:(qc + 1) * 512])
                        k.act(e1[:, :], psA[:, :], AF.Exp, scale=0.125)
                        k.act(sp[:, :], e1[:, :], AF.Ln, bias=1.0)
                        if diag:
                            k.tt("dve", sp[:, :], sp[:, :], M1[:, off:off + 512], ALU.mult)
                        k.mm(psB[:, :], ML, sp[:, :])
                        if j > 0:
                            k.mm(psC[:, :], ones, sp[:, :])
                        k.stt(u_[:, :], psA[:, :], 0.125, sp[:, :], ALU.mult, ALU.subtract)
                        k.tt("dve", u_[:, :], u_[:, :], psB[:, :], ALU.subtract)
                        if not first:
                            k.tt("dve", u_[:, :], u_[:, :], Rr[:, :], ALU.subtract)
                        if diag:
                            k.tt("dve", u_[:, :], u_[:, :], M0[:, off:off + 512], ALU.add)
                        e_ = eB[a_ % 2]
                        k.act(e_[:, :], u_[:, :], AF.Exp)
                        k.mm(psN[0:64, :], vv[:, j, 0:64], e_[:, :], start=first, stop=(j == 0))
                        if j > 0:
                            if first:
                                k.cp("dve", Rr[:, :], psC[:, :])
                            else:
                                k.tt("dve", Rr[:, :], Rr[:, :], psC[:, :], ALU.add)
                        first = False
                    o_ = ostg[cnt["o"] % 2]
                    cnt["o"] += 1
                    k.cp("act", o_[0:64, :], psN[0:64, :])
                    k.dma("sp", ym_d[512 + h * 64:512 + (h + 1) * 64, qc * 512:(qc + 1) * 512], o_[0:64, :])

        for l in range(NL):
            mc = lambda a, b, l=l: modc[:, l * 96 + a: l * 96 + b]
            ncols_in = NIN + (64 if l == 1 else 0)
            with contextlib.ExitStack() as es:
                hT = k.sb(es, "hT", [128, 16, S], BF16)
                with contextlib.ExitStack() as es2:
                    norm_phase(es2, hT, l, 0, S, A1[:, (l * 2) * 16:(l * 2 + 1) * 16], mc(0, 16))
                P.barrier()
                wts = [k.sb(es, "win%d" % i, [128, 16, 512], BF16) for i in range(2)]
                stg = [k.sb(es, "pstg%d" % i, [128, 512]) for i in range(4)]
                stgb = [k.sb(es, "vstg%d" % i, [128, 512], BF16) for i in range(2)]
                segs = [(0, C_VB), (C_QC, C_VC), (C_QD, C_VD)]
                blocks = []
                for (a, b) in segs:
                    for c0 in range(a, b, 512):
                        blocks.append(("fm", c0, min(512, b - c0)))
                if l == 1:
                    blocks.append(("pv", C_PV, 64))
                for i, c0 in enumerate((C_VB, C_VC, C_VD)):
                    blocks.append(("tm", c0, 512, i))

                def load_block(bi):
                    blk = blocks[bi]
                    wt = wts[bi % 2]
                    if blk[0] == "pv":
                        wload(wt, vres_down[0], 16, 64)
                    else:
                        wload(wt, w_in[l][:, blk[1]:blk[1] + blk[2]], 16, blk[2])
                load_block(0)
                nps = 0
                for bi, blk in enumerate(blocks):
                    if bi + 1 < len(blocks):
                        load_block(bi + 1)
                    wt = wts[bi % 2]
                    if blk[0] in ("fm", "pv"):
                        c0, wd = blk[1], blk[2]
                        for m in range(0, wd, 128):
                            mw = min(128, wd - m)
                            for tcn in range(NT):
                                ps = PS[nps % 4]
                                nps += 1
                                for kc in range(16):
                                    k.mm(ps[0:mw, :], wt[:, kc, m:m + mw], hT[:, kc, tcn * 512:(tcn + 1) * 512],
                                         start=(kc == 0), stop=(kc == 15))
                                st = stg[nps % 4]
                                k.cpa(st[0:mw, :], ps[0:mw, :])
                                k.dma("sp", pT_d[c0 + m:c0 + m + mw, tcn * 512:(tcn + 1) * 512], st[0:mw, :])
                    else:
                        vi = blk[3]
                        for tb in range(NB):
                            ps = PS[nps % 4]
                            nps += 1
                            for kc in range(16):
                                k.mm(ps[:, :], hT[:, kc, tb * 128:(tb + 1) * 128], wt[:, kc, :],
                                     start=(kc == 0), stop=(kc == 15))
                            st = stgb[nps % 2]
                            k.cpa(st[:, :], ps[:, :])
                            k.dma("sp", vtok_d[vi, tb * 128:(tb + 1) * 128, :], st[:, :])
            P.barrier()
            done("gemm1")

            with contextlib.ExitStack() as es:
                rwkv_phase(es, l)
            P.barrier()
            done("rwkv")
            with contextlib.ExitStack() as es:
                attn_phase(es, l)
            P.barrier()
            done("attn")

            with contextlib.ExitStack() as es:
                ymT = k.sb(es, "ymT", [128, 16, S], BF16)
                for kc in range(16):
                    k.dma("sp", ymT[:, kc, :], ym3[:, kc, :])
                wts = [k.sb(es, "wout%d" % i, [128, 16, 512], BF16) for i in range(2)]
                xst = [k.sb(es, "xst%d" % i, [128, 512]) for i in range(4)]
                wload(wts[0], w_out[l][:, 0:512], 16, 512)
                nps = 0
                for bi in range(4):
                    if bi + 1 < 4:
                        wload(wts[(bi + 1) % 2], w_out[l][:, (bi + 1) * 512:(bi + 2) * 512], 16, 512)
                    wt = wts[bi % 2]
                    for m in range(4):
                        dc = bi * 4 + m
                        for tcn in range(NT):
                            ps = PS[nps % 4]
                            nps += 1
                            for kc in range(16):
                                k.mm(ps[:, :], wt[:, kc, m * 128:(m + 1) * 128], ymT[:, kc, tcn * 512:(tcn + 1) * 512],
                                     start=(kc == 0), stop=(kc == 15))
                            resid_epilogue(xst, ps, dc * 128, slice(tcn * 512, (tcn + 1) * 512), mc(32 + dc, 33 + dc))
            P.barrier()
            done("wout")

            HT = min(S, 1024)
            for half in range(S // HT):
                t0 = half * HT
                with contextlib.ExitStack() as es:
                    aT = [k.sb(es, "aT%d" % f, [128, HT], BF16) for f in range(FC)]
                    with contextlib.ExitStack() as esA:
                        h2T = k.sb(esA, "h2T", [128, 16, HT], BF16)
                        with contextlib.ExitStack() as es2:
                            norm_phase(es2, h2T, l, t0, HT, A1[:, (l * 2 + 1) * 16:(l * 2 + 2) * 16], mc(48, 64))
                        P.barrier()
                        w13 = [k.sb(esA, "w13_%d" % i, [128, 16, 512], BF16) for i in range(2)]
                        nblk = 2 * DFF // 512

                        def ld13(b_):
                            wload(w13[b_ % 2], ffn_w13[l][:, b_ * 512:(b_ + 1) * 512], 16, 512)
                        ld13(0)
                        nps = 0
                        for b_ in range(nblk):
                            if b_ + 1 < nblk:
                                ld13(b_ + 1)
                            for m in range(4):
                                col = b_ * 512 + m * 128
                                is_up = col >= DFF
                                f = (col - DFF) // 128 if is_up else col // 128
                                for tcn in range(HT // 512):
                                    ps = PS[nps % 8]
                                    nps += 1
                                    ts_ = slice(tcn * 512, (tcn + 1) * 512)
                                    for kc in range(16):
                                        k.mm(ps[:, :], w13[b_ % 2][:, kc, m * 128:(m + 1) * 128], h2T[:, kc, ts_],
                                             start=(kc == 0), stop=(kc == 15))
                                    if not is_up:
                                        k.act(aT[f][:, ts_], ps[:, :], AF.Silu)
                                    else:
                                        k.tt("dve", aT[f][:, ts_], aT[f][:, ts_], ps[:, :], ALU.mult)
                    P.barrier()
                    with contextlib.ExitStack() as esB:
                        w2 = [k.sb(esB, "w2_%d" % i, [128, FC, 256], BF16) for i in range(2)]
                        xst = [k.sb(esB, "fxst%d" % i, [128, 512]) for i in range(4)]

                        def ld2(b_):
                            src_ = ffn_w2[l][:, b_ * 256:(b_ + 1) * 256].rearrange("(c p) n -> p c n", p=128)
                            wt = w2[b_ % 2]
                            for a_ in range(0, FC, 11):
                                k.dma("pool", wt[:, a_:a_ + 11, :], src_[:, a_:a_ + 11, :])
                        ld2(0)
                        nps = 0
                        for b_ in range(8):
                            if b_ + 1 < 8:
                                ld2(b_ + 1)
                            for m in range(2):
                                dc = b_ * 2 + m
                                for tcn in range(HT // 512):
                                    ps = PS[nps % 8]
                                    nps += 1
                                    ts_ = slice(tcn * 512, (tcn + 1) * 512)
                                    for f in range(FC):
                                        k.mm(ps[:, :], w2[b_ % 2][:, f, m * 128:(m + 1) * 128], aT[f][:, ts_],
                                             start=(f == 0), stop=(f == FC - 1))
                                    resid_epilogue(xst, ps, dc * 128, slice(t0 + tcn * 512, t0 + (tcn + 1) * 512),
                                                   mc(80 + dc, 81 + dc))
                P.barrier()

        with contextlib.ExitStack() as es:
            hn = k.sb(es, "hn", [128, 16, 512])
            ost = [k.sb(es, "ost%d" % i, [128, D]) for i in range(2)]

            for tcn in range(NT):
                with contextlib.ExitStack() as es2:
                    norm_phase(es2, hn, 0, tcn * 512, 512, gcol[:, 64:80], None)
                P.barrier()
                for q in range(4):
                    tb = tcn * 4 + q
                    o_ = ost[tb % 2]
                    for g in range(4):
                        ps = PS[(tb * 4 + g) % 8]
                        for j in range(4):
                            kc = g * 4 + j
                            k.tr(ps[:, j * 128:(j + 1) * 128], hn[:, kc, q * 128:(q + 1) * 128], ident)
                        k.cpa(o_[:, g * 512:(g + 1) * 512], ps[:, :])
                    k.dma("sp", out_d[tb * 128:(tb + 1) * 128, :], o_[:, :])
                P.barrier()
    except StopBuild:
        pass
    P.barrier()
    P.emit(es0)
    ncd.close()
    try:
        es0.close()
    except AssertionError:
        pass
    return nc


_CACHE = {}


def make_in_maps(inputs, S, nb):
    consts = host_consts(S)
    shared = {}
    for name, arr in inputs.items():
        if name in ("x", "c"):
            continue
        a = np.asarray(arr, dtype=np.float32)
        if name == "diff_lambda":
            a = a.reshape(2, 256)
        if name == "rwkv_r_k":
            a = a.reshape(2, 512)
        shared[name] = np.ascontiguousarray(a)
    shared.update(consts)
    x = np.asarray(inputs["x"], dtype=np.float32)
    c = np.asarray(inputs["c"], dtype=np.float32)
    maps = []
    for b in range(nb):
        m = dict(shared)
        m["x"] = np.ascontiguousarray(x[b, :S])
        m["c"] = np.ascontiguousarray(c[b])
        maps.append(m)
    return maps


def kernel(**inputs):
    S = 2048
    nc = build(S, 2)
    maps = make_in_maps(inputs, S, 8)
    res = run_bass_kernel_spmd(nc, maps, core_ids=list(range(8)))
    out = np.stack([np.asarray(r["out"], dtype=np.float32) for r in res.results], axis=0)
    return out
```
